# Optimizing a Trainium2 kernel written in Bass

```python
import math
import jax
import jax.numpy as jnp
from jax import lax
import numpy as np

D_MODEL = 1024
BATCH = 4
SEQ = 8192
DEPTH = 4

GRID_W = 64
CTX_LEN = 256
Q_BLOCK = 128
ROPE_BASE = 10000.0
LN_EPS = 1e-5
RMS_EPS = 1e-6

MLA_HEADS = 8
MLA_Q_RANK = 256
MLA_KV_RANK = 128
MLA_NOPE_DIM = 64
MLA_ROPE_DIM = 32
MLA_V_DIM = 64
MLA_SCALE = (MLA_NOPE_DIM + MLA_ROPE_DIM) ** -0.5

HG_HEADS = 4
HG_KEY_DIM = 128
HG_VAL_DIM = 128
HG_CHUNK = 64
HG_SCALE = HG_KEY_DIM ** -0.5

DIFF_HEADS = 4
DIFF_HEAD_DIM = 64
DIFF_V_DIM = 2 * DIFF_HEAD_DIM
DIFF_SCALE = DIFF_HEAD_DIM ** -0.5

BRANCH_WIDTH = 512
N_BRANCHES = 3
FF_HIDDEN = -(-8 * D_MODEL // (3 * 256)) * 256
DEEPNORM_ALPHA = (2 * DEPTH) ** 0.25
DEEPNORM_BETA = (8 * DEPTH) ** -0.25

IN_SPLITS = (
    MLA_Q_RANK, MLA_KV_RANK, MLA_ROPE_DIM,
    HG_HEADS * HG_KEY_DIM, HG_HEADS * HG_KEY_DIM,
    HG_HEADS * HG_KEY_DIM, HG_HEADS * HG_VAL_DIM,
    HG_HEADS * HG_VAL_DIM,
    DIFF_HEADS * 2 * DIFF_HEAD_DIM, DIFF_HEADS * 2 * DIFF_HEAD_DIM, DIFF_HEADS * DIFF_V_DIM,
)
IN_WIDTH = sum(IN_SPLITS)

kernel_name = 'hybrid_mla_hgrn2_diffattn_block'


def _layer_norm(x, g, b):
    xf = x.astype(jnp.float32)
    mu = jnp.mean(xf, axis=-1, keepdims=True)
    var = jnp.mean(jnp.square(xf - mu), axis=-1, keepdims=True)
    y = (xf - mu) * lax.rsqrt(var + LN_EPS)
    return (y * g.astype(jnp.float32) + b.astype(jnp.float32)).astype(x.dtype)


def _rms_norm(x, g):
    xf = x.astype(jnp.float32)
    y = xf * lax.rsqrt(jnp.mean(jnp.square(xf), axis=-1, keepdims=True) + RMS_EPS)
    return (y * g.astype(jnp.float32)).astype(x.dtype)


def _post_norm(res, y, g, b):
    return _layer_norm(DEEPNORM_ALPHA * res + y, g, b)


def _axial_rope_tables(rows, rot_dim):
    row, col = jnp.meshgrid(jnp.arange(rows, dtype=jnp.float32), jnp.arange(GRID_W, dtype=jnp.float32), indexing='ij')
    n_freq = rot_dim // 4
    inv_freq = ROPE_BASE ** (-jnp.arange(n_freq, dtype=jnp.float32) / n_freq)
    ang = jnp.concatenate([row.reshape(-1, 1) * inv_freq, col.reshape(-1, 1) * inv_freq], axis=-1)
    return jnp.cos(ang), jnp.sin(ang)


def _rope(x, cos, sin):
    x1, x2 = jnp.split(x, 2, axis=-1)
    cos = cos.astype(x.dtype)
    sin = sin.astype(x.dtype)
    return jnp.concatenate([x1 * cos - x2 * sin, x1 * sin + x2 * cos], axis=-1)


def _split_in(p):
    offsets = [int(o) for o in np.cumsum(IN_SPLITS)[:-1]]
    return jnp.split(p, offsets, axis=-1)


def _heads_to_tokens(o):
    b, h, L, d = o.shape
    return o.transpose(0, 2, 1, 3).reshape(b, L, h * d)


def _query_blocks(fn, q):
    b, h, s, d = q.shape
    nb = s // Q_BLOCK
    qb = jnp.moveaxis(q.reshape(b, h, nb, Q_BLOCK, d), 2, 0)
    ob = lax.map(fn, qb)
    return jnp.moveaxis(ob, 0, 2).reshape(b, h, s, ob.shape[-1])


def _mla_qkv(c_q, c_kv, k_pe, lp, rope):
    b, L, _ = c_q.shape
    q = (_rms_norm(c_q, lp['mla_q_norm']) @ lp['w_uq']).reshape(b, L, MLA_HEADS, MLA_NOPE_DIM + MLA_ROPE_DIM)
    kv = (_rms_norm(c_kv, lp['mla_kv_norm']) @ lp['w_ukv']).reshape(b, L, MLA_HEADS, MLA_NOPE_DIM + MLA_V_DIM)
    q_nope, q_pe = q[..., :MLA_NOPE_DIM], q[..., MLA_NOPE_DIM:]
    k_nope, v = kv[..., :MLA_NOPE_DIM], kv[..., MLA_NOPE_DIM:]
    if rope is not None:
        cos, sin = rope
        q_pe = _rope(q_pe, cos[:, None, :], sin[:, None, :])
        k_pe = _rope(k_pe, cos, sin)
    k_pe = jnp.broadcast_to(k_pe[:, :, None, :], (b, L, MLA_HEADS, MLA_ROPE_DIM))
    q = jnp.concatenate([q_nope, q_pe], axis=-1)
    k = jnp.concatenate([k_nope, k_pe], axis=-1)
    return q.transpose(0, 2, 1, 3), k.transpose(0, 2, 1, 3), v.transpose(0, 2, 1, 3)


def _mla_attend(q, k, v):
    s = jnp.einsum('bhqd,bhkd->bhqk', q, k) * MLA_SCALE
    p = jax.nn.softmax(s.astype(jnp.float32), axis=-1).astype(v.dtype)
    return jnp.einsum('bhqk,bhkd->bhqd', p, v)


def _hgrn2_inputs(q_raw, ff_raw, fb_raw, i_raw, lb_fwd, lb_bwd):
    b, L, _ = q_raw.shape

    def heads(t):
        return t.reshape(b, L, HG_HEADS, -1).transpose(0, 2, 1, 3).astype(jnp.float32)

    def gate(raw, lb):
        lb = lb.astype(jnp.float32).reshape(1, HG_HEADS, 1, HG_KEY_DIM)
        z = heads(raw)
        log_f = jnp.logaddexp(jnp.log(lb), jnp.log1p(-lb) + jax.nn.log_sigmoid(z))
        return (1.0 - lb) * jax.nn.sigmoid(-z), log_f

    q = jax.nn.silu(heads(q_raw)) * HG_SCALE
    k_f, g_f = gate(ff_raw, lb_fwd)
    k_b, g_b = gate(fb_raw, lb_bwd)
    return q, k_f, g_f, k_b, g_b, heads(i_raw)


def _gla_chunk_scan(q, k, v, log_f, s0, emit):
    b, h, L, _ = q.shape
    dv = v.shape[-1]
    n = L // HG_CHUNK

    def to_chunks(t):
        return jnp.moveaxis(t.reshape(b, h, n, HG_CHUNK, t.shape[-1]), 2, 0)

    incl = jnp.tril(jnp.ones((HG_CHUNK, HG_CHUNK), dtype=bool))

    def step(S, inp):
        qc, kc, vc, gc = inp
        cum = jnp.cumsum(gc, axis=2)
        last = cum[:, :, -1, :]
        S_new = jnp.exp(last)[..., None] * S + jnp.einsum('bhsk,bhsv->bhkv', kc * jnp.exp(last[:, :, None, :] - cum), vc)
        if not emit:
            return S_new, None
        rel = jnp.where(incl[:, :, None], cum[:, :, :, None, :] - cum[:, :, None, :, :], -jnp.inf)
        scores = jnp.einsum('bhtk,bhsk,bhtsk->bhts', qc, kc, jnp.exp(rel))
        o = jnp.einsum('bhts,bhsv->bhtv', scores, vc) + jnp.einsum('bhtk,bhkv->bhtv', qc * jnp.exp(cum), S)
        return S_new, o

    S, o = lax.scan(step, s0, (to_chunks(q), to_chunks(k), to_chunks(v), to_chunks(log_f)))
    if not emit:
        return S, None
    return S, jnp.moveaxis(o, 0, 2).reshape(b, h, L, dv)


def _flip(t):
    return t[:, :, ::-1]


def _hgrn2_out(o, g_raw, g_norm):
    b, h, L, dv = o.shape
    o = o.transpose(0, 2, 1, 3).astype(g_raw.dtype)
    g = g_raw.reshape(b, L, HG_HEADS, HG_VAL_DIM)
    return (_rms_norm(o, g_norm) * jax.nn.silu(g)).reshape(b, L, h * dv)


def _diff_qkv(q_raw, k_raw, v_raw, rope):
    b, L, _ = q_raw.shape
    q = q_raw.reshape(b, L, DIFF_HEADS, 2, DIFF_HEAD_DIM)
    k = k_raw.reshape(b, L, DIFF_HEADS, 2, DIFF_HEAD_DIM)
    if rope is not None:
        cos, sin = rope
        q = _rope(q, cos[:, None, None, :], sin[:, None, None, :])
        k = _rope(k, cos[:, None, None, :], sin[:, None, None, :])
    q = q.reshape(b, L, DIFF_HEADS, 2 * DIFF_HEAD_DIM).transpose(0, 2, 1, 3)
    k = k.reshape(b, L, DIFF_HEADS, 2 * DIFF_HEAD_DIM).transpose(0, 2, 1, 3)
    v = v_raw.reshape(b, L, DIFF_HEADS, DIFF_V_DIM).transpose(0, 2, 1, 3)
    return q, k, v


def _diff_attend(q, k, v, lam):
    q1, q2 = q[..., :DIFF_HEAD_DIM], q[..., DIFF_HEAD_DIM:]
    k1, k2 = k[..., :DIFF_HEAD_DIM], k[..., DIFF_HEAD_DIM:]
    p1 = jax.nn.softmax((jnp.einsum('bhqd,bhkd->bhqk', q1, k1) * DIFF_SCALE).astype(jnp.float32), axis=-1)
    p2 = jax.nn.softmax((jnp.einsum('bhqd,bhkd->bhqk', q2, k2) * DIFF_SCALE).astype(jnp.float32), axis=-1)
    return jnp.einsum('bhqk,bhkd->bhqd', (p1 - lam * p2).astype(v.dtype), v)


def _diff_out(o, subln, lam_init):
    b, h, L, d = o.shape
    return (_rms_norm(o.transpose(0, 2, 1, 3), subln) * (1.0 - lam_init)).reshape(b, L, h * d)


def _diff_lambda_init(layer):
    return 0.8 - 0.6 * math.exp(-0.3 * layer)


def _merge(h, ys, lp):
    m = jax.nn.sigmoid(h @ lp['w_gate'][0] + lp['b_gate'][0]) * (ys[0] @ lp['w_branch'][0])
    for i in range(1, N_BRANCHES):
        m = m + jax.nn.sigmoid(h @ lp['w_gate'][i] + lp['b_gate'][i]) * (ys[i] @ lp['w_branch'][i])
    return m @ lp['w_o']


def _swiglu(h, w1, w2):
    gate, up = jnp.split(h @ w1, 2, axis=-1)
    return (jax.nn.silu(gate) * up) @ w2


def _token_mixer(h_x, h_c, lp, lb_fwd, lb_bwd, lam_init, rope_mla, rope_diff, ctx_out):
    px = _split_in(h_x @ lp['w_in'])
    pc = _split_in(h_c @ lp['w_in'])

    q_mc, k_mc, v_mc = _mla_qkv(pc[0], pc[1], pc[2], lp, None)
    q_mx, k_mx, v_mx = _mla_qkv(px[0], px[1], px[2], lp, rope_mla)
    k_m = jnp.concatenate([k_mc, k_mx], axis=2)
    v_m = jnp.concatenate([v_mc, v_mx], axis=2)
    y_mla_x = _heads_to_tokens(_query_blocks(lambda qb: _mla_attend(qb, k_m, v_m), q_mx))

    q_hc, kf_c, gf_c, kb_c, gb_c, v_hc = _hgrn2_inputs(pc[3], pc[4], pc[5], pc[6], lb_fwd, lb_bwd)
    q_hx, kf_x, gf_x, kb_x, gb_x, v_hx = _hgrn2_inputs(px[3], px[4], px[5], px[6], lb_fwd, lb_bwd)
    zeros = jnp.zeros(q_hc.shape[:2] + (HG_KEY_DIM, HG_VAL_DIM), jnp.float32)
    s_f, o_f_c = _gla_chunk_scan(q_hc, kf_c, v_hc, gf_c, zeros, ctx_out)
    s_b, o_b_c = _gla_chunk_scan(_flip(q_hc), _flip(kb_c), _flip(v_hc), _flip(gb_c), zeros, ctx_out)
    _, o_f_x = _gla_chunk_scan(q_hx, kf_x, v_hx, gf_x, s_f, True)
    _, o_b_x = _gla_chunk_scan(_flip(q_hx), _flip(kb_x), _flip(v_hx), _flip(gb_x), s_b, True)
    y_hg_x = _hgrn2_out(o_f_x + _flip(o_b_x), px[7], lp['hg_norm'])

    dl = lp['diff_lambda'].astype(jnp.float32)
    lam = jnp.exp(jnp.sum(dl[0] * dl[1])) - jnp.exp(jnp.sum(dl[2] * dl[3])) + lam_init
    q_dc, k_dc, v_dc = _diff_qkv(pc[8], pc[9], pc[10], None)
    q_dx, k_dx, v_dx = _diff_qkv(px[8], px[9], px[10], rope_diff)
    k_d = jnp.concatenate([k_dc, k_dx], axis=2)
    v_d = jnp.concatenate([v_dc, v_dx], axis=2)
    y_diff_x = _diff_out(_query_blocks(lambda qb: _diff_attend(qb, k_d, v_d, lam), q_dx), lp['diff_subln'], lam_init)

    y_x = _merge(h_x, (y_mla_x, y_hg_x, y_diff_x), lp)
    if not ctx_out:
        return y_x, None
    y_mla_c = _heads_to_tokens(_mla_attend(q_mc, k_mc, v_mc))
    y_hg_c = _hgrn2_out(o_f_c + _flip(o_b_c), pc[7], lp['hg_norm'])
    y_diff_c = _diff_out(_diff_attend(q_dc, k_dc, v_dc, lam), lp['diff_subln'], lam_init)
    y_c = _merge(h_c, (y_mla_c, y_hg_c, y_diff_c), lp)
    return y_x, y_c


def setup_inputs(seed: int = 0) -> dict:
    key = jax.random.key(seed)
    ks = jax.random.split(key, 25)

    def nrm(k, shape, scale):
        return scale * jax.random.normal(k, shape, jnp.float32)

    d = D_MODEL
    return {
        'x': nrm(ks[0], (BATCH, SEQ, d), 1.0),
        'c': nrm(ks[1], (BATCH, d), 1.0),
        'ctx': nrm(ks[2], (BATCH, CTX_LEN, d), 1.0),
        'c_ctx': nrm(ks[3], (d,), 1.0),
        'w_mod': nrm(ks[4], (DEPTH, d, 6 * d), 0.5 * d ** -0.5),
        'b_mod': nrm(ks[5], (DEPTH, 6 * d), 0.01),
        'w_in': nrm(ks[6], (DEPTH, d, IN_WIDTH), d ** -0.5),
        'mla_q_norm': 1.0 + nrm(ks[7], (DEPTH, MLA_Q_RANK), 0.02),
        'mla_kv_norm': 1.0 + nrm(ks[8], (DEPTH, MLA_KV_RANK), 0.02),
        'w_uq': nrm(ks[9], (DEPTH, MLA_Q_RANK, MLA_HEADS * (MLA_NOPE_DIM + MLA_ROPE_DIM)), MLA_Q_RANK ** -0.5),
        'w_ukv': nrm(ks[10], (DEPTH, MLA_KV_RANK, MLA_HEADS * (MLA_NOPE_DIM + MLA_V_DIM)), MLA_KV_RANK ** -0.5),
        'hg_lb_logits': nrm(ks[11], (2, DEPTH, HG_HEADS * HG_KEY_DIM), 0.1),
        'hg_norm': 1.0 + nrm(ks[12], (DEPTH, HG_VAL_DIM), 0.02),
        'diff_lambda': nrm(ks[13], (DEPTH, 4, DIFF_HEAD_DIM), 0.1),
        'diff_subln': 1.0 + nrm(ks[14], (DEPTH, DIFF_V_DIM), 0.02),
        'w_branch': nrm(ks[15], (DEPTH, N_BRANCHES, BRANCH_WIDTH, d), BRANCH_WIDTH ** -0.5),
        'w_gate': nrm(ks[16], (DEPTH, N_BRANCHES, d, d), d ** -0.5),
        'b_gate': nrm(ks[17], (DEPTH, N_BRANCHES, d), 0.01),
        'w_o': nrm(ks[18], (DEPTH, d, d), DEEPNORM_BETA * d ** -0.5),
        'ln1_g': 1.0 + nrm(ks[19], (DEPTH, d), 0.02),
        'ln1_b': nrm(ks[20], (DEPTH, d), 0.01),
        'w_ff1': nrm(ks[21], (DEPTH, d, 2 * FF_HIDDEN), d ** -0.5),
        'w_ff2': nrm(ks[22], (DEPTH, FF_HIDDEN, d), DEEPNORM_BETA * FF_HIDDEN ** -0.5),
        'ln2_g': 1.0 + nrm(ks[23], (DEPTH, d), 0.02),
        'ln2_b': nrm(ks[24], (DEPTH, d), 0.01),
    }


def reference(x, c, ctx, c_ctx, w_mod, b_mod, w_in, mla_q_norm, mla_kv_norm, w_uq, w_ukv, hg_lb_logits, hg_norm,
              diff_lambda, diff_subln, w_branch, w_gate, b_gate, w_o, ln1_g, ln1_b, w_ff1, w_ff2, ln2_g, ln2_b):
    rows = x.shape[1] // GRID_W
    rope_mla = _axial_rope_tables(rows, MLA_ROPE_DIM)
    rope_diff = _axial_rope_tables(rows, DIFF_HEAD_DIM)
    cum = jnp.cumsum(jax.nn.softmax(hg_lb_logits.astype(jnp.float32), axis=1), axis=1)
    lower_bounds = cum - cum[:, :1]
    c_act = jax.nn.silu(c)
    c_ctx_act = jax.nn.silu(c_ctx)
    for l in range(DEPTH):
        ctx_out = l < DEPTH - 1
        lp = {
            'w_in': w_in[l], 'mla_q_norm': mla_q_norm[l], 'mla_kv_norm': mla_kv_norm[l],
            'w_uq': w_uq[l], 'w_ukv': w_ukv[l], 'hg_norm': hg_norm[l],
            'diff_lambda': diff_lambda[l], 'diff_subln': diff_subln[l],
            'w_branch': w_branch[l], 'w_gate': w_gate[l], 'b_gate': b_gate[l], 'w_o': w_o[l],
        }
        sh1, sc1, g1, sh2, sc2, g2 = jnp.split((c_act @ w_mod[l] + b_mod[l])[:, None, :], 6, axis=-1)
        csh1, csc1, cg1, csh2, csc2, cg2 = jnp.split(c_ctx_act @ w_mod[l] + b_mod[l], 6, axis=-1)
        y_x, y_c = _token_mixer(x * (1 + sc1) + sh1, ctx * (1 + csc1) + csh1, lp, lower_bounds[0, l],
                                lower_bounds[1, l], _diff_lambda_init(l), rope_mla, rope_diff, ctx_out)
        x = _post_norm(x, g1 * y_x, ln1_g[l], ln1_b[l])
        x = _post_norm(x, g2 * _swiglu(x * (1 + sc2) + sh2, w_ff1[l], w_ff2[l]), ln2_g[l], ln2_b[l])
        if ctx_out:
            ctx = _post_norm(ctx, cg1 * y_c, ln1_g[l], ln1_b[l])
            ctx = _post_norm(ctx, cg2 * _swiglu(ctx * (1 + csc2) + csh2, w_ff1[l], w_ff2[l]), ln2_g[l], ln2_b[l])
    return x
```

```python
import ml_dtypes
from concourse.bass_utils import run_bass_kernel_spmd
import numpy as np
import concourse.bass as bass
import concourse.mybir as mybir
from contextlib import ExitStack

F32 = mybir.dt.float32
BF16 = mybir.dt.bfloat16
AF = mybir.ActivationFunctionType
ALU = mybir.AluOpType
AX = mybir.AxisListType

ENGS = ("pe", "act", "dve", "pool", "sp")


class Buf:
    __slots__ = ("name", "w", "r", "semval", "sem", "track", "inc", "slot")

    def __init__(self, name, track=True, inc=16):
        self.name = name
        self.track = track
        self.inc = inc
        self.w = None
        self.r = []
        self.semval = 0
        self.sem = None
        self.slot = None


class V:
    __slots__ = ("ap", "buf")

    def __init__(self, ap, buf):
        self.ap = ap
        self.buf = buf

    def __getitem__(self, idx):
        return V(self.ap[idx], self.buf)

    def m(self, f):
        return V(f(self.ap), self.buf)


class Tl:
    def __init__(self, h, buf):
        self.h = h
        self.buf = buf

    def __getitem__(self, idx):
        return V(self.h[idx], self.buf)

    def ap(self):
        return V(self.h.ap() if hasattr(self.h, "ap") else self.h[:], self.buf)


class Prog:
    def __init__(self, nc):
        self.nc = nc
        self.ops = {e: [] for e in ENGS}
        self.vc = {e: {} for e in ENGS}
        self.seen_d = {e: {} for e in ENGS}
        self.snap = {e: [] for e in ENGS}
        self.dbufs = []
        self.slotval = []
        self.slotfree = []
        self.live = []
        self.epoch = 0
        self.epoch_of = {e: [] for e in ENGS}
        self.epoch_start = {e: 0 for e in ENGS}
        self.last_real = {e: -1 for e in ENGS}
        self.sb_off = 16512
        self.sb_hwm = 0
        self.nbuf = 0
        self.psum = []

    def sbuf(self, name, shape, dtype, nbufs=1):
        esz = mybir.dt.size(dtype)
        per = int(np.prod(shape[1:])) * esz
        per = (per + 63) // 64 * 64
        out = []
        for i in range(nbufs):
            self.nbuf += 1
            nm = f"{name}_{self.nbuf}"
            h = self.nc.alloc_sbuf_tensor_at(nm, list(shape), dtype, offset=self.sb_off)
            self.sb_off += per
            self.sb_hwm = max(self.sb_hwm, self.sb_off)
            assert self.sb_off <= 229376, f"SBUF overflow {self.sb_off}"
            tl = Tl(h, Buf(nm))
            self.live.append((self.sb_off - per, tl.buf))
            out.append(tl)
        return out[0] if nbufs == 1 else out

    def mark(self):
        return self.sb_off

    def release(self, mark):
        keep = []
        for off, b in self.live:
            if off >= mark:
                if b.slot is not None:
                    self.slotval[b.slot] = b.semval
                    self.slotfree.append(b.slot)
                    if b in self.dbufs:
                        self.dbufs.remove(b)
            else:
                keep.append((off, b))
        self.live = keep
        self.sb_off = mark

    def dram(self, name, shape, dtype, kind="Internal"):
        h = self.nc.dram_tensor(name, list(shape), dtype, kind=kind)
        return Tl(h, Buf(name, track=False))

    def _dep_needed(self, eng, dep):
        if dep is None:
            return False
        if dep[0] == "e":
            _, e2, idx = dep
            return self.vc[eng].get(e2, -1) < idx
        else:
            _, buf, val = dep
            return self.seen_d[eng].get(buf, 0) < val

    def _apply_wait(self, eng, dep):
        if dep[0] == "e":
            _, e2, idx = dep
            vc = self.vc[eng]
            for k, v in self.snap[e2][idx].items():
                if vc.get(k, -1) < v:
                    vc[k] = v
            if vc.get(e2, -1) < idx:
                vc[e2] = idx
        else:
            _, buf, val = dep
            self.seen_d[eng][buf] = val

    def op(self, eng, fn, reads=(), writes=(), dma_owner=None, acc=False):
        deps = []
        rb = [x if isinstance(x, Buf) else x.buf for x in reads]
        wb = [x if isinstance(x, Buf) else x.buf for x in writes]
        rb = [b for b in rb if b.track]
        wb = [b for b in wb if b.track]
        for b in rb:
            if b.w is not None:
                deps.append(b.w)
        for b in wb:
            if b.w is not None:
                if not (acc and b.w[0] == "e" and b.w[1] == "pe" and eng == "pe"):
                    deps.append(b.w)
            deps.extend(b.r)
        waits = []
        for d in deps:
            if eng == "pe" and d[0] == "e" and d[1] == "pe":
                continue
            if self._dep_needed(eng, d):
                self._apply_wait(eng, d)
                waits.append(d)
        best = {}
        for d in waits:
            key = (d[0], d[1])
            if key not in best or best[key][2] < d[2]:
                best[key] = d
        waits = list(best.values())
        idx = len(self.ops[eng])
        if dma_owner is not None:
            ob = dma_owner if isinstance(dma_owner, Buf) else dma_owner.buf
            if ob.slot is None:
                if self.slotfree:
                    ob.slot = self.slotfree.pop()
                else:
                    ob.slot = len(self.slotval)
                    self.slotval.append(0)
                ob.semval = self.slotval[ob.slot]
                self.dbufs.append(ob)
            ob.semval += ob.inc
            me = ("d", ob, ob.semval)
        else:
            ob = None
            me = ("e", eng, idx)
        self.ops[eng].append([fn, waits, False, ob])
        self.snap[eng].append(dict(self.vc[eng]))
        self.epoch_of[eng].append(self.epoch)
        if ob is None:
            self.last_real[eng] = idx
        for b in rb:
            b.r.append(me)
        for b in wb:
            b.w = me
            b.r = []
        return me

    def barrier(self):
        last = dict(self.last_real)
        for e in ENGS:
            waits = []
            for e2 in ENGS:
                if e2 != e and last[e2] >= 0:
                    d = ("e", e2, last[e2])
                    if self._dep_needed(e, d):
                        waits.append(d)
            for b in self.dbufs:
                d = ("d", b, b.semval)
                if self._dep_needed(e, d):
                    waits.append(d)
            for d in waits:
                self._apply_wait(e, d)
            self.ops[e].append([None, waits, False, None])
            self.snap[e].append(dict(self.vc[e]))
            self.epoch_of[e].append(self.epoch)
        if max(len(self.ops[e]) - self.epoch_start[e] for e in ENGS) > 20000:
            self.epoch += 1
            for e in ENGS:
                self.epoch_start[e] = len(self.ops[e])

    def dma(self, out, in_, eng="sp", owner=None, **kw):
        if owner is None:
            owner = out if out.buf.track else in_
        assert (owner.buf if not isinstance(owner, Buf) else owner).track
        return self.op(eng, lambda E: E.dma_start(out=out.ap, in_=in_.ap, **kw),
                       reads=[in_], writes=[out], dma_owner=owner)

    def allgather(self, dst, src, groups):
        if not hasattr(self, "ccbuf"):
            self.ccbuf = Buf("cc", track=True, inc=1)
        return self.op("pool", lambda E: E.collective_compute("AllGather", ALU.bypass, replica_groups=groups,
                                                             ins=[src.h.ap().opt()], outs=[dst.h.ap().opt()]),
                       reads=[], writes=[self.ccbuf], dma_owner=self.ccbuf)

    def mm(self, out, lhsT, rhs, start=True, stop=True, **kw):
        return self.op("pe", lambda E: E.matmul(out.ap, lhsT.ap, rhs.ap, start=start, stop=stop, **kw),
                       reads=[lhsT, rhs], writes=[out], acc=not start)

    def act(self, out, in_, func, bias=None, scale=None, accum_out=None, eng="act", extra_reads=()):
        reads = [in_] + list(extra_reads)
        kw = {}
        if bias is not None:
            if isinstance(bias, V):
                reads.append(bias)
                kw["bias"] = bias.ap
            else:
                kw["bias"] = bias
        if scale is not None:
            if isinstance(scale, V):
                reads.append(scale)
                kw["scale"] = scale.ap
            else:
                kw["scale"] = scale
        writes = [out]
        if accum_out is not None:
            kw["accum_out"] = accum_out.ap
            writes.append(accum_out)
        return self.op(eng, lambda E: E.activation(out.ap, in_.ap, func, **kw), reads=reads, writes=writes)

    def tt(self, out, in0, in1, op, eng="dve"):
        return self.op(eng, lambda E: E.tensor_tensor(out.ap, in0.ap, in1.ap, op), reads=[in0, in1], writes=[out])

    def ts(self, out, in0, s1, op0, s2=None, op1=None, eng="dve"):
        reads = [in0]
        a1 = s1
        if isinstance(s1, V):
            reads.append(s1)
            a1 = s1.ap
        a2 = s2
        if isinstance(s2, V):
            reads.append(s2)
            a2 = s2.ap
        if op1 is None:
            return self.op(eng, lambda E: E.tensor_scalar(out.ap, in0.ap, a1, None, op0), reads=reads, writes=[out])
        return self.op(eng, lambda E: E.tensor_scalar(out.ap, in0.ap, a1, a2, op0, op1), reads=reads, writes=[out])

    def stt(self, out, in0, scalar, in1, op0, op1, eng="dve"):
        reads = [in0, in1]
        a = scalar
        if isinstance(scalar, V):
            reads.append(scalar)
            a = scalar.ap
        return self.op(eng, lambda E: E.scalar_tensor_tensor(out.ap, in0.ap, a, in1.ap, op0, op1), reads=reads, writes=[out])

    def copy(self, out, in_, eng="dve"):
        if eng == "act":
            return self.op(eng, lambda E: E.copy(out.ap, in_.ap), reads=[in_], writes=[out])
        return self.op(eng, lambda E: E.tensor_copy(out.ap, in_.ap), reads=[in_], writes=[out])

    def memset(self, out, val, eng="dve"):
        return self.op(eng, lambda E: E.memset(out.ap, val), reads=[], writes=[out])

    def recip(self, out, in_):
        return self.op("dve", lambda E: E.reciprocal(out.ap, in_.ap), reads=[in_], writes=[out])

    def emit(self):
        nc = self.nc
        self.barrier()
        sig = {e: set() for e in ENGS}
        for e in ENGS:
            for fn, waits, _, _ in self.ops[e]:
                for d in waits:
                    if d[0] == "e":
                        sig[d[1]].add(d[2])
        rank = {}
        for e in ENGS:
            cnt = {}
            for idx in sorted(sig[e]):
                ep = self.epoch_of[e][idx]
                cnt[ep] = cnt.get(ep, 0) + 1
                rank[(e, idx)] = cnt[ep]
        with ExitStack() as st:
            esem = {(e, ep): st.enter_context(nc.semaphore(f"s_{e}_{ep}")) for e in ENGS for ep in range(self.epoch + 1)}
            slotsem = [st.enter_context(nc.semaphore(f"d{i}")) for i in range(len(self.slotval))]
            block = st.enter_context(nc.Block())
            engmap = {"pe": block.tensor, "act": block.scalar, "dve": block.vector,
                      "pool": block.gpsimd, "sp": block.sync}
            n_inst = 0
            for e in ENGS:
                ops = self.ops[e]
                n_inst += len(ops)

                def body(E, ops=ops, e=e):
                    for idx, (fn, waits, _, ob) in enumerate(ops):
                        for d in waits:
                            if d[0] == "e":
                                E.wait_ge(esem[(d[1], self.epoch_of[d[1]][d[2]])], rank[(d[1], d[2])])
                            else:
                                E.wait_ge(slotsem[d[1].slot], d[2])
                        if fn is None:
                            continue
                        ins = fn(E)
                        if ob is not None:
                            ins.then_inc(slotsem[ob.slot], ob.inc)
                        elif idx in sig[e]:
                            ins.then_inc(esem[(e, self.epoch_of[e][idx])], 1)

                engmap[e](body)
            self.n_inst = n_inst
        return nc

bf16 = ml_dtypes.bfloat16

D = 1024
FF = 2816
DEPTH = 4
ALPHA = (2 * DEPTH) ** 0.25
LN_EPS = 1e-5


class PS:
    def __init__(self, nc, n=8, prefix="ps"):
        self.b = [Tl(nc.alloc_psum_tensor(f"{prefix}{i}", [128, 512], F32), Buf(f"{prefix}{i}")) for i in range(n)]
        self.i = 0

    def get(self):
        t = self.b[self.i % len(self.b)]
        self.i += 1
        return t


def emit_mod(P, ps, cvec_d, wmod_d, bmod_d, groups, modT=None):
    cv = P.sbuf("cv", [128, 8, 2], F32)
    P.dma(cv[:], cvec_d[:])
    e = P.sbuf("cve", [128, 8, 2], F32)
    P.act(e[:], cv[:], AF.Exp, scale=-1.0)
    P.ts(e[:], e[:], 1.0, ALU.add)
    P.recip(e[:], e[:])
    ca = P.sbuf("ca", [128, 8, 2], F32)
    P.tt(ca[:], cv[:], e[:], ALU.mult)
    bm = P.sbuf("bm", [128, 4 * len(groups)], F32)
    P.dma(bm[:], bmod_d[:])
    if modT is None:
        modT = P.sbuf("modT", [128, 48, 2], F32)
    mk = P.mark()
    wslots = P.sbuf("wm", [128, 4, 1024], F32, nbufs=2)
    for gi, g in enumerate(groups):
        slot = wslots[gi % 2]
        P.dma(slot[:], wmod_d[4 * g:4 * g + 4].m(lambda a: a.rearrange("o p k -> p o k")))
        for j in range(4):
            oc = 4 * g + j
            pt = ps.get()
            for kc in range(8):
                P.mm(pt[:, 0:2], slot[:, j, kc * 128:(kc + 1) * 128], ca[:, kc, :], start=kc == 0, stop=kc == 7)
            P.ts(modT[:, oc, :], pt[:, 0:2], bm[:, oc:oc + 1], ALU.add)
    P.barrier()
    P.release(mk)
    return modT


def emit_ln(P, ps, r, out, W, g, b, onesD, sq, st1, st2, epsT):
    pm = ps.get()
    pq = ps.get()
    for kc in range(8):
        P.act(sq[:, kc, :W], r[:, kc, :W], AF.Square)
    for kc in range(8):
        P.mm(pm[:, :W], onesD[:], r[:, kc, :W], start=kc == 0, stop=kc == 7)
    for kc in range(8):
        P.mm(pq[:, :W], onesD[:], sq[:, kc, :W], start=kc == 0, stop=kc == 7)
    P.copy(st1[:, :W], pm[:, :W], eng="act")
    P.tt(st2[:, :W], st1[:, :W], st1[:, :W], ALU.mult)
    P.tt(st2[:, :W], pq[:, :W], st2[:, :W], ALU.subtract)
    P.act(st2[:, :W], st2[:, :W], AF.Sqrt, bias=epsT[:, 0:1], scale=1.0)
    P.recip(st2[:, :W], st2[:, :W])
    for kc in range(8):
        P.tt(out[:, kc, :W], r[:, kc, :W], st1[:, :W], ALU.subtract)
        P.tt(out[:, kc, :W], out[:, kc, :W], st2[:, :W], ALU.mult)
        P.ts(out[:, kc, :W], out[:, kc, :W], g[:, kc:kc + 1], ALU.mult, b[:, kc:kc + 1], ALU.add)


def emit_T(P, ps, S_half, E, modT, sels, last):
    nc = P.nc
    TT = 128 + S_half
    mk_all = P.mark()
    bg, lnp = E["bg"], E["lnp"]
    SH1, SC1, G1, SH2, SC2, G2 = 0, 8, 16, 24, 32, 40

    bgs = P.sbuf("bgs", [128, 3, 8], F32)
    P.dma(bgs[:], bg[:])
    lns = P.sbuf("lns", [128, 4, 8], F32)
    P.dma(lns[:], lnp[:])
    onesD = P.sbuf("onesD", [128, 128], F32)
    P.memset(onesD[:], 1.0 / D)
    epsT = P.sbuf("epsT", [128, 1], F32)
    P.memset(epsT[:], LN_EPS)

    xt = P.sbuf("xt", [128, 8, 512], F32, nbufs=1)
    xt = [xt, xt]
    yt = P.sbuf("yt", [128, 12, 512], BF16)
    h = P.sbuf("h", [128, 8, 512], BF16)
    m = P.sbuf("m", [128, 8, 512], F32)
    mb = P.sbuf("mb", [128, 8, 512], BF16)
    r = P.sbuf("r", [128, 8, 512], F32)
    sq = P.sbuf("sq", [128, 8, 512], F32)
    x1 = P.sbuf("x1", [128, 8, 512], F32)
    a = P.sbuf("a", [128, 22, 512], BF16)
    st1 = P.sbuf("st1", [128, 512], F32)
    st2 = P.sbuf("st2", [128, 512], F32)
    gate = P.sbuf("gate", [128, 512], F32, nbufs=2)
    tmp = P.sbuf("tmp", [128, 512], F32, nbufs=2)
    wgs = P.sbuf("wgs", [128, 1024], BF16, nbufs=3)
    wbs = P.sbuf("wbs", [128, 512], BF16, nbufs=3)
    wos = P.sbuf("wos", [128, 1024], BF16, nbufs=2)
    wf1s = P.sbuf("wf1s", [128, 1024], BF16, nbufs=4)
    wf2s = P.sbuf("wf2s", [128, 2816], BF16, nbufs=2)

    tiles = [(0, 128, 1)]
    t0 = 128
    while t0 < TT:
        tiles.append((t0, 512, 0))
        t0 += 512
    cnt = {"g": 0, "o": 0, "f1": 0, "f2": 0, "gt": 0}

    def xv(dr, t0, W):
        return dr[:, t0:t0 + W].m(lambda ap: ap.rearrange("(c p) t -> p c t", p=128))

    for ti, (t0, W, cm) in enumerate(tiles):
        X = xt[ti % 2]
        E["xtile_load"](X, t0, W)
        for hsel in range(2):
            E["ycand_load"](yt if hsel == 0 else a, hsel, t0, W, cm)
        P.ts(yt[:, :, :W], yt[:, :, :W], sels[:, 0:1], ALU.mult)
        P.stt(yt[:, :, :W], a[:, 0:12, :W], sels[:, 1:2], yt[:, :, :W], ALU.mult, ALU.add)
        for kc in range(8):
            P.ts(h[:, kc, :W], X[:, kc, :W], modT[:, SC1 + kc, cm:cm + 1], ALU.mult,
                 modT[:, SH1 + kc, cm:cm + 1], ALU.add)
        for oc in range(8):
            for i in range(3):
                wgt = wgs[cnt["g"] % 3]
                wbt = wbs[cnt["g"] % 3]
                cnt["g"] += 1
                P.dma(wgt[:], E["wg"](i, oc))
                P.dma(wbt[:], E["wb"](i, oc))
                pG = ps.get()
                for kc in range(8):
                    P.mm(pG[:, :W], wgt[:, kc * 128:(kc + 1) * 128], h[:, kc, :W], start=kc == 0, stop=kc == 7)
                gt = gate[cnt["gt"] % 2]
                tp = tmp[cnt["gt"] % 2]
                cnt["gt"] += 1
                P.act(gt[:, :W], pG[:, :W], AF.Sigmoid, bias=bgs[:, i, oc:oc + 1])
                pB = ps.get()
                for kc in range(4):
                    P.mm(pB[:, :W], wbt[:, kc * 128:(kc + 1) * 128], yt[:, 4 * i + kc, :W], start=kc == 0, stop=kc == 3)
                if i == 0:
                    P.tt(m[:, oc, :W], gt[:, :W], pB[:, :W], ALU.mult)
                elif i == 1:
                    P.tt(tp[:, :W], gt[:, :W], pB[:, :W], ALU.mult)
                    P.tt(m[:, oc, :W], m[:, oc, :W], tp[:, :W], ALU.add)
                else:
                    P.tt(tp[:, :W], gt[:, :W], pB[:, :W], ALU.mult)
                    P.tt(mb[:, oc, :W], m[:, oc, :W], tp[:, :W], ALU.add)
        for kc in range(8):
            P.ts(X[:, kc, :W], X[:, kc, :W], ALPHA, ALU.mult)
        for oc in range(8):
            wt = wos[cnt["o"] % 2]
            cnt["o"] += 1
            P.dma(wt[:], E["wo"](oc))
            pY = ps.get()
            for kc in range(8):
                P.mm(pY[:, :W], wt[:, kc * 128:(kc + 1) * 128], mb[:, kc, :W], start=kc == 0, stop=kc == 7)
            P.stt(r[:, oc, :W], pY[:, :W], modT[:, G1 + oc, cm:cm + 1], X[:, oc, :W], ALU.mult, ALU.add)
        emit_ln(P, ps, r, x1, W, lns[:, 0, :], lns[:, 1, :], onesD, sq, st1, st2, epsT)
        for kc in range(8):
            P.ts(h[:, kc, :W], x1[:, kc, :W], modT[:, SC2 + kc, cm:cm + 1], ALU.mult,
                 modT[:, SH2 + kc, cm:cm + 1], ALU.add)
        for fc in range(22):
            w1 = wf1s[cnt["f1"] % 4]
            w2 = wf1s[(cnt["f1"] + 1) % 4]
            cnt["f1"] += 2
            P.dma(w1[:], E["wf1"](fc))
            P.dma(w2[:], E["wf1"](22 + fc))
            pG = ps.get()
            for kc in range(8):
                P.mm(pG[:, :W], w1[:, kc * 128:(kc + 1) * 128], h[:, kc, :W], start=kc == 0, stop=kc == 7)
            pU = ps.get()
            for kc in range(8):
                P.mm(pU[:, :W], w2[:, kc * 128:(kc + 1) * 128], h[:, kc, :W], start=kc == 0, stop=kc == 7)
            gt = gate[cnt["gt"] % 2]
            cnt["gt"] += 1
            P.act(gt[:, :W], pG[:, :W], AF.Silu)
            P.tt(a[:, fc, :W], gt[:, :W], pU[:, :W], ALU.mult)
        for kc in range(8):
            P.ts(x1[:, kc, :W], x1[:, kc, :W], ALPHA, ALU.mult)
        for oc in range(8):
            wt = wf2s[cnt["f2"] % 2]
            cnt["f2"] += 1
            P.dma(wt[:], E["wf2"](oc))
            pF = ps.get()
            for fc in range(22):
                P.mm(pF[:, :W], wt[:, fc * 128:(fc + 1) * 128], a[:, fc, :W], start=fc == 0, stop=fc == 21)
            P.stt(r[:, oc, :W], pF[:, :W], modT[:, G2 + oc, cm:cm + 1], x1[:, oc, :W], ALU.mult, ALU.add)
        emit_ln(P, ps, r, m, W, lns[:, 2, :], lns[:, 3, :], onesD, sq, st1, st2, epsT)
        E["xtile_store"](m, t0, W)
        if last and not cm:
            P.dma(xv(E["xout"], t0 - 128, W), m[:, :, :W], eng="pool", owner=m)
    P.barrier()
    P.release(mk_all)


RMS_EPS = 1e-6
MLA_SCALE = 96 ** -0.5
DIFF_SCALE = 64 ** -0.5
HG_SCALE = 128 ** -0.5
CTX = 256


def emit_M(P, ps, S, E, modT):
    nc = P.nc
    dbg = False
    Tt = CTX + S
    NT = Tt // 128
    mk_all = P.mark()
    cvec, wmod, bmod = E["cvec"], E["wmod"], E["bmod"]
    wA, wq, wkv, ng, wD, wH = E["wA"], E["wq"], E["wkv"], E["ng"], E["wD"], E["wH"]
    lbfm, lbrep, dlrep, lcon, hgn, sub = E["lbfm"], E["lbrep"], E["dlrep"], E["lcon"], E["hgn"], E["sub"]
    cmT, smT, cdT, sdT, tri, negm = E["cmT"], E["smT"], E["cdT"], E["sdT"], E["tri"], E["negm"]
    hT, cqnT, ckvnT, kpeT, qTs, qdT, kdT = E["hT"], E["cqnT"], E["ckvnT"], E["kpeT"], E["qTs"], E["qdT"], E["kdT"]
    hqT, hkT, hogT, hg_, hkt, hv = E["hqT"], E["hkT"], E["hogT"], E["hg_"], E["hkt"], E["hv"]
    phases = "ABCD"
    bank = ps.b
    emit_mod(P, ps, cvec, wmod, bmod, list(range(12)), modT=modT)
    P.ts(modT[:, 8:16, :], modT[:, 8:16, :], 1.0, ALU.add)
    P.ts(modT[:, 32:40, :], modT[:, 32:40, :], 1.0, ALU.add)
    SH1, SC1 = 0, 8

    ones32 = P.sbuf("ones32", [128, 128], F32)
    P.memset(ones32[:], 1.0)
    onesb = P.sbuf("onesb", [128, 128], BF16)
    P.memset(onesb[:], 1.0)
    o256 = P.sbuf("o256", [128, 128], F32)
    P.memset(o256[:], 1.0 / 256)
    o128 = P.sbuf("o128", [128, 128], F32)
    P.memset(o128[:], 1.0 / 128)
    epsR = P.sbuf("epsR", [128, 1], F32)
    P.memset(epsR[:], RMS_EPS)
    ngs = P.sbuf("ngs", [128, 3], F32)
    P.dma(ngs[:], ng[:])
    lc = P.sbuf("lc", [128, 8], F32)
    P.dma(lc[:], lcon[:])
    hgns = P.sbuf("hgns", [128, 1], F32)
    P.dma(hgns[:], hgn[:])
    subs = P.sbuf("subs", [128, 1], F32)
    P.dma(subs[:], sub[:])
    P.ts(subs[:], subs[:], lc[:, 5:6], ALU.mult)
    tris = P.sbuf("tris", [128, 4, 128], F32)
    P.dma(tris[:], tri[:])
    lbf = P.sbuf("lbf", [128, 2, 4, 2], F32)
    P.dma(lbf[:], lbfm[:])
    lbr = P.sbuf("lbr", [128, 2, 4, 256], F32)
    P.dma(lbr[:], lbrep[:])
    lb_fm = P.sbuf("lb_fm", [128, 2, 2], F32)
    oml_fm = P.sbuf("oml_fm", [128, 2, 2], F32)
    lb_row = P.sbuf("lb_row", [128, 2, 256], F32)
    oml_row = P.sbuf("oml_row", [128, 2, 256], F32)
    nlam = P.sbuf("nlam", [128, 1], F32)
    mk0 = P.mark()
    for (src, dst, dst1, n) in ((lbf, lb_fm, oml_fm, 2), (lbr, lb_row, oml_row, 256)):
        P.act(src[:], src[:], AF.Exp)
        tot = P.sbuf("lb_tot", [128, 2, n], F32)
        part = P.sbuf("lb_part", [128, 2, n], F32)
        tmp = P.sbuf("lb_tmp", [128, 2, n], F32)
        P.tt(tot[:], src[:, :, 0, :], src[:, :, 1, :], ALU.add)
        P.tt(tot[:], tot[:], src[:, :, 2, :], ALU.add)
        P.tt(tot[:], tot[:], src[:, :, 3, :], ALU.add)
        P.ts(part[:], src[:, :, 0, :], lc[:, 0:1], ALU.mult)
        for j in range(1, 4):
            P.ts(tmp[:], src[:, :, j, :], lc[:, j:j + 1], ALU.mult)
            P.tt(part[:], part[:], tmp[:], ALU.add)
        P.recip(tot[:], tot[:])
        P.tt(dst[:], part[:], tot[:], ALU.mult)
        P.ts(dst1[:], dst[:], -1.0, ALU.mult, 1.0, ALU.add)
    dls = P.sbuf("dls", [128, 4, 64], F32)
    P.dma(dls[:], dlrep[:])
    pr = P.sbuf("dl_pr", [128, 2, 64], F32)
    P.tt(pr[:, 0, :], dls[:, 0, :], dls[:, 1, :], ALU.mult)
    P.tt(pr[:, 1, :], dls[:, 2, :], dls[:, 3, :], ALU.mult)
    sm2 = P.sbuf("dl_sm", [128, 2], F32)
    w_ = 64
    while w_ > 1:
        h_ = w_ // 2
        P.tt(pr[:, :, 0:h_], pr[:, :, 0:h_], pr[:, :, h_:w_], ALU.add)
        w_ = h_
    P.copy(sm2[:], pr[:, :, 0])
    P.act(sm2[:], sm2[:], AF.Exp)
    P.tt(nlam[:], sm2[:, 1:2], sm2[:, 0:1], ALU.subtract)
    P.tt(nlam[:], nlam[:], lc[:, 4:5], ALU.subtract)
    P.barrier()
    P.release(mk0)

    tiles = [(0, 256, 1)] + [(CTX + 512 * i, 512, 0) for i in range(S // 512)]

    def fm(dr, t0, W, nch):
        if nch == 1:
            return dr[:, t0:t0 + W]
        return dr[:, t0:t0 + W].m(lambda ap: ap.rearrange("(c p) t -> p c t", p=128))

    def proj_fm(pt, prow, w_of_kc, hsrc, W, KC=8):
        for kc in range(KC):
            P.mm(pt[0:prow, :W], w_of_kc(kc), hsrc(kc), start=kc == 0, stop=kc == KC - 1)

    mk = P.mark()
    xt = P.sbuf("xt", [128, 8, 512], F32, nbufs=2)
    hb = P.sbuf("hb", [128, 8, 512], BF16, nbufs=2)
    for ti, (t0, W, cm) in enumerate(tiles):
        X = xt[ti % 2]
        H = hb[ti % 2]
        E["xload"](X, ti, t0, W)
        for kc in range(8):
            P.ts(H[:, kc, :W], X[:, kc, :W], modT[:, SC1 + kc, cm:cm + 1], ALU.mult,
                 modT[:, SH1 + kc, cm:cm + 1], ALU.add)
        P.dma(fm(hT, t0, W, 8), H[:, :, :W], eng="pool")
    P.barrier()
    P.release(mk)

    def rope(o1, o2, a, b, cs, sn, W, n, t1, t2):
        P.tt(t1[0:n, :W], a, cs, ALU.mult)
        P.tt(t2[0:n, :W], b, sn, ALU.mult)
        P.tt(o1, t1[0:n, :W], t2[0:n, :W], ALU.subtract)
        P.tt(t1[0:n, :W], a, sn, ALU.mult)
        P.tt(t2[0:n, :W], b, cs, ALU.mult)
        P.tt(o2, t1[0:n, :W], t2[0:n, :W], ALU.add)

    if "A" in phases:
        mk = P.mark()
        wAs = P.sbuf("wAs", [128, 8, 416], BF16)
        P.dma(wAs[:], wA[:])
        hb = P.sbuf("hbA", [128, 8, 512], BF16, nbufs=2)
        cq = P.sbuf("cq", [128, 3, 512], F32)
        sq = P.sbuf("sqA", [128, 3, 512], F32)
        rs = P.sbuf("rsA", [128, 2, 512], F32)
        cqn = P.sbuf("cqn", [128, 3, 512], BF16, nbufs=2)
        cs = P.sbuf("csA", [16, 2, 512], F32, nbufs=2)
        t1 = P.sbuf("t1A", [16, 512], F32)
        t2 = P.sbuf("t2A", [16, 512], F32)
        kp = P.sbuf("kpA", [16, 2, 512], BF16, nbufs=2)
        for ti, (t0, W, cm) in enumerate(tiles):
            H = hb[ti % 2]
            CN = cqn[ti % 2]
            CS = cs[ti % 2]
            KP = kp[ti % 2]
            P.dma(H[:, :, :W], fm(hT, t0, W, 8))
            P.dma(CS[:, 0, :W], cmT[:, t0:t0 + W])
            P.dma(CS[:, 1, :W], smT[:, t0:t0 + W])
            for oc in range(3):
                pt = ps.get()
                proj_fm(pt, 128, lambda kc: wAs[:, kc, oc * 128:(oc + 1) * 128], lambda kc: H[:, kc, :W], W)
                P.copy(cq[:, oc, :W], pt[:, :W], eng="act")
                P.act(sq[:, oc, :W], pt[:, :W], AF.Square)
            p1 = ps.get()
            P.mm(p1[:, :W], o256[:], sq[:, 0, :W], start=True, stop=False)
            P.mm(p1[:, :W], o256[:], sq[:, 1, :W], start=False, stop=True)
            p2 = ps.get()
            P.mm(p2[:, :W], o128[:], sq[:, 2, :W], start=True, stop=True)
            for j, pp in enumerate((p1, p2)):
                P.act(rs[:, j, :W], pp[:, :W], AF.Sqrt, bias=epsR[:, 0:1], scale=1.0)
                P.recip(rs[:, j, :W], rs[:, j, :W])
            for oc in range(3):
                P.stt(CN[:, oc, :W], cq[:, oc, :W], ngs[:, oc:oc + 1], rs[:, 0 if oc < 2 else 1, :W], ALU.mult, ALU.mult)
            P.dma(fm(cqnT, t0, W, 2), CN[:, 0:2, :W], eng="pool")
            P.dma(ckvnT[:, t0:t0 + W], CN[:, 2, :W], eng="pool")
            pa = ps.get()
            proj_fm(pa, 16, lambda kc: wAs[:, kc, 384:400], lambda kc: H[:, kc, :W], W)
            pb = ps.get()
            proj_fm(pb, 16, lambda kc: wAs[:, kc, 400:416], lambda kc: H[:, kc, :W], W)
            rope(KP[:, 0, :W], KP[:, 1, :W], pa[0:16, :W], pb[0:16, :W], CS[:, 0, :W], CS[:, 1, :W], W, 16, t1, t2)
            P.dma(kpeT[0:16, t0:t0 + W], KP[:, 0, :W], eng="pool")
            P.dma(kpeT[16:32, t0:t0 + W], KP[:, 1, :W], eng="pool")
        P.barrier()
        P.release(mk)

    if "B" in phases:
        mk = P.mark()
        wqs = P.sbuf("wqs", [128, 2, 4, 96], BF16)
        P.dma(wqs[:], wq[:])
        wkvs = P.sbuf("wkvs", [128, 4, 128], BF16)
        P.dma(wkvs[:], wkv[:])
        KT = P.sbuf("KT", [96, Tt], BF16)
        VA = P.sbuf("VA", [128, NT, 65], BF16)
        cqb = P.sbuf("cqb", [128, 3, 512], BF16, nbufs=2)
        cs = P.sbuf("csB", [16, 2, 512], F32, nbufs=2)
        t1 = P.sbuf("t1B", [16, 512], F32)
        t2 = P.sbuf("t2B", [16, 512], F32)
        qn = P.sbuf("qnB", [64, 512], BF16, nbufs=2)
        qp = P.sbuf("qpB", [16, 2, 512], BF16, nbufs=2)
        qt = P.sbuf("qtB", [96, 512], BF16, nbufs=2)
        pT = P.sbuf("pTB", [128, 512], BF16, nbufs=3)
        rsum = P.sbuf("rsumB", [128, 512], F32)
        rc = P.sbuf("rcB", [64, 512], F32)
        yo = P.sbuf("yoB", [64, 512], BF16, nbufs=2)
        psS = [bank[0], bank[1], bank[2]]
        psO = [bank[3], bank[4]]
        psX = [bank[5], bank[6], bank[7]]
        cS = cO = cX = cP = 0
        for hh in range(4):
            P.memset(VA[:, :, 64:65], 1.0)
            for ti, (t0, W, cm) in enumerate(tiles):
                C = cqb[ti % 2]
                CS = cs[ti % 2]
                P.dma(C[:, 0:2, :W], fm(cqnT, t0, W, 2))
                P.dma(C[:, 2, :W], ckvnT[:, t0:t0 + W])
                P.dma(CS[:, 0, :W], cmT[:, t0:t0 + W])
                P.dma(CS[:, 1, :W], smT[:, t0:t0 + W])
                pt = psX[cX % 3]; cX += 1
                proj_fm(pt, 64, lambda kc: wqs[:, kc, hh, 0:64], lambda kc: C[:, kc, :W], W, KC=2)
                QN = qn[ti % 2]
                P.copy(QN[:, :W], pt[0:64, :W], eng="act")
                P.dma(qTs[0:64, t0:t0 + W], QN[:, :W], eng="pool")
                pa = psX[cX % 3]; cX += 1
                proj_fm(pa, 16, lambda kc: wqs[:, kc, hh, 64:80], lambda kc: C[:, kc, :W], W, KC=2)
                pb = psX[cX % 3]; cX += 1
                proj_fm(pb, 16, lambda kc: wqs[:, kc, hh, 80:96], lambda kc: C[:, kc, :W], W, KC=2)
                QP = qp[ti % 2]
                rope(QP[:, 0, :W], QP[:, 1, :W], pa[0:16, :W], pb[0:16, :W], CS[:, 0, :W], CS[:, 1, :W], W, 16, t1, t2)
                P.dma(qTs[64:80, t0:t0 + W], QP[:, 0, :W], eng="pool")
                P.dma(qTs[80:96, t0:t0 + W], QP[:, 1, :W], eng="pool")
                pk = psX[cX % 3]; cX += 1
                P.mm(pk[0:64, :W], wkvs[:, hh, 0:64], C[:, 2, :W])
                P.copy(KT[0:64, t0:t0 + W], pk[0:64, :W], eng="act")
                pv = psX[cX % 3]; cX += 1
                for j in range(W // 128):
                    P.mm(pv[:, j * 64:(j + 1) * 64], C[:, 2, j * 128:(j + 1) * 128], wkvs[:, hh, 64:128])
                kt0 = t0 // 128
                P.copy(VA[:, kt0:kt0 + W // 128, 0:64],
                       pv[:, 0:(W // 128) * 64].m(lambda ap: ap.rearrange("p (j d) -> p j d", d=64)))
            P.dma(KT[64:96, :], kpeT[:, :])
            P.barrier()
            for ti, (t0, W, cm) in enumerate(tiles):
                Q = qt[ti % 2]
                P.dma(Q[:, :W], qTs[:, t0:t0 + W])
                nk = 2 if cm else NT
                po = psO[cO % 2]; cO += 1
                for kt in range(nk):
                    pS = psS[cS % 3]; cS += 1
                    P.mm(pS[:, :W], KT[:, kt * 128:(kt + 1) * 128], Q[:, :W])
                    pt_ = pT[cP % 3]; cP += 1
                    P.act(pt_[:, :W], pS[:, :W], AF.Exp, scale=MLA_SCALE)
                    P.mm(po[0:65, :W], VA[:, kt, :], pt_[:, :W], start=kt == 0, stop=kt == nk - 1)
                P.copy(rsum[64:65, :W], po[64:65, :W])
                pbx = psX[cX % 3]; cX += 1
                P.mm(pbx[0:64, :W], ones32[64:65, 0:64], rsum[64:65, :W])
                P.recip(rc[:, :W], pbx[0:64, :W])
                Y = yo[ti % 2]
                P.tt(Y[:, :W], po[0:64, :W], rc[:, :W], ALU.mult)
                E["ywrite"](hh * 64, 64, t0, W, Y[:, :W])
            P.barrier()
        P.release(mk)

    if "C" in phases:
        mk = P.mark()
        wDs = P.sbuf("wDs", [128, 8, 3, 128], BF16)
        K12 = P.sbuf("K12", [64, 2, Tt], BF16)
        VD = P.sbuf("VD", [128, NT, 128], BF16)
        hb = P.sbuf("hbC", [128, 8, 512], BF16, nbufs=2)
        cs = P.sbuf("csC", [32, 2, 512], F32, nbufs=2)
        t1 = P.sbuf("t1C", [32, 512], F32)
        t2 = P.sbuf("t2C", [32, 512], F32)
        rp = P.sbuf("rpC", [32, 8, 512], BF16, nbufs=2)
        qt = P.sbuf("qtC", [64, 2, 512], BF16, nbufs=2)
        pT = P.sbuf("pTC", [128, 512], BF16, nbufs=3)
        o1 = P.sbuf("o1C", [128, 512], F32)
        o2 = P.sbuf("o2C", [128, 512], F32)
        rcp = P.sbuf("rcpC", [128, 512], F32)
        sqd = P.sbuf("sqdC", [128, 512], F32)
        yo = P.sbuf("yoC", [128, 512], BF16, nbufs=2)
        psS = [bank[0], bank[1], bank[2]]
        psO = [bank[3], bank[4]]
        psM = [bank[5], bank[6]]
        psX = [bank[7]]
        cS = cP = 0
        for hh in range(2):
            P.dma(wDs[:], wD[:, :, hh])
            for ti, (t0, W, cm) in enumerate(tiles):
                H = hb[ti % 2]
                CS = cs[ti % 2]
                R = rp[ti % 2]
                P.dma(H[:, :, :W], fm(hT, t0, W, 8))
                P.dma(CS[:, 0, :W], cdT[:, t0:t0 + W])
                P.dma(CS[:, 1, :W], sdT[:, t0:t0 + W])
                for w in range(2):
                    for j in range(2):
                        pa = psS[cS % 3]; cS += 1
                        proj_fm(pa, 32, lambda kc: wDs[:, kc, w, j * 64:j * 64 + 32], lambda kc: H[:, kc, :W], W)
                        pb = psS[cS % 3]; cS += 1
                        proj_fm(pb, 32, lambda kc: wDs[:, kc, w, j * 64 + 32:j * 64 + 64], lambda kc: H[:, kc, :W], W)
                        i0 = (w * 2 + j) * 2
                        rope(R[:, i0, :W], R[:, i0 + 1, :W], pa[0:32, :W], pb[0:32, :W], CS[:, 0, :W], CS[:, 1, :W], W, 32, t1, t2)
                        dst = qdT if w == 0 else kdT
                        P.dma(dst[j, 0:32, t0:t0 + W], R[:, i0, :W], eng="pool")
                        P.dma(dst[j, 32:64, t0:t0 + W], R[:, i0 + 1, :W], eng="pool")
                pv = psX[0]
                for j in range(W // 128):
                    for kc in range(8):
                        P.mm(pv[:, j * 128:(j + 1) * 128], H[:, kc, j * 128:(j + 1) * 128], wDs[:, kc, 2, :], start=kc == 0, stop=kc == 7)
                kt0 = t0 // 128
                P.copy(VD[:, kt0:kt0 + W // 128, :], pv[:, 0:W].m(lambda ap: ap.rearrange("p (j d) -> p j d", d=128)))
            P.barrier()
            P.dma(K12[:, 0, :], kdT[0])
            P.dma(K12[:, 1, :], kdT[1])
            for ti, (t0, W, cm) in enumerate(tiles):
                Q = qt[ti % 2]
                P.dma(Q[:, 0, :W], qdT[0, :, t0:t0 + W])
                P.dma(Q[:, 1, :W], qdT[1, :, t0:t0 + W])
                nk = 2 if cm else NT
                for j in range(2):
                    po = psO[j]
                    pm = psM[j]
                    for kt in range(nk):
                        pS = psS[cS % 3]; cS += 1
                        P.mm(pS[:, :W], K12[:, j, kt * 128:(kt + 1) * 128], Q[:, j, :W])
                        pt_ = pT[cP % 3]; cP += 1
                        P.act(pt_[:, :W], pS[:, :W], AF.Exp, scale=DIFF_SCALE)
                        P.mm(po[:, :W], VD[:, kt, :], pt_[:, :W], start=kt == 0, stop=kt == nk - 1)
                        P.mm(pm[:, :W], onesb[:], pt_[:, :W], start=kt == 0, stop=kt == nk - 1)
                    P.recip(rcp[:, :W], pm[:, :W])
                    P.tt((o1 if j == 0 else o2)[:, :W], po[:, :W], rcp[:, :W], ALU.mult)
                P.stt(o1[:, :W], o2[:, :W], nlam[:, 0:1], o1[:, :W], ALU.mult, ALU.add)
                P.act(sqd[:, :W], o1[:, :W], AF.Square)
                px = psX[0]
                P.mm(px[:, :W], o128[:], sqd[:, :W])
                P.act(rcp[:, :W], px[:, :W], AF.Sqrt, bias=epsR[:, 0:1], scale=1.0)
                P.recip(rcp[:, :W], rcp[:, :W])
                if dbg and hh == 1 and ti == 1:
                    dvd = _pd("dbg_vd", [128, NT, 128], BF16, kind="ExternalOutput")
                    P.dma(dvd[:], VD[:], eng="pool")
                    do1 = _pd("dbg_o1", [128, 512], F32, kind="ExternalOutput")
                    P.dma(do1[:], o1[:], eng="pool")
                    do2 = _pd("dbg_o2", [128, 512], F32, kind="ExternalOutput")
                    P.dma(do2[:], o2[:], eng="pool")
                    dnl = _pd("dbg_nl", [128, 1], F32, kind="ExternalOutput")
                    P.dma(dnl[:], nlam[:], eng="pool")
                Y = yo[ti % 2]
                P.stt(Y[:, :W], o1[:, :W], subs[:, 0:1], rcp[:, :W], ALU.mult, ALU.mult)
                E["ywrite"](512 + hh * 128, 128, t0, W, Y[:, :W])
            P.barrier()
        P.release(mk)

    if "D" in phases:
        mk = P.mark()
        wHs = P.sbuf("wHs", [128, 8, 5, 128], BF16)
        hb = P.sbuf("hbD", [128, 8, 512], BF16, nbufs=2)
        sg = P.sbuf("sgD", [128, 512], F32, nbufs=2)
        ofm = P.sbuf("ofmD", [128, 4, 512], BF16, nbufs=2)
        ztm = P.sbuf("ztmD", [128, 4, 3, 128], F32)
        gtm = P.sbuf("gtmD", [128, 4, 2, 128], F32, nbufs=2)
        ktm = P.sbuf("ktmD", [128, 4, 3, 128], BF16, nbufs=2)
        Oacc = P.sbuf("Oacc", [128, Tt], F32)
        qb = P.sbuf("qbD", [128, 512], BF16, nbufs=2)
        kb_ = P.sbuf("kbD", [128, 512], BF16, nbufs=2)
        gb = P.sbuf("gbD", [128, 4, 128], F32, nbufs=2)
        kt_ = P.sbuf("ktD", [128, 4, 128], BF16, nbufs=2)
        vb = P.sbuf("vbD", [128, 4, 128], BF16, nbufs=2)
        Eh = P.sbuf("EhD", [128, 512], F32)
        ER = P.sbuf("ERD", [128, 4, 128], F32)
        Qh = P.sbuf("QhD", [128, 512], BF16)
        Kh = P.sbuf("KhD", [128, 4, 128], BF16)
        Qt = P.sbuf("QtD", [128, 512], BF16)
        Dq = P.sbuf("DqD", [128, 512], F32)
        Dk = P.sbuf("DkD", [128, 8, 4, 64], F32)
        Kall = P.sbuf("KallD", [128, 8, 4, 128], BF16)
        P.memset(Kall[:], 0.0)
        refs = P.sbuf("refsD", [128, 8, 4], F32)
        negs = P.sbuf("negsD", [128, 2, 4, 64], F32)
        P.dma(negs[:], negm[:])
        PTt = P.sbuf("PTD", [128, 4, 128], BF16)
        Sf = P.sbuf("SfD", [128, 128], F32)
        Sb = P.sbuf("SbD", [128, 128], BF16)
        sqh = P.sbuf("sqhD", [128, 512], F32)
        rch = P.sbuf("rchD", [128, 512], F32)
        ogb = P.sbuf("ogbD", [128, 512], BF16, nbufs=2)
        yo = P.sbuf("yoD", [128, 512], BF16, nbufs=2)
        pX = [bank[0], bank[1], bank[2], bank[3]]
        cX = 0
        for hh in range(2):
            P.dma(wHs[:], wH[:, :, hh])
            for ti, (t0, W, cm) in enumerate(tiles):
                H = hb[ti % 2]
                OF = ofm[ti % 2]
                GT = gtm[ti % 2]
                KTm = ktm[ti % 2]
                P.dma(H[:, :, :W], fm(hT, t0, W, 8))
                pt = pX[cX % 4]; cX += 1
                proj_fm(pt, 128, lambda kc: wHs[:, kc, 0, :], lambda kc: H[:, kc, :W], W)
                S_ = sg[0]
                P.act(S_[:, :W], pt[:, :W], AF.Silu)
                P.ts(OF[:, 0, :W], S_[:, :W], HG_SCALE, ALU.mult)
                for d in range(2):
                    pt = pX[cX % 4]; cX += 1
                    proj_fm(pt, 128, lambda kc: wHs[:, kc, 1 + d, :], lambda kc: H[:, kc, :W], W)
                    S_ = sg[1]
                    P.act(S_[:, :W], pt[:, :W], AF.Sigmoid, scale=-1.0)
                    P.ts(OF[:, 1 + d, :W], S_[:, :W], oml_fm[:, d, hh:hh + 1], ALU.mult)
                pt = pX[cX % 4]; cX += 1
                proj_fm(pt, 128, lambda kc: wHs[:, kc, 4, :], lambda kc: H[:, kc, :W], W)
                P.act(OF[:, 3, :W], pt[:, :W], AF.Silu)
                P.dma(hqT[:, t0:t0 + W], OF[:, 0, :W], eng="pool")
                P.dma(hkT[0, :, t0:t0 + W], OF[:, 1, :W], eng="pool")
                P.dma(hkT[1, :, t0:t0 + W], OF[:, 2, :W], eng="pool")
                P.dma(hogT[:, t0:t0 + W], OF[:, 3, :W], eng="pool")
                nj = W // 128
                for j in range(nj):
                    pt = pX[cX % 4]; cX += 1
                    for c3, wi in enumerate((1, 2, 3)):
                        for kc in range(8):
                            P.mm(pt[:, c3 * 128:(c3 + 1) * 128], H[:, kc, j * 128:(j + 1) * 128], wHs[:, kc, wi, :],
                                 start=kc == 0, stop=kc == 7)
                    P.act(ztm[:, j, 0:2, :], pt[:, 0:256].m(lambda ap: ap.rearrange("p (a d) -> p a d", d=128)), AF.Sigmoid)
                    P.copy(KTm[:, j, 2, :], pt[:, 256:384], eng="act")
                for d in range(2):
                    lbv = lb_row[:, d, hh * 128:(hh + 1) * 128]
                    omv = oml_row[:, d, hh * 128:(hh + 1) * 128]
                    for j in range(nj):
                        P.tt(ztm[:, j, d, :], ztm[:, j, d, :], omv, ALU.mult)
                        P.tt(ztm[:, j, d, :], ztm[:, j, d, :], lbv, ALU.add)
                    P.act(GT[:, :nj, d, :], ztm[:, :nj, d, :], AF.Ln)
                    P.ts(KTm[:, :nj, d, :], ztm[:, :nj, d, :], -1.0, ALU.mult, 1.0, ALU.add)
                for d in range(2):
                    P.dma(hg_[d, t0:t0 + W, :].m(lambda ap: ap.rearrange("(j p) k -> p j k", p=128)), GT[:, :nj, d, :], eng="pool")
                    P.dma(hkt[d, t0:t0 + W, :].m(lambda ap: ap.rearrange("(j p) k -> p j k", p=128)), KTm[:, :nj, d, :], eng="pool")
                P.dma(hv[t0:t0 + W, :].m(lambda ap: ap.rearrange("(j p) k -> p j k", p=128)), KTm[:, :nj, 2, :], eng="pool")
            P.barrier()
            for d in range(2):
                TI = tris[:, d, :]
                TS = tris[:, 2 + d, :]
                order = list(range(len(tiles))) if d == 0 else [0] + list(range(len(tiles) - 1, 0, -1))
                P.memset(Sf[:], 0.0)
                P.memset(Sb[:], 0.0)
                for bi, ti in enumerate(order):
                    t0, W, cm = tiles[ti]
                    nj = W // 128
                    QB, KB, GB, KTB, VB = qb[bi % 2], kb_[bi % 2], gb[bi % 2], kt_[bi % 2], vb[bi % 2]
                    P.dma(QB[:, :W], hqT[:, t0:t0 + W])
                    P.dma(KB[:, :W], hkT[d, :, t0:t0 + W])
                    P.dma(GB[:, :nj, :], hg_[d, t0:t0 + W, :].m(lambda ap: ap.rearrange("(j p) k -> p j k", p=128)))
                    P.dma(KTB[:, :nj, :], hkt[d, t0:t0 + W, :].m(lambda ap: ap.rearrange("(j p) k -> p j k", p=128)))
                    P.dma(VB[:, :nj, :], hv[t0:t0 + W, :].m(lambda ap: ap.rearrange("(j p) k -> p j k", p=128)))
                    pA = bank[4]
                    pR = bank[5]
                    for j in range(nj):
                        P.mm(pA[:, j * 128:(j + 1) * 128], GB[:, j, :], TI)
                        P.mm(pR[:, j * 128:(j + 1) * 128], TS, GB[:, j, :])
                    P.act(Eh[:, :W], pA[:, :W], AF.Exp)
                    P.act(ER[:, :nj, :], pR[:, :W].m(lambda ap: ap.rearrange("p (j k) -> p j k", k=128)), AF.Exp)
                    P.tt(Qh[:, :W], QB[:, :W], Eh[:, :W], ALU.mult)
                    P.tt(Kh[:, :nj, :], KTB[:, :nj, :], ER[:, :nj, :], ALU.mult)
                    ncb = W // 64
                    nsb = W // 16
                    pA3 = pA[:, :W].m(lambda ap: ap.rearrange("p (c s) -> p c s", s=64))
                    P.memset(refs[:], 0.0)
                    if d == 0:
                        P.copy(refs[:, :ncb, 1:4], pA3[:, :, 15:48:16])
                    else:
                        P.copy(refs[:, :ncb, 0:3], pA3[:, :, 16:64:16])
                    rflat = refs[:, :ncb, :].m(lambda ap: ap.rearrange("p c i -> p (c i)"))
                    P.tt(Dq[:, :W].m(lambda ap: ap.rearrange("p (b s) -> p b s", s=16)),
                         pA[:, :W].m(lambda ap: ap.rearrange("p (b s) -> p b s", s=16)),
                         rflat.m(lambda ap: ap.unsqueeze(2).broadcast_to([128, nsb, 16])), ALU.subtract)
                    P.act(Dq[:, :W], Dq[:, :W], AF.Exp)
                    P.tt(Qt[:, :W], QB[:, :W], Dq[:, :W], ALU.mult)
                    P.tt(Dk[:, :ncb, :, :],
                         refs[:, :ncb, :].m(lambda ap: ap.unsqueeze(3).broadcast_to([128, ncb, 4, 64])),
                         pA3.m(lambda ap: ap.unsqueeze(2).broadcast_to([128, ncb, 4, 64])), ALU.subtract)
                    P.tt(Dk[:, :ncb, :, :], Dk[:, :ncb, :, :],
                         negs[:, d, :, :].m(lambda ap: ap.unsqueeze(1).broadcast_to([128, ncb, 4, 64])), ALU.add)
                    P.act(Dk[:, :ncb, :, :], Dk[:, :ncb, :, :], AF.Exp)
                    for hf in range(2):
                        P.tt(Kall[:, :ncb, :, :].m(lambda ap: ap.rearrange("p (j h) i s -> p j h i s", h=2)[:, :, hf, :, hf * 64:(hf + 1) * 64]),
                             Dk[:, :ncb, :, :].m(lambda ap: ap.rearrange("p (j h) i s -> p j h i s", h=2)[:, :, hf, :, :]),
                             KB[:, :W].m(lambda ap: ap.rearrange("p (j h s) -> p j h s", h=2, s=64)[:, :, hf, :].unsqueeze(2).broadcast_to([128, nj, 4, 64])),
                             ALU.mult)
                    pS = bank[6]
                    for j in range(nj):
                        for Ig in range(8):
                            col = j * 128 + Ig * 16
                            P.mm(pS[:, col:col + 16], Kall[:, 2 * j + Ig // 4, Ig % 4, :], Qt[:, col:col + 16])
                        P.tt(PTt[:, j, :], pS[:, j * 128:(j + 1) * 128], TI, ALU.mult)
                    pO = bank[7]
                    chunks = list(range(W // 64))
                    if d == 1:
                        chunks = chunks[::-1]
                    for c in chunks:
                        c0 = c * 64
                        j, hf = c // 2, c % 2
                        pb0 = 64 * hf
                        P.mm(pO[:, c0:c0 + 64], VB[pb0:pb0 + 64, j, :], PTt[pb0:pb0 + 64, j, pb0:pb0 + 64], start=True, stop=False)
                        P.mm(pO[:, c0:c0 + 64], Sb[:], Qh[:, c0:c0 + 64], start=False, stop=True)
                        pU = pX[cX % 4]; cX += 1
                        P.mm(pU[:, 0:128], Kh[pb0:pb0 + 64, j, :], VB[pb0:pb0 + 64, j, :])
                        ecol = c0 + 63 if d == 0 else c0
                        P.stt(Sf[:], Sf[:], Eh[:, ecol:ecol + 1], pU[:, 0:128], ALU.mult, ALU.add)
                        P.copy(Sb[:], Sf[:], eng="act")
                    if d == 0:
                        P.copy(Oacc[:, t0:t0 + W], pO[:, :W])
                    else:
                        P.tt(Oacc[:, t0:t0 + W], Oacc[:, t0:t0 + W], pO[:, :W], ALU.add)
            for ti, (t0, W, cm) in enumerate(tiles):
                OG = ogb[ti % 2]
                P.dma(OG[:, :W], hogT[:, t0:t0 + W])
                P.act(sqh[:, :W], Oacc[:, t0:t0 + W], AF.Square)
                px = pX[cX % 4]; cX += 1
                P.mm(px[:, :W], o128[:], sqh[:, :W])
                P.act(rch[:, :W], px[:, :W], AF.Sqrt, bias=epsR[:, 0:1], scale=1.0)
                P.recip(rch[:, :W], rch[:, :W])
                P.stt(rch[:, :W], Oacc[:, t0:t0 + W], hgns[:, 0:1], rch[:, :W], ALU.mult, ALU.mult)
                Y = yo[ti % 2]
                P.tt(Y[:, :W], rch[:, :W], OG[:, :W], ALU.mult)
                E["ywrite"](256 + hh * 128, 128, t0, W, Y[:, :W])
            P.barrier()
        P.release(mk)
    P.barrier()
    P.release(mk_all)


PAIRS = [[0, 1], [2, 3], [4, 5], [6, 7]]
W_SIZES = [("wA", 8 * 416), ("wq", 2 * 4 * 96), ("wkv", 4 * 128), ("wD", 8 * 2 * 3 * 128), ("wH", 8 * 2 * 5 * 128),
           ("wg", 3 * 8 * 1024), ("wb", 3 * 8 * 512), ("wo", 8 * 1024), ("wf1", 44 * 1024), ("wf2", 8 * 2816)]
W_LAYER = sum(n for _, n in W_SIZES)
NW = 4 * W_LAYER


def w_off(l, key):
    o = l * W_LAYER
    for k, n in W_SIZES:
        if k == key:
            return o, n
        o += n
    raise KeyError(key)


def build_fused(S):
    nc = bass.Bass("TRN2", target_bir_lowering=False)
    P = Prog(nc)
    Tt = CTX + S
    Sh = S // 2
    TT = 128 + Sh
    inp = lambda n, sh, dt=F32: P.dram(n, sh, dt, kind="ExternalInput")
    x_own = inp("x_own", [1024, TT])
    wall = inp("wall", [128, NW])
    cvec = inp("cvec", [128, 8, 2])
    wmod = inp("wmod", [4, 48, 128, 1024])
    bmod = inp("bmod", [4, 128, 48])
    ng = inp("ng", [4, 128, 3])
    lbfm = inp("lbfm", [128, 2, 4, 2])
    lbrep = inp("lbrep", [128, 2, 4, 256])
    dlrep = inp("dlrep", [4, 128, 4, 64])
    lcon = inp("lcon", [4, 128, 8])
    hgn = inp("hgn", [4, 128, 1])
    sub = inp("sub", [4, 128, 1])
    cmT = inp("cmT", [16, Tt])
    smT = inp("smT", [16, Tt])
    cdT = inp("cdT", [32, Tt])
    sdT = inp("sdT", [32, Tt])
    tri = inp("tri", [128, 4, 128])
    negm = inp("negm", [128, 2, 4, 64])
    bg = inp("bg", [4, 128, 3, 8])
    lnp = inp("lnp", [4, 128, 4, 8])
    sel = inp("sel", [128, 2])
    xout = P.dram("xout", [1024, Sh], F32, kind="ExternalOutput")

    w16 = P.dram("w16", [128, NW], BF16)
    XW = 4 * 512
    xcb = [0] + [min(TT, 128 + XW * (j + 1)) for j in range((Sh + XW - 1) // XW)]
    nxc = len(xcb) - 1
    xo_c = [[P.dram(f"xo_{c}_{j}", [128, xcb[j + 1] - xcb[j]], F32) for j in range(nxc)] for c in range(8)]
    xg_c = [[P.dram(f"xg_{c}_{j}", [256, xcb[j + 1] - xcb[j]], F32) for j in range(nxc)] for c in range(8)]
    ycb = [0, CTX + Sh, Tt]
    yp_c = [[P.dram(f"yp_{c}_{j}", [128, ycb[j + 1] - ycb[j]], BF16) for j in range(2)] for c in range(6)]
    ya_c = [[P.dram(f"ya_{c}_{j}", [256, ycb[j + 1] - ycb[j]], BF16) for j in range(2)] for c in range(6)]

    def xchunk(t0):
        for j in range(nxc):
            if xcb[j] <= t0 < xcb[j + 1]:
                return j, t0 - xcb[j]
        raise ValueError(t0)

    def xtile_load(X, t0, W):
        j, o = xchunk(t0)
        for c in range(8):
            P.dma(X[:, c, :W], xo_c[c][j][:, o:o + W])

    def xtile_store(M_, t0, W):
        j, o = xchunk(t0)
        for c in range(8):
            P.dma(xo_c[c][j][:, o:o + W], M_[:, c, :W], eng="pool", owner=M_)

    def ywrite(row0, nrows, t0, W, src):
        c, po = row0 // 128, row0 % 128
        j = 0 if t0 < ycb[1] else 1
        o = t0 - ycb[j]
        P.dma(yp_c[c][j][po:po + nrows, o:o + W], src, eng="pool")

    def ycand_load(dst, hsel, t0, W, cm):
        gc = hsel * 128 if cm else CTX + hsel * Sh + (t0 - 128)
        j = 0 if gc < ycb[1] else 1
        o = gc - ycb[j]
        for i in range(3):
            for rr in range(2):
                for cc in range(2):
                    P.dma(dst[:, 4 * i + 2 * rr + cc, :W], ya_c[2 * i + cc][j][rr * 128:(rr + 1) * 128, o:o + W])

    def exchange_x():
        for c in range(8):
            for j in range(nxc):
                P.allgather(xg_c[c][j], xo_c[c][j], PAIRS)
        P.barrier()

    def exchange_y():
        for c in range(6):
            for j in range(2):
                P.allgather(ya_c[c][j], yp_c[c][j], PAIRS)
        P.barrier()

    scr = {
        "hT": P.dram("s_hT", [1024, Tt], BF16), "cqnT": P.dram("s_cqn", [256, Tt], BF16),
        "ckvnT": P.dram("s_ckvn", [128, Tt], BF16), "kpeT": P.dram("s_kpe", [32, Tt], BF16),
        "qTs": P.dram("s_q", [96, Tt], BF16), "qdT": P.dram("s_qd", [2, 64, Tt], BF16),
        "kdT": P.dram("s_kd", [2, 64, Tt], BF16), "hqT": P.dram("s_hq", [128, Tt], BF16),
        "hkT": P.dram("s_hk", [2, 128, Tt], BF16), "hogT": P.dram("s_hog", [128, Tt], BF16),
        "hg_": P.dram("s_hg", [2, Tt, 128], F32), "hkt": P.dram("s_hkt", [2, Tt, 128], BF16),
        "hv": P.dram("s_hv", [Tt, 128], BF16),
    }
    ps = PS(nc)
    modT = P.sbuf("modT", [128, 48, 2], F32)
    sels = P.sbuf("sels", [128, 2], F32)
    P.dma(sels[:], sel[:])

    mk = P.mark()
    CH = 2048
    ca = P.sbuf("ca", [128, CH], F32, nbufs=3)
    cb = P.sbuf("cb", [128, CH], BF16, nbufs=3)
    for i in range(NW // CH):
        A = ca[i % 3]
        B = cb[i % 3]
        P.dma(A[:], wall[:, i * CH:(i + 1) * CH])
        P.copy(B[:], A[:], eng="dve" if i % 2 == 0 else "act")
        P.dma(w16[:, i * CH:(i + 1) * CH], B[:], eng="pool")
    xb = P.sbuf("xb", [128, 8, 512], F32, nbufs=2)
    for i, (t0, W) in enumerate([(0, 128)] + [(128 + 512 * q, 512) for q in range(Sh // 512)]):
        X = xb[i % 2]
        P.dma(X[:, :, :W], x_own[:, t0:t0 + W].m(lambda ap: ap.rearrange("(c p) t -> p c t", p=128)))
        xtile_store(X, t0, W)
    P.barrier()
    P.release(mk)
    exchange_x()

    def xload(X, ti, t0, W):
        if ti == 0:
            for r in range(2):
                for c in range(8):
                    P.dma(X[:, c, r * 128:(r + 1) * 128], xg_c[c][0][r * 128:(r + 1) * 128, 0:128])
        else:
            tok = t0 - CTX
            r = tok // Sh
            j, o = xchunk(128 + tok - r * Sh)
            for c in range(8):
                P.dma(X[:, c, :W], xg_c[c][j][r * 128:(r + 1) * 128, o:o + W])

    def wv(l, key, pat=None, **kw):
        o, n = w_off(l, key)
        v = w16[:, o:o + n]
        if pat is not None:
            v = v.m(lambda ap: ap.rearrange(pat, **kw))
        return v

    for l in range(4):
        wg_v = wv(l, "wg", "p (i o k) -> p i o k", i=3, o=8)
        wb_v = wv(l, "wb", "p (i o k) -> p i o k", i=3, o=8)
        wo_v = wv(l, "wo", "p (o k) -> p o k", o=8)
        wf1_v = wv(l, "wf1", "p (o k) -> p o k", o=44)
        wf2_v = wv(l, "wf2", "p (o k) -> p o k", o=8)
        E = dict(scr)
        E.update({
            "cvec": cvec, "wmod": wmod[l], "bmod": bmod[l], "ng": ng[l],
            "wA": wv(l, "wA", "p (k n) -> p k n", k=8), "wq": wv(l, "wq", "p (k h n) -> p k h n", k=2, h=4),
            "wkv": wv(l, "wkv", "p (h n) -> p h n", h=4),
            "wD": wv(l, "wD", "p (k h w n) -> p k h w n", k=8, h=2, w=3),
            "wH": wv(l, "wH", "p (k h w n) -> p k h w n", k=8, h=2, w=5),
            "lbfm": lbfm, "lbrep": lbrep, "dlrep": dlrep[l], "lcon": lcon[l], "hgn": hgn[l], "sub": sub[l],
            "cmT": cmT, "smT": smT, "cdT": cdT, "sdT": sdT, "tri": tri, "negm": negm,
            "xload": xload, "ywrite": ywrite, "xtile_load": xtile_load, "xtile_store": xtile_store,
            "ycand_load": ycand_load, "bg": bg[l], "lnp": lnp[l], "xout": xout,
            "wg": (lambda i, oc, v=wg_v: v[:, i, oc, :]), "wb": (lambda i, oc, v=wb_v: v[:, i, oc, :]),
            "wo": (lambda oc, v=wo_v: v[:, oc, :]), "wf1": (lambda fc, v=wf1_v: v[:, fc, :]),
            "wf2": (lambda oc, v=wf2_v: v[:, oc, :]),
        })
        emit_M(P, ps, S, E, modT)
        exchange_y()
        emit_T(P, ps, Sh, E, modT, sels, last=(l == 3))
        if l < 3:
            exchange_x()
    P.emit()
    return nc, P

bf16 = ml_dtypes.bfloat16
def lay_oc(W, mo=128):
    K, N = W.shape
    KC, NC = K // 128, N // mo
    return np.ascontiguousarray(W.reshape(KC, 128, NC, mo).transpose(2, 1, 0, 3).reshape(NC, 128, KC * mo))
def lay_k(W):
    K, N = W.shape
    return np.ascontiguousarray(W.reshape(K // 128, 128, N).transpose(1, 0, 2))
def vec_chunks(v):
    n = v.shape[-1] // 128
    return np.ascontiguousarray(v.reshape(n, 128).T)

def rope_tables(S, rot_dim, ctx=256):
    rows = S // 64
    row, col = np.meshgrid(np.arange(rows, dtype=np.float32), np.arange(64, dtype=np.float32), indexing='ij')
    n_freq = rot_dim // 4
    inv_freq = (np.float32(10000.0) ** (-np.arange(n_freq, dtype=np.float32) / np.float32(n_freq))).astype(np.float32)
    ang = np.concatenate([row.reshape(-1, 1) * inv_freq, col.reshape(-1, 1) * inv_freq], axis=-1).astype(np.float32)
    cos = np.concatenate([np.ones((ctx, rot_dim // 2), np.float32), np.cos(ang)], 0)
    sin = np.concatenate([np.zeros((ctx, rot_dim // 2), np.float32), np.sin(ang)], 0)
    return np.ascontiguousarray(cos.T), np.ascontiguousarray(sin.T)

def tri_consts():
    t = np.zeros((128, 4, 128), np.float32)
    for blk in range(2):
        o = blk * 64
        for s in range(64):
            for u in range(64):
                t[o + s, 0, o + u] = 1.0 if s <= u else 0.0
                t[o + s, 1, o + u] = 1.0 if s >= u else 0.0
                t[o + s, 2, o + u] = 1.0 if s > u else 0.0
                t[o + s, 3, o + u] = 1.0 if s < u else 0.0
    return t

def neg_consts():
    n = np.zeros((128, 2, 4, 64), np.float32)
    for I in range(4):
        for s in range(64):
            n[:, 0, I, s] = 0.0 if s < (I + 1) * 16 else -1e4
            n[:, 1, I, s] = 0.0 if s >= I * 16 else -1e4
    return n

def m_inputs(inp, l, b, g, xfull, S, W16=None, xT=None):
    import math
    if W16 is None:
        Wk = lay_k(inp["w_in"][l])
        heads2 = [2 * g, 2 * g + 1]
        wD = np.stack([np.stack([Wk[:, :, base + h * 128: base + (h + 1) * 128] for base in (2976, 3488, 4000)], 2) for h in heads2], 2)
        wH = np.stack([np.stack([Wk[:, :, base + h * 128: base + (h + 1) * 128] for base in (416, 928, 1440, 1952, 2464)], 2) for h in heads2], 2)
        wq = lay_k(inp["w_uq"][l]).reshape(128, 2, 8, 96)[:, :, 4 * g:4 * g + 4]
        wkv = lay_k(inp["w_ukv"][l]).reshape(128, 8, 128)[:, 4 * g:4 * g + 4]
        cast = lambda a: np.ascontiguousarray(a).astype(bf16)
        wd = {"wA": cast(Wk[:, :, 0:416]), "wq": cast(wq), "wkv": cast(wkv), "wD": cast(wD), "wH": cast(wH)}
    else:
        wd = {"wA": W16[("wA", l)], "wq": W16[("wq", l, g)], "wkv": W16[("wkv", l, g)], "wD": W16[("wD", l, g)], "wH": W16[("wH", l, g)]}
    if xT is None:
        xT = np.ascontiguousarray(xfull.T)
    heads2 = [2 * g, 2 * g + 1]
    lg = inp["hg_lb_logits"]
    lbfm = np.stack([lg[:, :, h * 128:(h + 1) * 128] for h in heads2], -1)
    lbfm = np.ascontiguousarray(lbfm.transpose(2, 0, 1, 3))
    lbrep = np.ascontiguousarray(np.broadcast_to(lg[None, :, :, 2 * g * 128: 2 * g * 128 + 256], (128, 2, 4, 256)))
    lam_init = 0.8 - 0.6 * math.exp(-0.3 * l)
    lcon = np.zeros((128, 8), np.float32)
    for j in range(4):
        lcon[:, j] = 1.0 if 1 <= j <= l else 0.0
    lcon[:, 4] = lam_init
    lcon[:, 5] = 1.0 - lam_init
    cm, sm = rope_tables(S, 32)
    cd, sd = rope_tables(S, 64)
    d = {
        "xT": xT,
        "cvec": np.ascontiguousarray(np.stack([vec_chunks(inp["c"][b]), vec_chunks(inp["c_ctx"])], -1)),
        "wmod": lay_oc(inp["w_mod"][l]), "bmod": vec_chunks(inp["b_mod"][l]),
        "ng": np.ascontiguousarray(np.concatenate([vec_chunks(inp["mla_q_norm"][l]), inp["mla_kv_norm"][l][:, None]], 1)),
        "lbfm": lbfm, "lbrep": lbrep,
        "dlrep": np.ascontiguousarray(np.broadcast_to(inp["diff_lambda"][l][None], (128, 4, 64))),
        "lcon": lcon, "hgn": np.ascontiguousarray(inp["hg_norm"][l][:, None]), "sub": np.ascontiguousarray(inp["diff_subln"][l][:, None]),
        "cmT": cm, "smT": sm, "cdT": cd, "sdT": sd, "tri": tri_consts(), "negm": neg_consts(),
    }
    d.update(wd)
    return d

_PROGS = {}


def _pack_weights(inp, g):
    out = np.zeros((128, NW), np.float32)
    for l in range(4):
        Wk = lay_k(inp["w_in"][l])
        heads2 = [2 * g, 2 * g + 1]
        wD = np.stack([np.stack([Wk[:, :, base + h * 128: base + (h + 1) * 128] for base in (2976, 3488, 4000)], 2) for h in heads2], 2)
        wH = np.stack([np.stack([Wk[:, :, base + h * 128: base + (h + 1) * 128] for base in (416, 928, 1440, 1952, 2464)], 2) for h in heads2], 2)
        parts = {
            "wA": Wk[:, :, 0:416],
            "wq": lay_k(inp["w_uq"][l]).reshape(128, 2, 8, 96)[:, :, 4 * g:4 * g + 4],
            "wkv": lay_k(inp["w_ukv"][l]).reshape(128, 8, 128)[:, 4 * g:4 * g + 4],
            "wD": wD, "wH": wH,
            "wg": np.stack([lay_oc(inp["w_gate"][l, i]) for i in range(3)]).transpose(2, 0, 1, 3),
            "wb": np.stack([lay_oc(inp["w_branch"][l, i]) for i in range(3)]).transpose(2, 0, 1, 3),
            "wo": lay_oc(inp["w_o"][l]).transpose(1, 0, 2),
            "wf1": lay_oc(inp["w_ff1"][l]).transpose(1, 0, 2),
            "wf2": lay_oc(inp["w_ff2"][l]).transpose(1, 0, 2),
        }
        for k, n in W_SIZES:
            o, _ = w_off(l, k)
            out[:, o:o + n] = parts[k].reshape(128, n)
    return out


def kernel(x, c, ctx, c_ctx, w_mod, b_mod, w_in, mla_q_norm, mla_kv_norm, w_uq, w_ukv, hg_lb_logits, hg_norm,
           diff_lambda, diff_subln, w_branch, w_gate, b_gate, w_o, ln1_g, ln1_b, w_ff1, w_ff2, ln2_g, ln2_b):
    import math
    inp = dict(x=x, c=c, ctx=ctx, c_ctx=c_ctx, w_mod=w_mod, b_mod=b_mod, w_in=w_in, mla_q_norm=mla_q_norm,
               mla_kv_norm=mla_kv_norm, w_uq=w_uq, w_ukv=w_ukv, hg_lb_logits=hg_lb_logits, hg_norm=hg_norm,
               diff_lambda=diff_lambda, diff_subln=diff_subln, w_branch=w_branch, w_gate=w_gate, b_gate=b_gate,
               w_o=w_o, ln1_g=ln1_g, ln1_b=ln1_b, w_ff1=w_ff1, w_ff2=w_ff2, ln2_g=ln2_g, ln2_b=ln2_b)
    inp = {k: np.asarray(v, dtype=np.float32) for k, v in inp.items()}
    B, S = inp["x"].shape[0], inp["x"].shape[1]
    Sh = S // 2
    if S not in _PROGS:
        _PROGS[S] = build_fused(S)[0]
    nc = _PROGS[S]
    wpk = [_pack_weights(inp, g) for g in range(2)]
    wmod = np.stack([lay_oc(inp["w_mod"][l]) for l in range(4)])
    bmod = np.stack([vec_chunks(inp["b_mod"][l]) for l in range(4)])
    ng = np.stack([np.concatenate([vec_chunks(inp["mla_q_norm"][l]), inp["mla_kv_norm"][l][:, None]], 1) for l in range(4)])
    dlrep = np.stack([np.broadcast_to(inp["diff_lambda"][l][None], (128, 4, 64)) for l in range(4)])
    lcon = np.zeros((4, 128, 8), np.float32)
    for l in range(4):
        lam_init = 0.8 - 0.6 * math.exp(-0.3 * l)
        for j in range(4):
            lcon[l, :, j] = 1.0 if 1 <= j <= l else 0.0
        lcon[l, :, 4] = lam_init
        lcon[l, :, 5] = 1.0 - lam_init
    hgn = np.stack([inp["hg_norm"][l][:, None] for l in range(4)])
    sub = np.stack([inp["diff_subln"][l][:, None] for l in range(4)])
    bg = np.stack([np.stack([vec_chunks(inp["b_gate"][l, i]) for i in range(3)], 1) for l in range(4)])
    lnp = np.stack([np.stack([vec_chunks(inp[k][l]) for k in ("ln1_g", "ln1_b", "ln2_g", "ln2_b")], 1) for l in range(4)])
    cm, sm = rope_tables(S, 32)
    cd, sd = rope_tables(S, 64)
    tri, negm = tri_consts(), neg_consts()
    lg = inp["hg_lb_logits"]
    ins = []
    for core in range(8):
        b, g = core // 2, core % 2
        heads2 = [2 * g, 2 * g + 1]
        lbfm = np.ascontiguousarray(np.stack([lg[:, :, h * 128:(h + 1) * 128] for h in heads2], -1).transpose(2, 0, 1, 3))
        lbrep = np.ascontiguousarray(np.broadcast_to(lg[None, :, :, 2 * g * 128: 2 * g * 128 + 256], (128, 2, 4, 256)))
        x_own = np.concatenate([inp["ctx"][b, g * 128:(g + 1) * 128], inp["x"][b, g * Sh:(g + 1) * Sh]], 0).T
        selv = np.zeros((128, 2), np.float32)
        selv[:, g] = 1.0
        ins.append({
            "x_own": np.ascontiguousarray(x_own), "wall": wpk[g],
            "cvec": np.ascontiguousarray(np.stack([vec_chunks(inp["c"][b]), vec_chunks(inp["c_ctx"])], -1)),
            "wmod": wmod, "bmod": np.ascontiguousarray(bmod), "ng": np.ascontiguousarray(ng),
            "lbfm": lbfm, "lbrep": lbrep, "dlrep": np.ascontiguousarray(dlrep), "lcon": lcon,
            "hgn": np.ascontiguousarray(hgn), "sub": np.ascontiguousarray(sub),
            "cmT": cm, "smT": sm, "cdT": cd, "sdT": sd, "tri": tri, "negm": negm,
            "bg": np.ascontiguousarray(bg), "lnp": np.ascontiguousarray(lnp), "sel": selv,
        })
    res = run_bass_kernel_spmd(nc, ins, core_ids=list(range(8)))
    out = np.empty((B, S, 1024), np.float32)
    for core in range(8):
        b, g = core // 2, core % 2
        out[b, g * Sh:(g + 1) * Sh] = np.asarray(res.results[core]["xout"]).T
    return out
```

```python
import ml_dtypes
from concourse.bass_utils import run_bass_kernel_spmd
import numpy as np
import concourse.bass as bass
import concourse.mybir as mybir
from contextlib import ExitStack

F32 = mybir.dt.float32
BF16 = mybir.dt.bfloat16
AF = mybir.ActivationFunctionType
ALU = mybir.AluOpType
AX = mybir.AxisListType

ENGS = ("pe", "act", "dve", "pool", "sp")


class Buf:
    __slots__ = ("name", "w", "r", "semval", "sem", "track", "inc", "slot")

    def __init__(self, name, track=True, inc=16):
        self.name = name
        self.track = track
        self.inc = inc
        self.w = None
        self.r = []
        self.semval = 0
        self.sem = None
        self.slot = None


class V:
    __slots__ = ("ap", "buf")

    def __init__(self, ap, buf):
        self.ap = ap
        self.buf = buf

    def __getitem__(self, idx):
        return V(self.ap[idx], self.buf)

    def m(self, f):
        return V(f(self.ap), self.buf)


class Tl:
    def __init__(self, h, buf):
        self.h = h
        self.buf = buf

    def __getitem__(self, idx):
        return V(self.h[idx], self.buf)

    def ap(self):
        return V(self.h.ap() if hasattr(self.h, "ap") else self.h[:], self.buf)


class Prog:
    def __init__(self, nc):
        self.nc = nc
        self.ops = {e: [] for e in ENGS}
        self.vc = {e: {} for e in ENGS}
        self.seen_d = {e: {} for e in ENGS}
        self.snap = {e: [] for e in ENGS}
        self.dbufs = []
        self.slotval = []
        self.slotfree = []
        self.live = []
        self.epoch = 0
        self.epoch_of = {e: [] for e in ENGS}
        self.epoch_start = {e: 0 for e in ENGS}
        self.last_real = {e: -1 for e in ENGS}
        self.sb_off = 16512
        self.sb_hwm = 0
        self.nbuf = 0
        self.psum = []

    def sbuf(self, name, shape, dtype, nbufs=1):
        esz = mybir.dt.size(dtype)
        per = int(np.prod(shape[1:])) * esz
        per = (per + 63) // 64 * 64
        out = []
        for i in range(nbufs):
            self.nbuf += 1
            nm = f"{name}_{self.nbuf}"
            h = self.nc.alloc_sbuf_tensor_at(nm, list(shape), dtype, offset=self.sb_off)
            self.sb_off += per
            self.sb_hwm = max(self.sb_hwm, self.sb_off)
            assert self.sb_off <= 229376, f"SBUF overflow {self.sb_off}"
            tl = Tl(h, Buf(nm))
            self.live.append((self.sb_off - per, tl.buf))
            out.append(tl)
        return out[0] if nbufs == 1 else out

    def mark(self):
        return self.sb_off

    def release(self, mark):
        keep = []
        for off, b in self.live:
            if off >= mark:
                if b.slot is not None:
                    self.slotval[b.slot] = b.semval
                    self.slotfree.append(b.slot)
                    if b in self.dbufs:
                        self.dbufs.remove(b)
            else:
                keep.append((off, b))
        self.live = keep
        self.sb_off = mark

    def dram(self, name, shape, dtype, kind="Internal"):
        h = self.nc.dram_tensor(name, list(shape), dtype, kind=kind)
        return Tl(h, Buf(name, track=False))

    def _dep_needed(self, eng, dep):
        if dep is None:
            return False
        if dep[0] == "e":
            _, e2, idx = dep
            return self.vc[eng].get(e2, -1) < idx
        else:
            _, buf, val = dep
            return self.seen_d[eng].get(buf, 0) < val

    def _apply_wait(self, eng, dep):
        if dep[0] == "e":
            _, e2, idx = dep
            vc = self.vc[eng]
            for k, v in self.snap[e2][idx].items():
                if vc.get(k, -1) < v:
                    vc[k] = v
            if vc.get(e2, -1) < idx:
                vc[e2] = idx
        else:
            _, buf, val = dep
            self.seen_d[eng][buf] = val

    def op(self, eng, fn, reads=(), writes=(), dma_owner=None, acc=False):
        deps = []
        rb = [x if isinstance(x, Buf) else x.buf for x in reads]
        wb = [x if isinstance(x, Buf) else x.buf for x in writes]
        rb = [b for b in rb if b.track]
        wb = [b for b in wb if b.track]
        for b in rb:
            if b.w is not None:
                deps.append(b.w)
        for b in wb:
            if b.w is not None:
                if not (acc and b.w[0] == "e" and b.w[1] == "pe" and eng == "pe"):
                    deps.append(b.w)
            deps.extend(b.r)
        waits = []
        for d in deps:
            if eng == "pe" and d[0] == "e" and d[1] == "pe":
                continue
            if self._dep_needed(eng, d):
                self._apply_wait(eng, d)
                waits.append(d)
        best = {}
        for d in waits:
            key = (d[0], d[1])
            if key not in best or best[key][2] < d[2]:
                best[key] = d
        waits = list(best.values())
        idx = len(self.ops[eng])
        if dma_owner is not None:
            ob = dma_owner if isinstance(dma_owner, Buf) else dma_owner.buf
            if ob.slot is None:
                if self.slotfree:
                    ob.slot = self.slotfree.pop()
                else:
                    ob.slot = len(self.slotval)
                    self.slotval.append(0)
                ob.semval = self.slotval[ob.slot]
                self.dbufs.append(ob)
            ob.semval += ob.inc
            me = ("d", ob, ob.semval)
        else:
            ob = None
            me = ("e", eng, idx)
        self.ops[eng].append([fn, waits, False, ob])
        self.snap[eng].append(dict(self.vc[eng]))
        self.epoch_of[eng].append(self.epoch)
        if ob is None:
            self.last_real[eng] = idx
        for b in rb:
            b.r.append(me)
        for b in wb:
            b.w = me
            b.r = []
        return me

    def barrier(self):
        last = dict(self.last_real)
        for e in ENGS:
            waits = []
            for e2 in ENGS:
                if e2 != e and last[e2] >= 0:
                    d = ("e", e2, last[e2])
                    if self._dep_needed(e, d):
                        waits.append(d)
            for b in self.dbufs:
                d = ("d", b, b.semval)
                if self._dep_needed(e, d):
                    waits.append(d)
            for d in waits:
                self._apply_wait(e, d)
            self.ops[e].append([None, waits, False, None])
            self.snap[e].append(dict(self.vc[e]))
            self.epoch_of[e].append(self.epoch)
        if max(len(self.ops[e]) - self.epoch_start[e] for e in ENGS) > 20000:
            self.epoch += 1
            for e in ENGS:
                self.epoch_start[e] = len(self.ops[e])

    def dma(self, out, in_, eng="sp", owner=None, **kw):
        if owner is None:
            owner = out if out.buf.track else in_
        assert (owner.buf if not isinstance(owner, Buf) else owner).track
        return self.op(eng, lambda E: E.dma_start(out=out.ap, in_=in_.ap, **kw),
                       reads=[in_], writes=[out], dma_owner=owner)

    def allgather(self, dst, src, groups):
        if not hasattr(self, "ccbuf"):
            self.ccbuf = Buf("cc", track=True, inc=1)
        return self.op("pool", lambda E: E.collective_compute("AllGather", ALU.bypass, replica_groups=groups,
                                                             ins=[src.h.ap().opt()], outs=[dst.h.ap().opt()]),
                       reads=[], writes=[self.ccbuf], dma_owner=self.ccbuf)

    def mm(self, out, lhsT, rhs, start=True, stop=True, **kw):
        return self.op("pe", lambda E: E.matmul(out.ap, lhsT.ap, rhs.ap, start=start, stop=stop, **kw),
                       reads=[lhsT, rhs], writes=[out], acc=not start)

    def act(self, out, in_, func, bias=None, scale=None, accum_out=None, eng="act", extra_reads=()):
        reads = [in_] + list(extra_reads)
        kw = {}
        if bias is not None:
            if isinstance(bias, V):
                reads.append(bias)
                kw["bias"] = bias.ap
            else:
                kw["bias"] = bias
        if scale is not None:
            if isinstance(scale, V):
                reads.append(scale)
                kw["scale"] = scale.ap
            else:
                kw["scale"] = scale
        writes = [out]
        if accum_out is not None:
            kw["accum_out"] = accum_out.ap
            writes.append(accum_out)
        return self.op(eng, lambda E: E.activation(out.ap, in_.ap, func, **kw), reads=reads, writes=writes)

    def tt(self, out, in0, in1, op, eng="dve"):
        return self.op(eng, lambda E: E.tensor_tensor(out.ap, in0.ap, in1.ap, op), reads=[in0, in1], writes=[out])

    def ts(self, out, in0, s1, op0, s2=None, op1=None, eng="dve"):
        reads = [in0]
        a1 = s1
        if isinstance(s1, V):
            reads.append(s1)
            a1 = s1.ap
        a2 = s2
        if isinstance(s2, V):
            reads.append(s2)
            a2 = s2.ap
        if op1 is None:
            return self.op(eng, lambda E: E.tensor_scalar(out.ap, in0.ap, a1, None, op0), reads=reads, writes=[out])
        return self.op(eng, lambda E: E.tensor_scalar(out.ap, in0.ap, a1, a2, op0, op1), reads=reads, writes=[out])

    def stt(self, out, in0, scalar, in1, op0, op1, eng="dve"):
        reads = [in0, in1]
        a = scalar
        if isinstance(scalar, V):
            reads.append(scalar)
            a = scalar.ap
        return self.op(eng, lambda E: E.scalar_tensor_tensor(out.ap, in0.ap, a, in1.ap, op0, op1), reads=reads, writes=[out])

    def copy(self, out, in_, eng="dve"):
        if eng == "act":
            return self.op(eng, lambda E: E.copy(out.ap, in_.ap), reads=[in_], writes=[out])
        return self.op(eng, lambda E: E.tensor_copy(out.ap, in_.ap), reads=[in_], writes=[out])

    def memset(self, out, val, eng="dve"):
        return self.op(eng, lambda E: E.memset(out.ap, val), reads=[], writes=[out])

    def recip(self, out, in_):
        return self.op("dve", lambda E: E.reciprocal(out.ap, in_.ap), reads=[in_], writes=[out])

    def emit(self):
        nc = self.nc
        self.barrier()
        sig = {e: set() for e in ENGS}
        for e in ENGS:
            for fn, waits, _, _ in self.ops[e]:
                for d in waits:
                    if d[0] == "e":
                        sig[d[1]].add(d[2])
        rank = {}
        for e in ENGS:
            cnt = {}
            for idx in sorted(sig[e]):
                ep = self.epoch_of[e][idx]
                cnt[ep] = cnt.get(ep, 0) + 1
                rank[(e, idx)] = cnt[ep]
        with ExitStack() as st:
            esem = {(e, ep): st.enter_context(nc.semaphore(f"s_{e}_{ep}")) for e in ENGS for ep in range(self.epoch + 1)}
            slotsem = [st.enter_context(nc.semaphore(f"d{i}")) for i in range(len(self.slotval))]
            block = st.enter_context(nc.Block())
            engmap = {"pe": block.tensor, "act": block.scalar, "dve": block.vector,
                      "pool": block.gpsimd, "sp": block.sync}
            n_inst = 0
            for e in ENGS:
                ops = self.ops[e]
                n_inst += len(ops)

                def body(E, ops=ops, e=e):
                    for idx, (fn, waits, _, ob) in enumerate(ops):
                        for d in waits:
                            if d[0] == "e":
                                E.wait_ge(esem[(d[1], self.epoch_of[d[1]][d[2]])], rank[(d[1], d[2])])
                            else:
                                E.wait_ge(slotsem[d[1].slot], d[2])
                        if fn is None:
                            continue
                        ins = fn(E)
                        if ob is not None:
                            ins.then_inc(slotsem[ob.slot], ob.inc)
                        elif idx in sig[e]:
                            ins.then_inc(esem[(e, self.epoch_of[e][idx])], 1)

                engmap[e](body)
            self.n_inst = n_inst
        return nc

bf16 = ml_dtypes.bfloat16

D = 1024
FF = 2816
DEPTH = 4
ALPHA = (2 * DEPTH) ** 0.25
LN_EPS = 1e-5


class PS:
    def __init__(self, nc, n=8, prefix="ps"):
        self.b = [Tl(nc.alloc_psum_tensor(f"{prefix}{i}", [128, 512], F32), Buf(f"{prefix}{i}")) for i in range(n)]
        self.i = 0

    def get(self):
        t = self.b[self.i % len(self.b)]
        self.i += 1
        return t


def emit_mod(P, ps, cvec_d, wmod_d, bmod_d, groups, modT=None):
    cv = P.sbuf("cv", [128, 8, 2], F32)
    P.dma(cv[:], cvec_d[:])
    e = P.sbuf("cve", [128, 8, 2], F32)
    P.act(e[:], cv[:], AF.Exp, scale=-1.0)
    P.ts(e[:], e[:], 1.0, ALU.add)
    P.recip(e[:], e[:])
    ca = P.sbuf("ca", [128, 8, 2], F32)
    P.tt(ca[:], cv[:], e[:], ALU.mult)
    bm = P.sbuf("bm", [128, 4 * len(groups)], F32)
    P.dma(bm[:], bmod_d[:])
    if modT is None:
        modT = P.sbuf("modT", [128, 48, 2], F32)
    mk = P.mark()
    wslots = P.sbuf("wm", [128, 4, 1024], F32, nbufs=2)
    for gi, g in enumerate(groups):
        slot = wslots[gi % 2]
        P.dma(slot[:], wmod_d[4 * g:4 * g + 4].m(lambda a: a.rearrange("o p k -> p o k")))
        for j in range(4):
            oc = 4 * g + j
            pt = ps.get()
            for kc in range(8):
                P.mm(pt[:, 0:2], slot[:, j, kc * 128:(kc + 1) * 128], ca[:, kc, :], start=kc == 0, stop=kc == 7)
            P.ts(modT[:, oc, :], pt[:, 0:2], bm[:, oc:oc + 1], ALU.add)
    P.barrier()
    P.release(mk)
    return modT


def emit_ln(P, ps, r, out, W, g, b, onesD, sq, st1, st2, epsT):
    pm = ps.get()
    pq = ps.get()
    for kc in range(8):
        P.act(sq[:, kc, :W], r[:, kc, :W], AF.Square)
    for kc in range(8):
        P.mm(pm[:, :W], onesD[:], r[:, kc, :W], start=kc == 0, stop=kc == 7)
    for kc in range(8):
        P.mm(pq[:, :W], onesD[:], sq[:, kc, :W], start=kc == 0, stop=kc == 7)
    P.copy(st1[:, :W], pm[:, :W], eng="act")
    P.tt(st2[:, :W], st1[:, :W], st1[:, :W], ALU.mult)
    P.tt(st2[:, :W], pq[:, :W], st2[:, :W], ALU.subtract)
    P.act(st2[:, :W], st2[:, :W], AF.Sqrt, bias=epsT[:, 0:1], scale=1.0)
    P.recip(st2[:, :W], st2[:, :W])
    for kc in range(8):
        P.tt(out[:, kc, :W], r[:, kc, :W], st1[:, :W], ALU.subtract)
        P.tt(out[:, kc, :W], out[:, kc, :W], st2[:, :W], ALU.mult)
        P.ts(out[:, kc, :W], out[:, kc, :W], g[:, kc:kc + 1], ALU.mult, b[:, kc:kc + 1], ALU.add)


def emit_T(P, ps, S_half, E, modT, sels, last):
    nc = P.nc
    TT = 128 + S_half
    mk_all = P.mark()
    bg, lnp = E["bg"], E["lnp"]
    SH1, SC1, G1, SH2, SC2, G2 = 0, 8, 16, 24, 32, 40

    bgs = P.sbuf("bgs", [128, 3, 8], F32)
    P.dma(bgs[:], bg[:])
    lns = P.sbuf("lns", [128, 4, 8], F32)
    P.dma(lns[:], lnp[:])
    onesD = P.sbuf("onesD", [128, 128], F32)
    P.memset(onesD[:], 1.0 / D)
    epsT = P.sbuf("epsT", [128, 1], F32)
    P.memset(epsT[:], LN_EPS)

    xt = P.sbuf("xt", [128, 8, 512], F32, nbufs=1)
    xt = [xt, xt]
    yt = P.sbuf("yt", [128, 12, 512], BF16)
    h = P.sbuf("h", [128, 8, 512], BF16)
    m = P.sbuf("m", [128, 8, 512], F32)
    mb = P.sbuf("mb", [128, 8, 512], BF16)
    r = P.sbuf("r", [128, 8, 512], F32)
    sq = P.sbuf("sq", [128, 8, 512], F32)
    x1 = P.sbuf("x1", [128, 8, 512], F32)
    a = P.sbuf("a", [128, 22, 512], BF16)
    st1 = P.sbuf("st1", [128, 512], F32)
    st2 = P.sbuf("st2", [128, 512], F32)
    gate = P.sbuf("gate", [128, 512], F32, nbufs=2)
    tmp = P.sbuf("tmp", [128, 512], F32, nbufs=2)
    wgs = P.sbuf("wgs", [128, 1024], BF16, nbufs=3)
    wbs = P.sbuf("wbs", [128, 512], BF16, nbufs=3)
    wos = P.sbuf("wos", [128, 1024], BF16, nbufs=2)
    wf1s = P.sbuf("wf1s", [128, 1024], BF16, nbufs=4)
    wf2s = P.sbuf("wf2s", [128, 2816], BF16, nbufs=2)

    tiles = [(0, 128, 1)]
    t0 = 128
    while t0 < TT:
        tiles.append((t0, 512, 0))
        t0 += 512
    cnt = {"g": 0, "o": 0, "f1": 0, "f2": 0, "gt": 0}

    def xv(dr, t0, W):
        return dr[:, t0:t0 + W].m(lambda ap: ap.rearrange("(c p) t -> p c t", p=128))

    for ti, (t0, W, cm) in enumerate(tiles):
        X = xt[ti % 2]
        E["xtile_load"](X, t0, W)
        for hsel in range(2):
            E["ycand_load"](yt if hsel == 0 else a, hsel, t0, W, cm)
        P.ts(yt[:, :, :W], yt[:, :, :W], sels[:, 0:1], ALU.mult)
        P.stt(yt[:, :, :W], a[:, 0:12, :W], sels[:, 1:2], yt[:, :, :W], ALU.mult, ALU.add)
        for kc in range(8):
            P.ts(h[:, kc, :W], X[:, kc, :W], modT[:, SC1 + kc, cm:cm + 1], ALU.mult,
                 modT[:, SH1 + kc, cm:cm + 1], ALU.add)
        for oc in range(8):
            for i in range(3):
                wgt = wgs[cnt["g"] % 3]
                wbt = wbs[cnt["g"] % 3]
                cnt["g"] += 1
                P.dma(wgt[:], E["wg"](i, oc))
                P.dma(wbt[:], E["wb"](i, oc))
                pG = ps.get()
                for kc in range(8):
                    P.mm(pG[:, :W], wgt[:, kc * 128:(kc + 1) * 128], h[:, kc, :W], start=kc == 0, stop=kc == 7)
                gt = gate[cnt["gt"] % 2]
                tp = tmp[cnt["gt"] % 2]
                cnt["gt"] += 1
                P.act(gt[:, :W], pG[:, :W], AF.Sigmoid, bias=bgs[:, i, oc:oc + 1])
                pB = ps.get()
                for kc in range(4):
                    P.mm(pB[:, :W], wbt[:, kc * 128:(kc + 1) * 128], yt[:, 4 * i + kc, :W], start=kc == 0, stop=kc == 3)
                if i == 0:
                    P.tt(m[:, oc, :W], gt[:, :W], pB[:, :W], ALU.mult)
                elif i == 1:
                    P.tt(tp[:, :W], gt[:, :W], pB[:, :W], ALU.mult)
                    P.tt(m[:, oc, :W], m[:, oc, :W], tp[:, :W], ALU.add)
                else:
                    P.tt(tp[:, :W], gt[:, :W], pB[:, :W], ALU.mult)
                    P.tt(mb[:, oc, :W], m[:, oc, :W], tp[:, :W], ALU.add)
        for kc in range(8):
            P.ts(X[:, kc, :W], X[:, kc, :W], ALPHA, ALU.mult)
        for oc in range(8):
            wt = wos[cnt["o"] % 2]
            cnt["o"] += 1
            P.dma(wt[:], E["wo"](oc))
            pY = ps.get()
            for kc in range(8):
                P.mm(pY[:, :W], wt[:, kc * 128:(kc + 1) * 128], mb[:, kc, :W], start=kc == 0, stop=kc == 7)
            P.stt(r[:, oc, :W], pY[:, :W], modT[:, G1 + oc, cm:cm + 1], X[:, oc, :W], ALU.mult, ALU.add)
        emit_ln(P, ps, r, x1, W, lns[:, 0, :], lns[:, 1, :], onesD, sq, st1, st2, epsT)
        for kc in range(8):
            P.ts(h[:, kc, :W], x1[:, kc, :W], modT[:, SC2 + kc, cm:cm + 1], ALU.mult,
                 modT[:, SH2 + kc, cm:cm + 1], ALU.add)
        for fc in range(22):
            w1 = wf1s[cnt["f1"] % 4]
            w2 = wf1s[(cnt["f1"] + 1) % 4]
            cnt["f1"] += 2
            P.dma(w1[:], E["wf1"](fc))
            P.dma(w2[:], E["wf1"](22 + fc))
            pG = ps.get()
            for kc in range(8):
                P.mm(pG[:, :W], w1[:, kc * 128:(kc + 1) * 128], h[:, kc, :W], start=kc == 0, stop=kc == 7)
            pU = ps.get()
            for kc in range(8):
                P.mm(pU[:, :W], w2[:, kc * 128:(kc + 1) * 128], h[:, kc, :W], start=kc == 0, stop=kc == 7)
            gt = gate[cnt["gt"] % 2]
            cnt["gt"] += 1
            P.act(gt[:, :W], pG[:, :W], AF.Silu)
            P.tt(a[:, fc, :W], gt[:, :W], pU[:, :W], ALU.mult)
        for kc in range(8):
            P.ts(x1[:, kc, :W], x1[:, kc, :W], ALPHA, ALU.mult)
        for oc in range(8):
            wt = wf2s[cnt["f2"] % 2]
            cnt["f2"] += 1
            P.dma(wt[:], E["wf2"](oc))
            pF = ps.get()
            for fc in range(22):
                P.mm(pF[:, :W], wt[:, fc * 128:(fc + 1) * 128], a[:, fc, :W], start=fc == 0, stop=fc == 21)
            P.stt(r[:, oc, :W], pF[:, :W], modT[:, G2 + oc, cm:cm + 1], x1[:, oc, :W], ALU.mult, ALU.add)
        emit_ln(P, ps, r, m, W, lns[:, 2, :], lns[:, 3, :], onesD, sq, st1, st2, epsT)
        E["xtile_store"](m, t0, W)
        if last and not cm:
            P.dma(xv(E["xout"], t0 - 128, W), m[:, :, :W], eng="pool", owner=m)
    P.barrier()
    P.release(mk_all)


RMS_EPS = 1e-6
MLA_SCALE = 96 ** -0.5
DIFF_SCALE = 64 ** -0.5
HG_SCALE = 128 ** -0.5
CTX = 256


def emit_M(P, ps, S, E, modT):
    nc = P.nc
    dbg = False
    Tt = CTX + S
    NT = Tt // 128
    mk_all = P.mark()
    cvec, wmod, bmod = E["cvec"], E["wmod"], E["bmod"]
    wA, wq, wkv, ng, wD, wH = E["wA"], E["wq"], E["wkv"], E["ng"], E["wD"], E["wH"]
    lbfm, lbrep, dlrep, lcon, hgn, sub = E["lbfm"], E["lbrep"], E["dlrep"], E["lcon"], E["hgn"], E["sub"]
    cmT, smT, cdT, sdT, tri, negm = E["cmT"], E["smT"], E["cdT"], E["sdT"], E["tri"], E["negm"]
    hT, cqnT, ckvnT, kpeT, qTs, qdT, kdT = E["hT"], E["cqnT"], E["ckvnT"], E["kpeT"], E["qTs"], E["qdT"], E["kdT"]
    hqT, hkT, hogT, hg_, hkt, hv = E["hqT"], E["hkT"], E["hogT"], E["hg_"], E["hkt"], E["hv"]
    phases = "ABCD"
    bank = ps.b
    emit_mod(P, ps, cvec, wmod, bmod, list(range(12)), modT=modT)
    P.ts(modT[:, 8:16, :], modT[:, 8:16, :], 1.0, ALU.add)
    P.ts(modT[:, 32:40, :], modT[:, 32:40, :], 1.0, ALU.add)
    SH1, SC1 = 0, 8

    ones32 = P.sbuf("ones32", [128, 128], F32)
    P.memset(ones32[:], 1.0)
    onesb = P.sbuf("onesb", [128, 128], BF16)
    P.memset(onesb[:], 1.0)
    o256 = P.sbuf("o256", [128, 128], F32)
    P.memset(o256[:], 1.0 / 256)
    o128 = P.sbuf("o128", [128, 128], F32)
    P.memset(o128[:], 1.0 / 128)
    epsR = P.sbuf("epsR", [128, 1], F32)
    P.memset(epsR[:], RMS_EPS)
    ngs = P.sbuf("ngs", [128, 3], F32)
    P.dma(ngs[:], ng[:])
    lc = P.sbuf("lc", [128, 8], F32)
    P.dma(lc[:], lcon[:])
    hgns = P.sbuf("hgns", [128, 1], F32)
    P.dma(hgns[:], hgn[:])
    subs = P.sbuf("subs", [128, 1], F32)
    P.dma(subs[:], sub[:])
    P.ts(subs[:], subs[:], lc[:, 5:6], ALU.mult)
    tris = P.sbuf("tris", [128, 4, 128], F32)
    P.dma(tris[:], tri[:])
    lbf = P.sbuf("lbf", [128, 2, 4, 2], F32)
    P.dma(lbf[:], lbfm[:])
    lbr = P.sbuf("lbr", [128, 2, 4, 256], F32)
    P.dma(lbr[:], lbrep[:])
    lb_fm = P.sbuf("lb_fm", [128, 2, 2], F32)
    oml_fm = P.sbuf("oml_fm", [128, 2, 2], F32)
    lb_row = P.sbuf("lb_row", [128, 2, 256], F32)
    oml_row = P.sbuf("oml_row", [128, 2, 256], F32)
    nlam = P.sbuf("nlam", [128, 1], F32)
    mk0 = P.mark()
    for (src, dst, dst1, n) in ((lbf, lb_fm, oml_fm, 2), (lbr, lb_row, oml_row, 256)):
        P.act(src[:], src[:], AF.Exp)
        tot = P.sbuf("lb_tot", [128, 2, n], F32)
        part = P.sbuf("lb_part", [128, 2, n], F32)
        tmp = P.sbuf("lb_tmp", [128, 2, n], F32)
        P.tt(tot[:], src[:, :, 0, :], src[:, :, 1, :], ALU.add)
        P.tt(tot[:], tot[:], src[:, :, 2, :], ALU.add)
        P.tt(tot[:], tot[:], src[:, :, 3, :], ALU.add)
        P.ts(part[:], src[:, :, 0, :], lc[:, 0:1], ALU.mult)
        for j in range(1, 4):
            P.ts(tmp[:], src[:, :, j, :], lc[:, j:j + 1], ALU.mult)
            P.tt(part[:], part[:], tmp[:], ALU.add)
        P.recip(tot[:], tot[:])
        P.tt(dst[:], part[:], tot[:], ALU.mult)
        P.ts(dst1[:], dst[:], -1.0, ALU.mult, 1.0, ALU.add)
    dls = P.sbuf("dls", [128, 4, 64], F32)
    P.dma(dls[:], dlrep[:])
    pr = P.sbuf("dl_pr", [128, 2, 64], F32)
    P.tt(pr[:, 0, :], dls[:, 0, :], dls[:, 1, :], ALU.mult)
    P.tt(pr[:, 1, :], dls[:, 2, :], dls[:, 3, :], ALU.mult)
    sm2 = P.sbuf("dl_sm", [128, 2], F32)
    w_ = 64
    while w_ > 1:
        h_ = w_ // 2
        P.tt(pr[:, :, 0:h_], pr[:, :, 0:h_], pr[:, :, h_:w_], ALU.add)
        w_ = h_
    P.copy(sm2[:], pr[:, :, 0])
    P.act(sm2[:], sm2[:], AF.Exp)
    P.tt(nlam[:], sm2[:, 1:2], sm2[:, 0:1], ALU.subtract)
    P.tt(nlam[:], nlam[:], lc[:, 4:5], ALU.subtract)
    P.barrier()
    P.release(mk0)

    tiles = [(0, 256, 1)] + [(CTX + 512 * i, 512, 0) for i in range(S // 512)]

    def fm(dr, t0, W, nch):
        if nch == 1:
            return dr[:, t0:t0 + W]
        return dr[:, t0:t0 + W].m(lambda ap: ap.rearrange("(c p) t -> p c t", p=128))

    def proj_fm(pt, prow, w_of_kc, hsrc, W, KC=8):
        for kc in range(KC):
            P.mm(pt[0:prow, :W], w_of_kc(kc), hsrc(kc), start=kc == 0, stop=kc == KC - 1)

    mk = P.mark()
    xt = P.sbuf("xt", [128, 8, 512], F32, nbufs=2)
    hb = P.sbuf("hb", [128, 8, 512], BF16, nbufs=2)
    for ti, (t0, W, cm) in enumerate(tiles):
        X = xt[ti % 2]
        H = hb[ti % 2]
        E["xload"](X, ti, t0, W)
        for kc in range(8):
            P.ts(H[:, kc, :W], X[:, kc, :W], modT[:, SC1 + kc, cm:cm + 1], ALU.mult,
                 modT[:, SH1 + kc, cm:cm + 1], ALU.add)
        P.dma(fm(hT, t0, W, 8), H[:, :, :W], eng="pool")
    P.barrier()
    P.release(mk)

    def rope(o1, o2, a, b, cs, sn, W, n, t1, t2):
        P.tt(t1[0:n, :W], a, cs, ALU.mult)
        P.tt(t2[0:n, :W], b, sn, ALU.mult)
        P.tt(o1, t1[0:n, :W], t2[0:n, :W], ALU.subtract)
        P.tt(t1[0:n, :W], a, sn, ALU.mult)
        P.tt(t2[0:n, :W], b, cs, ALU.mult)
        P.tt(o2, t1[0:n, :W], t2[0:n, :W], ALU.add)

    if "A" in phases:
        mk = P.mark()
        wAs = P.sbuf("wAs", [128, 8, 416], BF16)
        P.dma(wAs[:], wA[:])
        hb = P.sbuf("hbA", [128, 8, 512], BF16, nbufs=2)
        cq = P.sbuf("cq", [128, 3, 512], F32)
        sq = P.sbuf("sqA", [128, 3, 512], F32)
        rs = P.sbuf("rsA", [128, 2, 512], F32)
        cqn = P.sbuf("cqn", [128, 3, 512], BF16, nbufs=2)
        cs = P.sbuf("csA", [16, 2, 512], F32, nbufs=2)
        t1 = P.sbuf("t1A", [16, 512], F32)
        t2 = P.sbuf("t2A", [16, 512], F32)
        kp = P.sbuf("kpA", [16, 2, 512], BF16, nbufs=2)
        for ti, (t0, W, cm) in enumerate(tiles):
            H = hb[ti % 2]
            CN = cqn[ti % 2]
            CS = cs[ti % 2]
            KP = kp[ti % 2]
            P.dma(H[:, :, :W], fm(hT, t0, W, 8))
            P.dma(CS[:, 0, :W], cmT[:, t0:t0 + W])
            P.dma(CS[:, 1, :W], smT[:, t0:t0 + W])
            for oc in range(3):
                pt = ps.get()
                proj_fm(pt, 128, lambda kc: wAs[:, kc, oc * 128:(oc + 1) * 128], lambda kc: H[:, kc, :W], W)
                P.copy(cq[:, oc, :W], pt[:, :W], eng="act")
                P.act(sq[:, oc, :W], pt[:, :W], AF.Square)
            p1 = ps.get()
            P.mm(p1[:, :W], o256[:], sq[:, 0, :W], start=True, stop=False)
            P.mm(p1[:, :W], o256[:], sq[:, 1, :W], start=False, stop=True)
            p2 = ps.get()
            P.mm(p2[:, :W], o128[:], sq[:, 2, :W], start=True, stop=True)
            for j, pp in enumerate((p1, p2)):
                P.act(rs[:, j, :W], pp[:, :W], AF.Sqrt, bias=epsR[:, 0:1], scale=1.0)
                P.recip(rs[:, j, :W], rs[:, j, :W])
            for oc in range(3):
                P.stt(CN[:, oc, :W], cq[:, oc, :W], ngs[:, oc:oc + 1], rs[:, 0 if oc < 2 else 1, :W], ALU.mult, ALU.mult)
            P.dma(fm(cqnT, t0, W, 2), CN[:, 0:2, :W], eng="pool")
            P.dma(ckvnT[:, t0:t0 + W], CN[:, 2, :W], eng="pool")
            pa = ps.get()
            proj_fm(pa, 16, lambda kc: wAs[:, kc, 384:400], lambda kc: H[:, kc, :W], W)
            pb = ps.get()
            proj_fm(pb, 16, lambda kc: wAs[:, kc, 400:416], lambda kc: H[:, kc, :W], W)
            rope(KP[:, 0, :W], KP[:, 1, :W], pa[0:16, :W], pb[0:16, :W], CS[:, 0, :W], CS[:, 1, :W], W, 16, t1, t2)
            P.dma(kpeT[0:16, t0:t0 + W], KP[:, 0, :W], eng="pool")
            P.dma(kpeT[16:32, t0:t0 + W], KP[:, 1, :W], eng="pool")
        P.barrier()
        P.release(mk)

    if "B" in phases:
        mk = P.mark()
        wqs = P.sbuf("wqs", [128, 2, 4, 96], BF16)
        P.dma(wqs[:], wq[:])
        wkvs = P.sbuf("wkvs", [128, 4, 128], BF16)
        P.dma(wkvs[:], wkv[:])
        KT = P.sbuf("KT", [96, Tt], BF16)
        VA = P.sbuf("VA", [128, NT, 65], BF16)
        cqb = P.sbuf("cqb", [128, 3, 512], BF16, nbufs=2)
        cs = P.sbuf("csB", [16, 2, 512], F32, nbufs=2)
        t1 = P.sbuf("t1B", [16, 512], F32)
        t2 = P.sbuf("t2B", [16, 512], F32)
        qn = P.sbuf("qnB", [64, 512], BF16, nbufs=2)
        qp = P.sbuf("qpB", [16, 2, 512], BF16, nbufs=2)
        qt = P.sbuf("qtB", [96, 512], BF16, nbufs=2)
        pT = P.sbuf("pTB", [128, 512], BF16, nbufs=3)
        rsum = P.sbuf("rsumB", [128, 512], F32)
        rc = P.sbuf("rcB", [64, 512], F32)
        yo = P.sbuf("yoB", [64, 512], BF16, nbufs=2)
        psS = [bank[0], bank[1], bank[2]]
        psO = [bank[3], bank[4]]
        psX = [bank[5], bank[6], bank[7]]
        cS = cO = cX = cP = 0
        for hh in range(4):
            P.memset(VA[:, :, 64:65], 1.0)
            for ti, (t0, W, cm) in enumerate(tiles):
                C = cqb[ti % 2]
                CS = cs[ti % 2]
                P.dma(C[:, 0:2, :W], fm(cqnT, t0, W, 2))
                P.dma(C[:, 2, :W], ckvnT[:, t0:t0 + W])
                P.dma(CS[:, 0, :W], cmT[:, t0:t0 + W])
                P.dma(CS[:, 1, :W], smT[:, t0:t0 + W])
                pt = psX[cX % 3]; cX += 1
                proj_fm(pt, 64, lambda kc: wqs[:, kc, hh, 0:64], lambda kc: C[:, kc, :W], W, KC=2)
                QN = qn[ti % 2]
                P.copy(QN[:, :W], pt[0:64, :W], eng="act")
                P.dma(qTs[0:64, t0:t0 + W], QN[:, :W], eng="pool")
                pa = psX[cX % 3]; cX += 1
                proj_fm(pa, 16, lambda kc: wqs[:, kc, hh, 64:80], lambda kc: C[:, kc, :W], W, KC=2)
                pb = psX[cX % 3]; cX += 1
                proj_fm(pb, 16, lambda kc: wqs[:, kc, hh, 80:96], lambda kc: C[:, kc, :W], W, KC=2)
                QP = qp[ti % 2]
                rope(QP[:, 0, :W], QP[:, 1, :W], pa[0:16, :W], pb[0:16, :W], CS[:, 0, :W], CS[:, 1, :W], W, 16, t1, t2)
                P.dma(qTs[64:80, t0:t0 + W], QP[:, 0, :W], eng="pool")
                P.dma(qTs[80:96, t0:t0 + W], QP[:, 1, :W], eng="pool")
                pk = psX[cX % 3]; cX += 1
                P.mm(pk[0:64, :W], wkvs[:, hh, 0:64], C[:, 2, :W])
                P.copy(KT[0:64, t0:t0 + W], pk[0:64, :W], eng="act")
                pv = psX[cX % 3]; cX += 1
                for j in range(W // 128):
                    P.mm(pv[:, j * 64:(j + 1) * 64], C[:, 2, j * 128:(j + 1) * 128], wkvs[:, hh, 64:128])
                kt0 = t0 // 128
                P.copy(VA[:, kt0:kt0 + W // 128, 0:64],
                       pv[:, 0:(W // 128) * 64].m(lambda ap: ap.rearrange("p (j d) -> p j d", d=64)))
            P.dma(KT[64:96, :], kpeT[:, :])
            P.barrier()
            for ti, (t0, W, cm) in enumerate(tiles):
                Q = qt[ti % 2]
                P.dma(Q[:, :W], qTs[:, t0:t0 + W])
                nk = 2 if cm else NT
                po = psO[cO % 2]; cO += 1
                LA = 2
                pend = []
                for kt in range(nk + LA):
                    if kt < nk:
                        pS = psS[cS % 3]; cS += 1
                        P.mm(pS[:, :W], KT[:, kt * 128:(kt + 1) * 128], Q[:, :W])
                        pt_ = pT[cP % 3]; cP += 1
                        P.act(pt_[:, :W], pS[:, :W], AF.Exp, scale=MLA_SCALE)
                        pend.append(pt_)
                    if kt >= LA:
                        k2 = kt - LA
                        P.mm(po[0:65, :W], VA[:, k2, :], pend[k2][:, :W], start=k2 == 0, stop=k2 == nk - 1)
                P.copy(rsum[64:65, :W], po[64:65, :W])
                pbx = psX[cX % 3]; cX += 1
                P.mm(pbx[0:64, :W], ones32[64:65, 0:64], rsum[64:65, :W])
                P.recip(rc[:, :W], pbx[0:64, :W])
                Y = yo[ti % 2]
                P.tt(Y[:, :W], po[0:64, :W], rc[:, :W], ALU.mult)
                E["ywrite"](hh * 64, 64, t0, W, Y[:, :W])
            P.barrier()
        P.release(mk)

    if "C" in phases:
        mk = P.mark()
        wDs = P.sbuf("wDs", [128, 8, 3, 128], BF16)
        K12 = P.sbuf("K12", [64, 2, Tt], BF16)
        VD = P.sbuf("VD", [128, NT, 128], BF16)
        hb = P.sbuf("hbC", [128, 8, 512], BF16, nbufs=2)
        cs = P.sbuf("csC", [32, 2, 512], F32, nbufs=2)
        t1 = P.sbuf("t1C", [32, 512], F32)
        t2 = P.sbuf("t2C", [32, 512], F32)
        rp = P.sbuf("rpC", [32, 8, 512], BF16, nbufs=2)
        qt = P.sbuf("qtC", [64, 2, 512], BF16, nbufs=2)
        pT = P.sbuf("pTC", [128, 512], BF16, nbufs=3)
        o1 = P.sbuf("o1C", [128, 512], F32)
        o2 = P.sbuf("o2C", [128, 512], F32)
        rcp = P.sbuf("rcpC", [128, 512], F32)
        sqd = P.sbuf("sqdC", [128, 512], F32)
        yo = P.sbuf("yoC", [128, 512], BF16, nbufs=2)
        psS = [bank[0], bank[1], bank[2]]
        psO = [bank[3], bank[4]]
        psM = [bank[5], bank[6]]
        psX = [bank[7]]
        cS = cP = 0
        for hh in range(2):
            P.dma(wDs[:], wD[:, :, hh])
            for ti, (t0, W, cm) in enumerate(tiles):
                H = hb[ti % 2]
                CS = cs[ti % 2]
                R = rp[ti % 2]
                P.dma(H[:, :, :W], fm(hT, t0, W, 8))
                P.dma(CS[:, 0, :W], cdT[:, t0:t0 + W])
                P.dma(CS[:, 1, :W], sdT[:, t0:t0 + W])
                for w in range(2):
                    for j in range(2):
                        pa = psS[cS % 3]; cS += 1
                        proj_fm(pa, 32, lambda kc: wDs[:, kc, w, j * 64:j * 64 + 32], lambda kc: H[:, kc, :W], W)
                        pb = psS[cS % 3]; cS += 1
                        proj_fm(pb, 32, lambda kc: wDs[:, kc, w, j * 64 + 32:j * 64 + 64], lambda kc: H[:, kc, :W], W)
                        i0 = (w * 2 + j) * 2
                        rope(R[:, i0, :W], R[:, i0 + 1, :W], pa[0:32, :W], pb[0:32, :W], CS[:, 0, :W], CS[:, 1, :W], W, 32, t1, t2)
                        dst = qdT if w == 0 else kdT
                        P.dma(dst[j, 0:32, t0:t0 + W], R[:, i0, :W], eng="pool")
                        P.dma(dst[j, 32:64, t0:t0 + W], R[:, i0 + 1, :W], eng="pool")
                pv = psX[0]
                for j in range(W // 128):
                    for kc in range(8):
                        P.mm(pv[:, j * 128:(j + 1) * 128], H[:, kc, j * 128:(j + 1) * 128], wDs[:, kc, 2, :], start=kc == 0, stop=kc == 7)
                kt0 = t0 // 128
                P.copy(VD[:, kt0:kt0 + W // 128, :], pv[:, 0:W].m(lambda ap: ap.rearrange("p (j d) -> p j d", d=128)))
            P.barrier()
            P.dma(K12[:, 0, :], kdT[0])
            P.dma(K12[:, 1, :], kdT[1])
            for ti, (t0, W, cm) in enumerate(tiles):
                Q = qt[ti % 2]
                P.dma(Q[:, 0, :W], qdT[0, :, t0:t0 + W])
                P.dma(Q[:, 1, :W], qdT[1, :, t0:t0 + W])
                nk = 2 if cm else NT
                for j in range(2):
                    po = psO[j]
                    pm = psM[j]
                    LA = 2
                    pend = []
                    for kt in range(nk + LA):
                        if kt < nk:
                            pS = psS[cS % 3]; cS += 1
                            P.mm(pS[:, :W], K12[:, j, kt * 128:(kt + 1) * 128], Q[:, j, :W])
                            pt_ = pT[cP % 3]; cP += 1
                            P.act(pt_[:, :W], pS[:, :W], AF.Exp, scale=DIFF_SCALE)
                            pend.append(pt_)
                        if kt >= LA:
                            k2 = kt - LA
                            P.mm(po[:, :W], VD[:, k2, :], pend[k2][:, :W], start=k2 == 0, stop=k2 == nk - 1)
                            P.mm(pm[:, :W], onesb[:], pend[k2][:, :W], start=k2 == 0, stop=k2 == nk - 1)
                    P.recip(rcp[:, :W], pm[:, :W])
                    P.tt((o1 if j == 0 else o2)[:, :W], po[:, :W], rcp[:, :W], ALU.mult)
                P.stt(o1[:, :W], o2[:, :W], nlam[:, 0:1], o1[:, :W], ALU.mult, ALU.add)
                P.act(sqd[:, :W], o1[:, :W], AF.Square)
                px = psX[0]
                P.mm(px[:, :W], o128[:], sqd[:, :W])
                P.act(rcp[:, :W], px[:, :W], AF.Sqrt, bias=epsR[:, 0:1], scale=1.0)
                P.recip(rcp[:, :W], rcp[:, :W])
                if dbg and hh == 1 and ti == 1:
                    dvd = _pd("dbg_vd", [128, NT, 128], BF16, kind="ExternalOutput")
                    P.dma(dvd[:], VD[:], eng="pool")
                    do1 = _pd("dbg_o1", [128, 512], F32, kind="ExternalOutput")
                    P.dma(do1[:], o1[:], eng="pool")
                    do2 = _pd("dbg_o2", [128, 512], F32, kind="ExternalOutput")
                    P.dma(do2[:], o2[:], eng="pool")
                    dnl = _pd("dbg_nl", [128, 1], F32, kind="ExternalOutput")
                    P.dma(dnl[:], nlam[:], eng="pool")
                Y = yo[ti % 2]
                P.stt(Y[:, :W], o1[:, :W], subs[:, 0:1], rcp[:, :W], ALU.mult, ALU.mult)
                E["ywrite"](512 + hh * 128, 128, t0, W, Y[:, :W])
            P.barrier()
        P.release(mk)

    if "D" in phases:
        mk = P.mark()
        wHs = P.sbuf("wHs", [128, 8, 5, 128], BF16)
        hb = P.sbuf("hbD", [128, 8, 512], BF16, nbufs=2)
        sg = P.sbuf("sgD", [128, 512], F32, nbufs=2)
        ofm = P.sbuf("ofmD", [128, 4, 512], BF16, nbufs=2)
        ztm = P.sbuf("ztmD", [128, 4, 3, 128], F32)
        gtm = P.sbuf("gtmD", [128, 4, 2, 128], F32, nbufs=2)
        ktm = P.sbuf("ktmD", [128, 4, 3, 128], BF16, nbufs=2)
        Oacc = P.sbuf("Oacc", [128, Tt], F32)
        qb = P.sbuf("qbD", [128, 512], BF16, nbufs=2)
        kb_ = P.sbuf("kbD", [128, 512], BF16, nbufs=2)
        gb = P.sbuf("gbD", [128, 4, 128], F32, nbufs=2)
        kt_ = P.sbuf("ktD", [128, 4, 128], BF16, nbufs=2)
        vb = P.sbuf("vbD", [128, 4, 128], BF16, nbufs=2)
        Eh = P.sbuf("EhD", [128, 512], F32)
        ER = P.sbuf("ERD", [128, 4, 128], F32)
        Qh = P.sbuf("QhD", [128, 512], BF16)
        Kh = P.sbuf("KhD", [128, 4, 128], BF16)
        Qt = P.sbuf("QtD", [128, 512], BF16)
        Dq = P.sbuf("DqD", [128, 512], F32)
        Dk = P.sbuf("DkD", [128, 8, 4, 64], F32)
        Kall = P.sbuf("KallD", [128, 8, 4, 128], BF16)
        P.memset(Kall[:], 0.0)
        refs = P.sbuf("refsD", [128, 8, 4], F32)
        negs = P.sbuf("negsD", [128, 2, 4, 64], F32)
        P.dma(negs[:], negm[:])
        PTt = P.sbuf("PTD", [128, 4, 128], BF16)
        Sf = P.sbuf("SfD", [128, 128], F32)
        Sb = P.sbuf("SbD", [128, 128], BF16)
        sqh = P.sbuf("sqhD", [128, 512], F32)
        rch = P.sbuf("rchD", [128, 512], F32)
        ogb = P.sbuf("ogbD", [128, 512], BF16, nbufs=2)
        yo = P.sbuf("yoD", [128, 512], BF16, nbufs=2)
        pX = [bank[0], bank[1], bank[2], bank[3]]
        cX = 0
        for hh in range(2):
            P.dma(wHs[:], wH[:, :, hh])
            for ti, (t0, W, cm) in enumerate(tiles):
                H = hb[ti % 2]
                OF = ofm[ti % 2]
                GT = gtm[ti % 2]
                KTm = ktm[ti % 2]
                P.dma(H[:, :, :W], fm(hT, t0, W, 8))
                pt = pX[cX % 4]; cX += 1
                proj_fm(pt, 128, lambda kc: wHs[:, kc, 0, :], lambda kc: H[:, kc, :W], W)
                S_ = sg[0]
                P.act(S_[:, :W], pt[:, :W], AF.Silu)
                P.ts(OF[:, 0, :W], S_[:, :W], HG_SCALE, ALU.mult)
                for d in range(2):
                    pt = pX[cX % 4]; cX += 1
                    proj_fm(pt, 128, lambda kc: wHs[:, kc, 1 + d, :], lambda kc: H[:, kc, :W], W)
                    S_ = sg[1]
                    P.act(S_[:, :W], pt[:, :W], AF.Sigmoid, scale=-1.0)
                    P.ts(OF[:, 1 + d, :W], S_[:, :W], oml_fm[:, d, hh:hh + 1], ALU.mult)
                pt = pX[cX % 4]; cX += 1
                proj_fm(pt, 128, lambda kc: wHs[:, kc, 4, :], lambda kc: H[:, kc, :W], W)
                P.act(OF[:, 3, :W], pt[:, :W], AF.Silu)
                P.dma(hqT[:, t0:t0 + W], OF[:, 0, :W], eng="pool")
                P.dma(hkT[0, :, t0:t0 + W], OF[:, 1, :W], eng="pool")
                P.dma(hkT[1, :, t0:t0 + W], OF[:, 2, :W], eng="pool")
                P.dma(hogT[:, t0:t0 + W], OF[:, 3, :W], eng="pool")
                nj = W // 128
                for j in range(nj):
                    pt = pX[cX % 4]; cX += 1
                    for c3, wi in enumerate((1, 2, 3)):
                        for kc in range(8):
                            P.mm(pt[:, c3 * 128:(c3 + 1) * 128], H[:, kc, j * 128:(j + 1) * 128], wHs[:, kc, wi, :],
                                 start=kc == 0, stop=kc == 7)
                    P.act(ztm[:, j, 0:2, :], pt[:, 0:256].m(lambda ap: ap.rearrange("p (a d) -> p a d", d=128)), AF.Sigmoid)
                    P.copy(KTm[:, j, 2, :], pt[:, 256:384], eng="act")
                for d in range(2):
                    lbv = lb_row[:, d, hh * 128:(hh + 1) * 128]
                    omv = oml_row[:, d, hh * 128:(hh + 1) * 128]
                    for j in range(nj):
                        P.tt(ztm[:, j, d, :], ztm[:, j, d, :], omv, ALU.mult)
                        P.tt(ztm[:, j, d, :], ztm[:, j, d, :], lbv, ALU.add)
                    P.act(GT[:, :nj, d, :], ztm[:, :nj, d, :], AF.Ln)
                    P.ts(KTm[:, :nj, d, :], ztm[:, :nj, d, :], -1.0, ALU.mult, 1.0, ALU.add)
                for d in range(2):
                    P.dma(hg_[d, t0:t0 + W, :].m(lambda ap: ap.rearrange("(j p) k -> p j k", p=128)), GT[:, :nj, d, :], eng="pool")
                    P.dma(hkt[d, t0:t0 + W, :].m(lambda ap: ap.rearrange("(j p) k -> p j k", p=128)), KTm[:, :nj, d, :], eng="pool")
                P.dma(hv[t0:t0 + W, :].m(lambda ap: ap.rearrange("(j p) k -> p j k", p=128)), KTm[:, :nj, 2, :], eng="pool")
            P.barrier()
            for d in range(2):
                TI = tris[:, d, :]
                TS = tris[:, 2 + d, :]
                order = list(range(len(tiles))) if d == 0 else [0] + list(range(len(tiles) - 1, 0, -1))
                P.memset(Sf[:], 0.0)
                P.memset(Sb[:], 0.0)
                for bi, ti in enumerate(order):
                    t0, W, cm = tiles[ti]
                    nj = W // 128
                    QB, KB, GB, KTB, VB = qb[bi % 2], kb_[bi % 2], gb[bi % 2], kt_[bi % 2], vb[bi % 2]
                    P.dma(QB[:, :W], hqT[:, t0:t0 + W])
                    P.dma(KB[:, :W], hkT[d, :, t0:t0 + W])
                    P.dma(GB[:, :nj, :], hg_[d, t0:t0 + W, :].m(lambda ap: ap.rearrange("(j p) k -> p j k", p=128)))
                    P.dma(KTB[:, :nj, :], hkt[d, t0:t0 + W, :].m(lambda ap: ap.rearrange("(j p) k -> p j k", p=128)))
                    P.dma(VB[:, :nj, :], hv[t0:t0 + W, :].m(lambda ap: ap.rearrange("(j p) k -> p j k", p=128)))
                    pA = bank[4]
                    pR = bank[5]
                    for j in range(nj):
                        P.mm(pA[:, j * 128:(j + 1) * 128], GB[:, j, :], TI)
                        P.mm(pR[:, j * 128:(j + 1) * 128], TS, GB[:, j, :])
                    P.act(Eh[:, :W], pA[:, :W], AF.Exp)
                    P.act(ER[:, :nj, :], pR[:, :W].m(lambda ap: ap.rearrange("p (j k) -> p j k", k=128)), AF.Exp)
                    P.tt(Qh[:, :W], QB[:, :W], Eh[:, :W], ALU.mult)
                    P.tt(Kh[:, :nj, :], KTB[:, :nj, :], ER[:, :nj, :], ALU.mult)
                    ncb = W // 64
                    nsb = W // 16
                    pA3 = pA[:, :W].m(lambda ap: ap.rearrange("p (c s) -> p c s", s=64))
                    P.memset(refs[:], 0.0)
                    if d == 0:
                        P.copy(refs[:, :ncb, 1:4], pA3[:, :, 15:48:16])
                    else:
                        P.copy(refs[:, :ncb, 0:3], pA3[:, :, 16:64:16])
                    rflat = refs[:, :ncb, :].m(lambda ap: ap.rearrange("p c i -> p (c i)"))
                    P.tt(Dq[:, :W].m(lambda ap: ap.rearrange("p (b s) -> p b s", s=16)),
                         pA[:, :W].m(lambda ap: ap.rearrange("p (b s) -> p b s", s=16)),
                         rflat.m(lambda ap: ap.unsqueeze(2).broadcast_to([128, nsb, 16])), ALU.subtract)
                    P.act(Dq[:, :W], Dq[:, :W], AF.Exp)
                    P.tt(Qt[:, :W], QB[:, :W], Dq[:, :W], ALU.mult)
                    P.tt(Dk[:, :ncb, :, :],
                         refs[:, :ncb, :].m(lambda ap: ap.unsqueeze(3).broadcast_to([128, ncb, 4, 64])),
                         pA3.m(lambda ap: ap.unsqueeze(2).broadcast_to([128, ncb, 4, 64])), ALU.subtract)
                    P.tt(Dk[:, :ncb, :, :], Dk[:, :ncb, :, :],
                         negs[:, d, :, :].m(lambda ap: ap.unsqueeze(1).broadcast_to([128, ncb, 4, 64])), ALU.add)
                    P.act(Dk[:, :ncb, :, :], Dk[:, :ncb, :, :], AF.Exp)
                    for hf in range(2):
                        P.tt(Kall[:, :ncb, :, :].m(lambda ap: ap.rearrange("p (j h) i s -> p j h i s", h=2)[:, :, hf, :, hf * 64:(hf + 1) * 64]),
                             Dk[:, :ncb, :, :].m(lambda ap: ap.rearrange("p (j h) i s -> p j h i s", h=2)[:, :, hf, :, :]),
                             KB[:, :W].m(lambda ap: ap.rearrange("p (j h s) -> p j h s", h=2, s=64)[:, :, hf, :].unsqueeze(2).broadcast_to([128, nj, 4, 64])),
                             ALU.mult)
                    pS = bank[6]
                    for j in range(nj):
                        for Ig in range(8):
                            col = j * 128 + Ig * 16
                            P.mm(pS[:, col:col + 16], Kall[:, 2 * j + Ig // 4, Ig % 4, :], Qt[:, col:col + 16])
                        P.tt(PTt[:, j, :], pS[:, j * 128:(j + 1) * 128], TI, ALU.mult)
                    pO = bank[7]
                    chunks = list(range(W // 64))
                    if d == 1:
                        chunks = chunks[::-1]
                    for c in chunks:
                        c0 = c * 64
                        j, hf = c // 2, c % 2
                        pb0 = 64 * hf
                        P.mm(pO[:, c0:c0 + 64], VB[pb0:pb0 + 64, j, :], PTt[pb0:pb0 + 64, j, pb0:pb0 + 64], start=True, stop=False)
                        P.mm(pO[:, c0:c0 + 64], Sb[:], Qh[:, c0:c0 + 64], start=False, stop=True)
                        pU = pX[cX % 4]; cX += 1
                        P.mm(pU[:, 0:128], Kh[pb0:pb0 + 64, j, :], VB[pb0:pb0 + 64, j, :])
                        ecol = c0 + 63 if d == 0 else c0
                        P.stt(Sf[:], Sf[:], Eh[:, ecol:ecol + 1], pU[:, 0:128], ALU.mult, ALU.add)
                        P.copy(Sb[:], Sf[:], eng="act")
                    if d == 0:
                        P.copy(Oacc[:, t0:t0 + W], pO[:, :W])
                    else:
                        P.tt(Oacc[:, t0:t0 + W], Oacc[:, t0:t0 + W], pO[:, :W], ALU.add)
            for ti, (t0, W, cm) in enumerate(tiles):
                OG = ogb[ti % 2]
                P.dma(OG[:, :W], hogT[:, t0:t0 + W])
                P.act(sqh[:, :W], Oacc[:, t0:t0 + W], AF.Square)
                px = pX[cX % 4]; cX += 1
                P.mm(px[:, :W], o128[:], sqh[:, :W])
                P.act(rch[:, :W], px[:, :W], AF.Sqrt, bias=epsR[:, 0:1], scale=1.0)
                P.recip(rch[:, :W], rch[:, :W])
                P.stt(rch[:, :W], Oacc[:, t0:t0 + W], hgns[:, 0:1], rch[:, :W], ALU.mult, ALU.mult)
                Y = yo[ti % 2]
                P.tt(Y[:, :W], rch[:, :W], OG[:, :W], ALU.mult)
                E["ywrite"](256 + hh * 128, 128, t0, W, Y[:, :W])
            P.barrier()
        P.release(mk)
    P.barrier()
    P.release(mk_all)


PAIRS = [[0, 1], [2, 3], [4, 5], [6, 7]]
W_SIZES = [("wA", 8 * 416), ("wq", 2 * 4 * 96), ("wkv", 4 * 128), ("wD", 8 * 2 * 3 * 128), ("wH", 8 * 2 * 5 * 128),
           ("wg", 3 * 8 * 1024), ("wb", 3 * 8 * 512), ("wo", 8 * 1024), ("wf1", 44 * 1024), ("wf2", 8 * 2816)]
W_LAYER = sum(n for _, n in W_SIZES)
NW = 4 * W_LAYER


def w_off(l, key):
    o = l * W_LAYER
    for k, n in W_SIZES:
        if k == key:
            return o, n
        o += n
    raise KeyError(key)


def build_fused(S):
    nc = bass.Bass("TRN2", target_bir_lowering=False)
    P = Prog(nc)
    Tt = CTX + S
    Sh = S // 2
    TT = 128 + Sh
    inp = lambda n, sh, dt=F32: P.dram(n, sh, dt, kind="ExternalInput")
    x_own = inp("x_own", [1024, TT])
    wall = inp("wall", [128, NW])
    cvec = inp("cvec", [128, 8, 2])
    wmod = inp("wmod", [4, 48, 128, 1024])
    bmod = inp("bmod", [4, 128, 48])
    ng = inp("ng", [4, 128, 3])
    lbfm = inp("lbfm", [128, 2, 4, 2])
    lbrep = inp("lbrep", [128, 2, 4, 256])
    dlrep = inp("dlrep", [4, 128, 4, 64])
    lcon = inp("lcon", [4, 128, 8])
    hgn = inp("hgn", [4, 128, 1])
    sub = inp("sub", [4, 128, 1])
    cmT = inp("cmT", [16, Tt])
    smT = inp("smT", [16, Tt])
    cdT = inp("cdT", [32, Tt])
    sdT = inp("sdT", [32, Tt])
    tri = inp("tri", [128, 4, 128])
    negm = inp("negm", [128, 2, 4, 64])
    bg = inp("bg", [4, 128, 3, 8])
    lnp = inp("lnp", [4, 128, 4, 8])
    sel = inp("sel", [128, 2])
    xout = P.dram("xout", [1024, Sh], F32, kind="ExternalOutput")

    w16 = P.dram("w16", [128, NW], BF16)
    XW = 4 * 512
    xcb = [0] + [min(TT, 128 + XW * (j + 1)) for j in range((Sh + XW - 1) // XW)]
    nxc = len(xcb) - 1
    xo_c = [[P.dram(f"xo_{c}_{j}", [128, xcb[j + 1] - xcb[j]], F32) for j in range(nxc)] for c in range(8)]
    xg_c = [[P.dram(f"xg_{c}_{j}", [256, xcb[j + 1] - xcb[j]], F32) for j in range(nxc)] for c in range(8)]
    ycb = [0, CTX + Sh, Tt]
    yp_c = [[P.dram(f"yp_{c}_{j}", [128, ycb[j + 1] - ycb[j]], BF16) for j in range(2)] for c in range(6)]
    ya_c = [[P.dram(f"ya_{c}_{j}", [256, ycb[j + 1] - ycb[j]], BF16) for j in range(2)] for c in range(6)]

    def xchunk(t0):
        for j in range(nxc):
            if xcb[j] <= t0 < xcb[j + 1]:
                return j, t0 - xcb[j]
        raise ValueError(t0)

    def xtile_load(X, t0, W):
        j, o = xchunk(t0)
        for c in range(8):
            P.dma(X[:, c, :W], xo_c[c][j][:, o:o + W])

    def xtile_store(M_, t0, W):
        j, o = xchunk(t0)
        for c in range(8):
            P.dma(xo_c[c][j][:, o:o + W], M_[:, c, :W], eng="pool", owner=M_)

    def ywrite(row0, nrows, t0, W, src):
        c, po = row0 // 128, row0 % 128
        j = 0 if t0 < ycb[1] else 1
        o = t0 - ycb[j]
        P.dma(yp_c[c][j][po:po + nrows, o:o + W], src, eng="pool")

    def ycand_load(dst, hsel, t0, W, cm):
        gc = hsel * 128 if cm else CTX + hsel * Sh + (t0 - 128)
        j = 0 if gc < ycb[1] else 1
        o = gc - ycb[j]
        for i in range(3):
            for rr in range(2):
                for cc in range(2):
                    P.dma(dst[:, 4 * i + 2 * rr + cc, :W], ya_c[2 * i + cc][j][rr * 128:(rr + 1) * 128, o:o + W])

    def exchange_x():
        for c in range(8):
            for j in range(nxc):
                P.allgather(xg_c[c][j], xo_c[c][j], PAIRS)
        P.barrier()

    def exchange_y():
        for c in range(6):
            for j in range(2):
                P.allgather(ya_c[c][j], yp_c[c][j], PAIRS)
        P.barrier()

    scr = {
        "hT": P.dram("s_hT", [1024, Tt], BF16), "cqnT": P.dram("s_cqn", [256, Tt], BF16),
        "ckvnT": P.dram("s_ckvn", [128, Tt], BF16), "kpeT": P.dram("s_kpe", [32, Tt], BF16),
        "qTs": P.dram("s_q", [96, Tt], BF16), "qdT": P.dram("s_qd", [2, 64, Tt], BF16),
        "kdT": P.dram("s_kd", [2, 64, Tt], BF16), "hqT": P.dram("s_hq", [128, Tt], BF16),
        "hkT": P.dram("s_hk", [2, 128, Tt], BF16), "hogT": P.dram("s_hog", [128, Tt], BF16),
        "hg_": P.dram("s_hg", [2, Tt, 128], F32), "hkt": P.dram("s_hkt", [2, Tt, 128], BF16),
        "hv": P.dram("s_hv", [Tt, 128], BF16),
    }
    ps = PS(nc)
    modT = P.sbuf("modT", [128, 48, 2], F32)
    sels = P.sbuf("sels", [128, 2], F32)
    P.dma(sels[:], sel[:])

    mk = P.mark()
    CH = 2048
    ca = P.sbuf("ca", [128, CH], F32, nbufs=3)
    cb = P.sbuf("cb", [128, CH], BF16, nbufs=3)
    for i in range(NW // CH):
        A = ca[i % 3]
        B = cb[i % 3]
        P.dma(A[:], wall[:, i * CH:(i + 1) * CH])
        P.copy(B[:], A[:], eng="dve" if i % 2 == 0 else "act")
        P.dma(w16[:, i * CH:(i + 1) * CH], B[:], eng="pool")
    xb = P.sbuf("xb", [128, 8, 512], F32, nbufs=2)
    for i, (t0, W) in enumerate([(0, 128)] + [(128 + 512 * q, 512) for q in range(Sh // 512)]):
        X = xb[i % 2]
        P.dma(X[:, :, :W], x_own[:, t0:t0 + W].m(lambda ap: ap.rearrange("(c p) t -> p c t", p=128)))
        xtile_store(X, t0, W)
    P.barrier()
    P.release(mk)
    exchange_x()

    def xload(X, ti, t0, W):
        if ti == 0:
            for r in range(2):
                for c in range(8):
                    P.dma(X[:, c, r * 128:(r + 1) * 128], xg_c[c][0][r * 128:(r + 1) * 128, 0:128])
        else:
            tok = t0 - CTX
            r = tok // Sh
            j, o = xchunk(128 + tok - r * Sh)
            for c in range(8):
                P.dma(X[:, c, :W], xg_c[c][j][r * 128:(r + 1) * 128, o:o + W])

    def wv(l, key, pat=None, **kw):
        o, n = w_off(l, key)
        v = w16[:, o:o + n]
        if pat is not None:
            v = v.m(lambda ap: ap.rearrange(pat, **kw))
        return v

    for l in range(4):
        wg_v = wv(l, "wg", "p (i o k) -> p i o k", i=3, o=8)
        wb_v = wv(l, "wb", "p (i o k) -> p i o k", i=3, o=8)
        wo_v = wv(l, "wo", "p (o k) -> p o k", o=8)
        wf1_v = wv(l, "wf1", "p (o k) -> p o k", o=44)
        wf2_v = wv(l, "wf2", "p (o k) -> p o k", o=8)
        E = dict(scr)
        E.update({
            "cvec": cvec, "wmod": wmod[l], "bmod": bmod[l], "ng": ng[l],
            "wA": wv(l, "wA", "p (k n) -> p k n", k=8), "wq": wv(l, "wq", "p (k h n) -> p k h n", k=2, h=4),
            "wkv": wv(l, "wkv", "p (h n) -> p h n", h=4),
            "wD": wv(l, "wD", "p (k h w n) -> p k h w n", k=8, h=2, w=3),
            "wH": wv(l, "wH", "p (k h w n) -> p k h w n", k=8, h=2, w=5),
            "lbfm": lbfm, "lbrep": lbrep, "dlrep": dlrep[l], "lcon": lcon[l], "hgn": hgn[l], "sub": sub[l],
            "cmT": cmT, "smT": smT, "cdT": cdT, "sdT": sdT, "tri": tri, "negm": negm,
            "xload": xload, "ywrite": ywrite, "xtile_load": xtile_load, "xtile_store": xtile_store,
            "ycand_load": ycand_load, "bg": bg[l], "lnp": lnp[l], "xout": xout,
            "wg": (lambda i, oc, v=wg_v: v[:, i, oc, :]), "wb": (lambda i, oc, v=wb_v: v[:, i, oc, :]),
            "wo": (lambda oc, v=wo_v: v[:, oc, :]), "wf1": (lambda fc, v=wf1_v: v[:, fc, :]),
            "wf2": (lambda oc, v=wf2_v: v[:, oc, :]),
        })
        emit_M(P, ps, S, E, modT)
        exchange_y()
        emit_T(P, ps, Sh, E, modT, sels, last=(l == 3))
        if l < 3:
            exchange_x()
    P.emit()
    return nc, P

bf16 = ml_dtypes.bfloat16
def lay_oc(W, mo=128):
    K, N = W.shape
    KC, NC = K // 128, N // mo
    return np.ascontiguousarray(W.reshape(KC, 128, NC, mo).transpose(2, 1, 0, 3).reshape(NC, 128, KC * mo))
def lay_k(W):
    K, N = W.shape
    return np.ascontiguousarray(W.reshape(K // 128, 128, N).transpose(1, 0, 2))
def vec_chunks(v):
    n = v.shape[-1] // 128
    return np.ascontiguousarray(v.reshape(n, 128).T)

def rope_tables(S, rot_dim, ctx=256):
    rows = S // 64
    row, col = np.meshgrid(np.arange(rows, dtype=np.float32), np.arange(64, dtype=np.float32), indexing='ij')
    n_freq = rot_dim // 4
    inv_freq = (np.float32(10000.0) ** (-np.arange(n_freq, dtype=np.float32) / np.float32(n_freq))).astype(np.float32)
    ang = np.concatenate([row.reshape(-1, 1) * inv_freq, col.reshape(-1, 1) * inv_freq], axis=-1).astype(np.float32)
    cos = np.concatenate([np.ones((ctx, rot_dim // 2), np.float32), np.cos(ang)], 0)
    sin = np.concatenate([np.zeros((ctx, rot_dim // 2), np.float32), np.sin(ang)], 0)
    return np.ascontiguousarray(cos.T), np.ascontiguousarray(sin.T)

def tri_consts():
    t = np.zeros((128, 4, 128), np.float32)
    for blk in range(2):
        o = blk * 64
        for s in range(64):
            for u in range(64):
                t[o + s, 0, o + u] = 1.0 if s <= u else 0.0
                t[o + s, 1, o + u] = 1.0 if s >= u else 0.0
                t[o + s, 2, o + u] = 1.0 if s > u else 0.0
                t[o + s, 3, o + u] = 1.0 if s < u else 0.0
    return t

def neg_consts():
    n = np.zeros((128, 2, 4, 64), np.float32)
    for I in range(4):
        for s in range(64):
            n[:, 0, I, s] = 0.0 if s < (I + 1) * 16 else -1e4
            n[:, 1, I, s] = 0.0 if s >= I * 16 else -1e4
    return n

def m_inputs(inp, l, b, g, xfull, S, W16=None, xT=None):
    import math
    if W16 is None:
        Wk = lay_k(inp["w_in"][l])
        heads2 = [2 * g, 2 * g + 1]
        wD = np.stack([np.stack([Wk[:, :, base + h * 128: base + (h + 1) * 128] for base in (2976, 3488, 4000)], 2) for h in heads2], 2)
        wH = np.stack([np.stack([Wk[:, :, base + h * 128: base + (h + 1) * 128] for base in (416, 928, 1440, 1952, 2464)], 2) for h in heads2], 2)
        wq = lay_k(inp["w_uq"][l]).reshape(128, 2, 8, 96)[:, :, 4 * g:4 * g + 4]
        wkv = lay_k(inp["w_ukv"][l]).reshape(128, 8, 128)[:, 4 * g:4 * g + 4]
        cast = lambda a: np.ascontiguousarray(a).astype(bf16)
        wd = {"wA": cast(Wk[:, :, 0:416]), "wq": cast(wq), "wkv": cast(wkv), "wD": cast(wD), "wH": cast(wH)}
    else:
        wd = {"wA": W16[("wA", l)], "wq": W16[("wq", l, g)], "wkv": W16[("wkv", l, g)], "wD": W16[("wD", l, g)], "wH": W16[("wH", l, g)]}
    if xT is None:
        xT = np.ascontiguousarray(xfull.T)
    heads2 = [2 * g, 2 * g + 1]
    lg = inp["hg_lb_logits"]
    lbfm = np.stack([lg[:, :, h * 128:(h + 1) * 128] for h in heads2], -1)
    lbfm = np.ascontiguousarray(lbfm.transpose(2, 0, 1, 3))
    lbrep = np.ascontiguousarray(np.broadcast_to(lg[None, :, :, 2 * g * 128: 2 * g * 128 + 256], (128, 2, 4, 256)))
    lam_init = 0.8 - 0.6 * math.exp(-0.3 * l)
    lcon = np.zeros((128, 8), np.float32)
    for j in range(4):
        lcon[:, j] = 1.0 if 1 <= j <= l else 0.0
    lcon[:, 4] = lam_init
    lcon[:, 5] = 1.0 - lam_init
    cm, sm = rope_tables(S, 32)
    cd, sd = rope_tables(S, 64)
    d = {
        "xT": xT,
        "cvec": np.ascontiguousarray(np.stack([vec_chunks(inp["c"][b]), vec_chunks(inp["c_ctx"])], -1)),
        "wmod": lay_oc(inp["w_mod"][l]), "bmod": vec_chunks(inp["b_mod"][l]),
        "ng": np.ascontiguousarray(np.concatenate([vec_chunks(inp["mla_q_norm"][l]), inp["mla_kv_norm"][l][:, None]], 1)),
        "lbfm": lbfm, "lbrep": lbrep,
        "dlrep": np.ascontiguousarray(np.broadcast_to(inp["diff_lambda"][l][None], (128, 4, 64))),
        "lcon": lcon, "hgn": np.ascontiguousarray(inp["hg_norm"][l][:, None]), "sub": np.ascontiguousarray(inp["diff_subln"][l][:, None]),
        "cmT": cm, "smT": sm, "cdT": cd, "sdT": sd, "tri": tri_consts(), "negm": neg_consts(),
    }
    d.update(wd)
    return d

_PROGS = {}


def _pack_weights(inp, g):
    out = np.zeros((128, NW), np.float32)
    for l in range(4):
        Wk = lay_k(inp["w_in"][l])
        heads2 = [2 * g, 2 * g + 1]
        wD = np.stack([np.stack([Wk[:, :, base + h * 128: base + (h + 1) * 128] for base in (2976, 3488, 4000)], 2) for h in heads2], 2)
        wH = np.stack([np.stack([Wk[:, :, base + h * 128: base + (h + 1) * 128] for base in (416, 928, 1440, 1952, 2464)], 2) for h in heads2], 2)
        parts = {
            "wA": Wk[:, :, 0:416],
            "wq": lay_k(inp["w_uq"][l]).reshape(128, 2, 8, 96)[:, :, 4 * g:4 * g + 4],
            "wkv": lay_k(inp["w_ukv"][l]).reshape(128, 8, 128)[:, 4 * g:4 * g + 4],
            "wD": wD, "wH": wH,
            "wg": np.stack([lay_oc(inp["w_gate"][l, i]) for i in range(3)]).transpose(2, 0, 1, 3),
            "wb": np.stack([lay_oc(inp["w_branch"][l, i]) for i in range(3)]).transpose(2, 0, 1, 3),
            "wo": lay_oc(inp["w_o"][l]).transpose(1, 0, 2),
            "wf1": lay_oc(inp["w_ff1"][l]).transpose(1, 0, 2),
            "wf2": lay_oc(inp["w_ff2"][l]).transpose(1, 0, 2),
        }
        for k, n in W_SIZES:
            o, _ = w_off(l, k)
            out[:, o:o + n] = parts[k].reshape(128, n)
    return out


def kernel(x, c, ctx, c_ctx, w_mod, b_mod, w_in, mla_q_norm, mla_kv_norm, w_uq, w_ukv, hg_lb_logits, hg_norm,
           diff_lambda, diff_subln, w_branch, w_gate, b_gate, w_o, ln1_g, ln1_b, w_ff1, w_ff2, ln2_g, ln2_b):
    import math
    inp = dict(x=x, c=c, ctx=ctx, c_ctx=c_ctx, w_mod=w_mod, b_mod=b_mod, w_in=w_in, mla_q_norm=mla_q_norm,
               mla_kv_norm=mla_kv_norm, w_uq=w_uq, w_ukv=w_ukv, hg_lb_logits=hg_lb_logits, hg_norm=hg_norm,
               diff_lambda=diff_lambda, diff_subln=diff_subln, w_branch=w_branch, w_gate=w_gate, b_gate=b_gate,
               w_o=w_o, ln1_g=ln1_g, ln1_b=ln1_b, w_ff1=w_ff1, w_ff2=w_ff2, ln2_g=ln2_g, ln2_b=ln2_b)
    inp = {k: np.asarray(v, dtype=np.float32) for k, v in inp.items()}
    B, S = inp["x"].shape[0], inp["x"].shape[1]
    Sh = S // 2
    if S not in _PROGS:
        _PROGS[S] = build_fused(S)[0]
    nc = _PROGS[S]
    wpk = [_pack_weights(inp, g) for g in range(2)]
    wmod = np.stack([lay_oc(inp["w_mod"][l]) for l in range(4)])
    bmod = np.stack([vec_chunks(inp["b_mod"][l]) for l in range(4)])
    ng = np.stack([np.concatenate([vec_chunks(inp["mla_q_norm"][l]), inp["mla_kv_norm"][l][:, None]], 1) for l in range(4)])
    dlrep = np.stack([np.broadcast_to(inp["diff_lambda"][l][None], (128, 4, 64)) for l in range(4)])
    lcon = np.zeros((4, 128, 8), np.float32)
    for l in range(4):
        lam_init = 0.8 - 0.6 * math.exp(-0.3 * l)
        for j in range(4):
            lcon[l, :, j] = 1.0 if 1 <= j <= l else 0.0
        lcon[l, :, 4] = lam_init
        lcon[l, :, 5] = 1.0 - lam_init
    hgn = np.stack([inp["hg_norm"][l][:, None] for l in range(4)])
    sub = np.stack([inp["diff_subln"][l][:, None] for l in range(4)])
    bg = np.stack([np.stack([vec_chunks(inp["b_gate"][l, i]) for i in range(3)], 1) for l in range(4)])
    lnp = np.stack([np.stack([vec_chunks(inp[k][l]) for k in ("ln1_g", "ln1_b", "ln2_g", "ln2_b")], 1) for l in range(4)])
    cm, sm = rope_tables(S, 32)
    cd, sd = rope_tables(S, 64)
    tri, negm = tri_consts(), neg_consts()
    lg = inp["hg_lb_logits"]
    ins = []
    for core in range(8):
        b, g = core // 2, core % 2
        heads2 = [2 * g, 2 * g + 1]
        lbfm = np.ascontiguousarray(np.stack([lg[:, :, h * 128:(h + 1) * 128] for h in heads2], -1).transpose(2, 0, 1, 3))
        lbrep = np.ascontiguousarray(np.broadcast_to(lg[None, :, :, 2 * g * 128: 2 * g * 128 + 256], (128, 2, 4, 256)))
        x_own = np.concatenate([inp["ctx"][b, g * 128:(g + 1) * 128], inp["x"][b, g * Sh:(g + 1) * Sh]], 0).T
        selv = np.zeros((128, 2), np.float32)
        selv[:, g] = 1.0
        ins.append({
            "x_own": np.ascontiguousarray(x_own), "wall": wpk[g],
            "cvec": np.ascontiguousarray(np.stack([vec_chunks(inp["c"][b]), vec_chunks(inp["c_ctx"])], -1)),
            "wmod": wmod, "bmod": np.ascontiguousarray(bmod), "ng": np.ascontiguousarray(ng),
            "lbfm": lbfm, "lbrep": lbrep, "dlrep": np.ascontiguousarray(dlrep), "lcon": lcon,
            "hgn": np.ascontiguousarray(hgn), "sub": np.ascontiguousarray(sub),
            "cmT": cm, "smT": sm, "cdT": cd, "sdT": sd, "tri": tri, "negm": negm,
            "bg": np.ascontiguousarray(bg), "lnp": np.ascontiguousarray(lnp), "sel": selv,
        })
    res = run_bass_kernel_spmd(nc, ins, core_ids=list(range(8)))
    out = np.empty((B, S, 1024), np.float32)
    for core in range(8):
        b, g = core // 2, core % 2
        out[b, g * Sh:(g + 1) * Sh] = np.asarray(res.results[core]["xout"]).T
    return out
```

```python
import ml_dtypes
from concourse.bass_utils import run_bass_kernel_spmd
import numpy as np
import concourse.bass as bass
import concourse.mybir as mybir
from contextlib import ExitStack

F32 = mybir.dt.float32
BF16 = mybir.dt.bfloat16
AF = mybir.ActivationFunctionType
ALU = mybir.AluOpType
AX = mybir.AxisListType

ENGS = ("pe", "act", "dve", "pool", "sp")


class Buf:
    __slots__ = ("name", "w", "r", "semval", "sem", "track", "inc", "slot")

    def __init__(self, name, track=True, inc=16):
        self.name = name
        self.track = track
        self.inc = inc
        self.w = None
        self.r = []
        self.semval = 0
        self.sem = None
        self.slot = None


class V:
    __slots__ = ("ap", "buf")

    def __init__(self, ap, buf):
        self.ap = ap
        self.buf = buf

    def __getitem__(self, idx):
        return V(self.ap[idx], self.buf)

    def m(self, f):
        return V(f(self.ap), self.buf)


class Tl:
    def __init__(self, h, buf):
        self.h = h
        self.buf = buf

    def __getitem__(self, idx):
        return V(self.h[idx], self.buf)

    def ap(self):
        return V(self.h.ap() if hasattr(self.h, "ap") else self.h[:], self.buf)


class Prog:
    def __init__(self, nc):
        self.nc = nc
        self.ops = {e: [] for e in ENGS}
        self.vc = {e: {} for e in ENGS}
        self.seen_d = {e: {} for e in ENGS}
        self.snap = {e: [] for e in ENGS}
        self.dbufs = []
        self.slotval = []
        self.slotfree = []
        self.live = []
        self.epoch = 0
        self.epoch_of = {e: [] for e in ENGS}
        self.epoch_start = {e: 0 for e in ENGS}
        self.last_real = {e: -1 for e in ENGS}
        self.sb_off = 16512
        self.sb_hwm = 0
        self.nbuf = 0
        self.psum = []

    def sbuf(self, name, shape, dtype, nbufs=1):
        esz = mybir.dt.size(dtype)
        per = int(np.prod(shape[1:])) * esz
        per = (per + 63) // 64 * 64
        out = []
        for i in range(nbufs):
            self.nbuf += 1
            nm = f"{name}_{self.nbuf}"
            h = self.nc.alloc_sbuf_tensor_at(nm, list(shape), dtype, offset=self.sb_off)
            self.sb_off += per
            self.sb_hwm = max(self.sb_hwm, self.sb_off)
            assert self.sb_off <= 229376, f"SBUF overflow {self.sb_off}"
            tl = Tl(h, Buf(nm))
            self.live.append((self.sb_off - per, tl.buf))
            out.append(tl)
        return out[0] if nbufs == 1 else out

    def mark(self):
        return self.sb_off

    def release(self, mark):
        keep = []
        for off, b in self.live:
            if off >= mark:
                if b.slot is not None:
                    self.slotval[b.slot] = b.semval
                    self.slotfree.append(b.slot)
                    if b in self.dbufs:
                        self.dbufs.remove(b)
            else:
                keep.append((off, b))
        self.live = keep
        self.sb_off = mark

    def dram(self, name, shape, dtype, kind="Internal"):
        h = self.nc.dram_tensor(name, list(shape), dtype, kind=kind)
        return Tl(h, Buf(name, track=False))

    def _dep_needed(self, eng, dep):
        if dep is None:
            return False
        if dep[0] == "e":
            _, e2, idx = dep
            return self.vc[eng].get(e2, -1) < idx
        else:
            _, buf, val = dep
            return self.seen_d[eng].get(buf, 0) < val

    def _apply_wait(self, eng, dep):
        if dep[0] == "e":
            _, e2, idx = dep
            vc = self.vc[eng]
            for k, v in self.snap[e2][idx].items():
                if vc.get(k, -1) < v:
                    vc[k] = v
            if vc.get(e2, -1) < idx:
                vc[e2] = idx
        else:
            _, buf, val = dep
            self.seen_d[eng][buf] = val

    def op(self, eng, fn, reads=(), writes=(), dma_owner=None, acc=False):
        deps = []
        rb = [x if isinstance(x, Buf) else x.buf for x in reads]
        wb = [x if isinstance(x, Buf) else x.buf for x in writes]
        rb = [b for b in rb if b.track]
        wb = [b for b in wb if b.track]
        for b in rb:
            if b.w is not None:
                deps.append(b.w)
        for b in wb:
            if b.w is not None:
                if not (acc and b.w[0] == "e" and b.w[1] == "pe" and eng == "pe"):
                    deps.append(b.w)
            deps.extend(b.r)
        waits = []
        for d in deps:
            if eng == "pe" and d[0] == "e" and d[1] == "pe":
                continue
            if self._dep_needed(eng, d):
                self._apply_wait(eng, d)
                waits.append(d)
        best = {}
        for d in waits:
            key = (d[0], d[1])
            if key not in best or best[key][2] < d[2]:
                best[key] = d
        waits = list(best.values())
        idx = len(self.ops[eng])
        if dma_owner is not None:
            ob = dma_owner if isinstance(dma_owner, Buf) else dma_owner.buf
            if ob.slot is None:
                if self.slotfree:
                    ob.slot = self.slotfree.pop()
                else:
                    ob.slot = len(self.slotval)
                    self.slotval.append(0)
                ob.semval = self.slotval[ob.slot]
                self.dbufs.append(ob)
            ob.semval += ob.inc
            me = ("d", ob, ob.semval)
        else:
            ob = None
            me = ("e", eng, idx)
        self.ops[eng].append([fn, waits, False, ob])
        self.snap[eng].append(dict(self.vc[eng]))
        self.epoch_of[eng].append(self.epoch)
        if ob is None:
            self.last_real[eng] = idx
        for b in rb:
            b.r.append(me)
        for b in wb:
            b.w = me
            b.r = []
        return me

    def barrier(self):
        last = dict(self.last_real)
        for e in ENGS:
            waits = []
            for e2 in ENGS:
                if e2 != e and last[e2] >= 0:
                    d = ("e", e2, last[e2])
                    if self._dep_needed(e, d):
                        waits.append(d)
            for b in self.dbufs:
                d = ("d", b, b.semval)
                if self._dep_needed(e, d):
                    waits.append(d)
            for d in waits:
                self._apply_wait(e, d)
            self.ops[e].append([None, waits, False, None])
            self.snap[e].append(dict(self.vc[e]))
            self.epoch_of[e].append(self.epoch)
        if max(len(self.ops[e]) - self.epoch_start[e] for e in ENGS) > 20000:
            self.epoch += 1
            for e in ENGS:
                self.epoch_start[e] = len(self.ops[e])

    def dma(self, out, in_, eng="sp", owner=None, **kw):
        if owner is None:
            owner = out if out.buf.track else in_
        assert (owner.buf if not isinstance(owner, Buf) else owner).track
        return self.op(eng, lambda E: E.dma_start(out=out.ap, in_=in_.ap, **kw),
                       reads=[in_], writes=[out], dma_owner=owner)

    def allgather(self, dst, src, groups):
        if not hasattr(self, "ccbuf"):
            self.ccbuf = Buf("cc", track=True, inc=1)
        return self.op("pool", lambda E: E.collective_compute("AllGather", ALU.bypass, replica_groups=groups,
                                                             ins=[src.h.ap().opt()], outs=[dst.h.ap().opt()]),
                       reads=[], writes=[self.ccbuf], dma_owner=self.ccbuf)

    def mm(self, out, lhsT, rhs, start=True, stop=True, **kw):
        return self.op("pe", lambda E: E.matmul(out.ap, lhsT.ap, rhs.ap, start=start, stop=stop, **kw),
                       reads=[lhsT, rhs], writes=[out], acc=not start)

    def act(self, out, in_, func, bias=None, scale=None, accum_out=None, eng="act", extra_reads=()):
        reads = [in_] + list(extra_reads)
        kw = {}
        if bias is not None:
            if isinstance(bias, V):
                reads.append(bias)
                kw["bias"] = bias.ap
            else:
                kw["bias"] = bias
        if scale is not None:
            if isinstance(scale, V):
                reads.append(scale)
                kw["scale"] = scale.ap
            else:
                kw["scale"] = scale
        writes = [out]
        if accum_out is not None:
            kw["accum_out"] = accum_out.ap
            writes.append(accum_out)
        return self.op(eng, lambda E: E.activation(out.ap, in_.ap, func, **kw), reads=reads, writes=writes)

    def tt(self, out, in0, in1, op, eng="dve"):
        return self.op(eng, lambda E: E.tensor_tensor(out.ap, in0.ap, in1.ap, op), reads=[in0, in1], writes=[out])

    def ts(self, out, in0, s1, op0, s2=None, op1=None, eng="dve"):
        reads = [in0]
        a1 = s1
        if isinstance(s1, V):
            reads.append(s1)
            a1 = s1.ap
        a2 = s2
        if isinstance(s2, V):
            reads.append(s2)
            a2 = s2.ap
        if op1 is None:
            return self.op(eng, lambda E: E.tensor_scalar(out.ap, in0.ap, a1, None, op0), reads=reads, writes=[out])
        return self.op(eng, lambda E: E.tensor_scalar(out.ap, in0.ap, a1, a2, op0, op1), reads=reads, writes=[out])

    def stt(self, out, in0, scalar, in1, op0, op1, eng="dve"):
        reads = [in0, in1]
        a = scalar
        if isinstance(scalar, V):
            reads.append(scalar)
            a = scalar.ap
        return self.op(eng, lambda E: E.scalar_tensor_tensor(out.ap, in0.ap, a, in1.ap, op0, op1), reads=reads, writes=[out])

    def copy(self, out, in_, eng="dve"):
        if eng == "act":
            return self.op(eng, lambda E: E.copy(out.ap, in_.ap), reads=[in_], writes=[out])
        return self.op(eng, lambda E: E.tensor_copy(out.ap, in_.ap), reads=[in_], writes=[out])

    def memset(self, out, val, eng="dve"):
        return self.op(eng, lambda E: E.memset(out.ap, val), reads=[], writes=[out])

    def recip(self, out, in_):
        return self.op("dve", lambda E: E.reciprocal(out.ap, in_.ap), reads=[in_], writes=[out])

    def emit(self):
        nc = self.nc
        self.barrier()
        sig = {e: set() for e in ENGS}
        for e in ENGS:
            for fn, waits, _, _ in self.ops[e]:
                for d in waits:
                    if d[0] == "e":
                        sig[d[1]].add(d[2])
        rank = {}
        for e in ENGS:
            cnt = {}
            for idx in sorted(sig[e]):
                ep = self.epoch_of[e][idx]
                cnt[ep] = cnt.get(ep, 0) + 1
                rank[(e, idx)] = cnt[ep]
        with ExitStack() as st:
            esem = {(e, ep): st.enter_context(nc.semaphore(f"s_{e}_{ep}")) for e in ENGS for ep in range(self.epoch + 1)}
            slotsem = [st.enter_context(nc.semaphore(f"d{i}")) for i in range(len(self.slotval))]
            block = st.enter_context(nc.Block())
            engmap = {"pe": block.tensor, "act": block.scalar, "dve": block.vector,
                      "pool": block.gpsimd, "sp": block.sync}
            n_inst = 0
            for e in ENGS:
                ops = self.ops[e]
                n_inst += len(ops)

                def body(E, ops=ops, e=e):
                    for idx, (fn, waits, _, ob) in enumerate(ops):
                        for d in waits:
                            if d[0] == "e":
                                E.wait_ge(esem[(d[1], self.epoch_of[d[1]][d[2]])], rank[(d[1], d[2])])
                            else:
                                E.wait_ge(slotsem[d[1].slot], d[2])
                        if fn is None:
                            continue
                        ins = fn(E)
                        if ob is not None:
                            ins.then_inc(slotsem[ob.slot], ob.inc)
                        elif idx in sig[e]:
                            ins.then_inc(esem[(e, self.epoch_of[e][idx])], 1)

                engmap[e](body)
            self.n_inst = n_inst
        return nc

bf16 = ml_dtypes.bfloat16

D = 1024
FF = 2816
DEPTH = 4
ALPHA = (2 * DEPTH) ** 0.25
LN_EPS = 1e-5


class PS:
    def __init__(self, nc, n=8, prefix="ps"):
        self.b = [Tl(nc.alloc_psum_tensor(f"{prefix}{i}", [128, 512], F32), Buf(f"{prefix}{i}")) for i in range(n)]
        self.i = 0

    def get(self):
        t = self.b[self.i % len(self.b)]
        self.i += 1
        return t


def emit_mod(P, ps, cvec_d, wmod_d, bmod_d, groups, modT=None):
    cv = P.sbuf("cv", [128, 8, 2], F32)
    P.dma(cv[:], cvec_d[:])
    e = P.sbuf("cve", [128, 8, 2], F32)
    P.act(e[:], cv[:], AF.Exp, scale=-1.0)
    P.ts(e[:], e[:], 1.0, ALU.add)
    P.recip(e[:], e[:])
    ca = P.sbuf("ca", [128, 8, 2], F32)
    P.tt(ca[:], cv[:], e[:], ALU.mult)
    bm = P.sbuf("bm", [128, 4 * len(groups)], F32)
    P.dma(bm[:], bmod_d[:])
    if modT is None:
        modT = P.sbuf("modT", [128, 48, 2], F32)
    mk = P.mark()
    wslots = P.sbuf("wm", [128, 4, 1024], F32, nbufs=2)
    for gi, g in enumerate(groups):
        slot = wslots[gi % 2]
        P.dma(slot[:], wmod_d[4 * g:4 * g + 4].m(lambda a: a.rearrange("o p k -> p o k")))
        for j in range(4):
            oc = 4 * g + j
            pt = ps.get()
            for kc in range(8):
                P.mm(pt[:, 0:2], slot[:, j, kc * 128:(kc + 1) * 128], ca[:, kc, :], start=kc == 0, stop=kc == 7)
            P.ts(modT[:, oc, :], pt[:, 0:2], bm[:, oc:oc + 1], ALU.add)
    P.barrier()
    P.release(mk)
    return modT


def emit_ln(P, ps, r, out, W, g, b, onesD, sq, st1, st2, epsT):
    pm = ps.get()
    pq = ps.get()
    for kc in range(8):
        P.act(sq[:, kc, :W], r[:, kc, :W], AF.Square)
    for kc in range(8):
        P.mm(pm[:, :W], onesD[:], r[:, kc, :W], start=kc == 0, stop=kc == 7)
    for kc in range(8):
        P.mm(pq[:, :W], onesD[:], sq[:, kc, :W], start=kc == 0, stop=kc == 7)
    P.copy(st1[:, :W], pm[:, :W], eng="act")
    P.tt(st2[:, :W], st1[:, :W], st1[:, :W], ALU.mult)
    P.tt(st2[:, :W], pq[:, :W], st2[:, :W], ALU.subtract)
    P.act(st2[:, :W], st2[:, :W], AF.Sqrt, bias=epsT[:, 0:1], scale=1.0)
    P.recip(st2[:, :W], st2[:, :W])
    for kc in range(8):
        P.tt(out[:, kc, :W], r[:, kc, :W], st1[:, :W], ALU.subtract)
        P.tt(out[:, kc, :W], out[:, kc, :W], st2[:, :W], ALU.mult)
        P.ts(out[:, kc, :W], out[:, kc, :W], g[:, kc:kc + 1], ALU.mult, b[:, kc:kc + 1], ALU.add)


def emit_T(P, ps, S_half, E, modT, sels, last):
    nc = P.nc
    TT = 128 + S_half
    mk_all = P.mark()
    bg, lnp = E["bg"], E["lnp"]
    SH1, SC1, G1, SH2, SC2, G2 = 0, 8, 16, 24, 32, 40

    bgs = P.sbuf("bgs", [128, 3, 8], F32)
    P.dma(bgs[:], bg[:])
    lns = P.sbuf("lns", [128, 4, 8], F32)
    P.dma(lns[:], lnp[:])
    onesD = P.sbuf("onesD", [128, 128], F32)
    P.memset(onesD[:], 1.0 / D)
    epsT = P.sbuf("epsT", [128, 1], F32)
    P.memset(epsT[:], LN_EPS)

    xt = P.sbuf("xt", [128, 8, 512], F32, nbufs=1)
    xt = [xt, xt]
    yt = P.sbuf("yt", [128, 12, 512], BF16)
    h = P.sbuf("h", [128, 8, 512], BF16)
    m = P.sbuf("m", [128, 8, 512], F32)
    mb = P.sbuf("mb", [128, 8, 512], BF16)
    r = P.sbuf("r", [128, 8, 512], F32)
    x1 = P.sbuf("x1", [128, 8, 512], F32)
    a = P.sbuf("a", [128, 22, 512], BF16)
    st1 = P.sbuf("st1", [128, 512], F32)
    st2 = P.sbuf("st2", [128, 512], F32)
    gate = P.sbuf("gate", [128, 512], F32, nbufs=2)
    tmp = P.sbuf("tmp", [128, 512], F32, nbufs=2)
    wgs = P.sbuf("wgs", [128, 1024], BF16, nbufs=6)
    wbs = P.sbuf("wbs", [128, 512], BF16, nbufs=6)
    wos = P.sbuf("wos", [128, 1024], BF16, nbufs=4)
    wf1s = P.sbuf("wf1s", [128, 1024], BF16, nbufs=8)
    wf2s = P.sbuf("wf2s", [128, 2816], BF16, nbufs=3)

    tiles = [(0, 128, 1)]
    t0 = 128
    while t0 < TT:
        tiles.append((t0, 512, 0))
        t0 += 512
    cnt = {"g": 0, "o": 0, "f1": 0, "f2": 0, "gt": 0}

    def xv(dr, t0, W):
        return dr[:, t0:t0 + W].m(lambda ap: ap.rearrange("(c p) t -> p c t", p=128))

    for ti, (t0, W, cm) in enumerate(tiles):
        X = xt[ti % 2]
        E["xtile_load"](X, t0, W)
        for hsel in range(2):
            E["ycand_load"](yt if hsel == 0 else a, hsel, t0, W, cm)
        P.ts(yt[:, :, :W], yt[:, :, :W], sels[:, 0:1], ALU.mult)
        P.stt(yt[:, :, :W], a[:, 0:12, :W], sels[:, 1:2], yt[:, :, :W], ALU.mult, ALU.add)
        for kc in range(8):
            P.ts(h[:, kc, :W], X[:, kc, :W], modT[:, SC1 + kc, cm:cm + 1], ALU.mult,
                 modT[:, SH1 + kc, cm:cm + 1], ALU.add)
        for oc in range(8):
            for i in range(3):
                wgt = wgs[cnt["g"] % 6]
                wbt = wbs[cnt["g"] % 6]
                cnt["g"] += 1
                P.dma(wgt[:], E["wg"](i, oc))
                P.dma(wbt[:], E["wb"](i, oc))
                pG = ps.get()
                for kc in range(8):
                    P.mm(pG[:, :W], wgt[:, kc * 128:(kc + 1) * 128], h[:, kc, :W], start=kc == 0, stop=kc == 7)
                gt = gate[cnt["gt"] % 2]
                tp = tmp[cnt["gt"] % 2]
                cnt["gt"] += 1
                P.act(gt[:, :W], pG[:, :W], AF.Sigmoid, bias=bgs[:, i, oc:oc + 1])
                pB = ps.get()
                for kc in range(4):
                    P.mm(pB[:, :W], wbt[:, kc * 128:(kc + 1) * 128], yt[:, 4 * i + kc, :W], start=kc == 0, stop=kc == 3)
                if i == 0:
                    P.tt(m[:, oc, :W], gt[:, :W], pB[:, :W], ALU.mult)
                elif i == 1:
                    P.tt(tp[:, :W], gt[:, :W], pB[:, :W], ALU.mult)
                    P.tt(m[:, oc, :W], m[:, oc, :W], tp[:, :W], ALU.add)
                else:
                    P.tt(tp[:, :W], gt[:, :W], pB[:, :W], ALU.mult)
                    P.tt(mb[:, oc, :W], m[:, oc, :W], tp[:, :W], ALU.add)
        for kc in range(8):
            P.ts(X[:, kc, :W], X[:, kc, :W], ALPHA, ALU.mult)
        for oc in range(8):
            wt = wos[cnt["o"] % 4]
            cnt["o"] += 1
            P.dma(wt[:], E["wo"](oc))
            pY = ps.get()
            for kc in range(8):
                P.mm(pY[:, :W], wt[:, kc * 128:(kc + 1) * 128], mb[:, kc, :W], start=kc == 0, stop=kc == 7)
            P.stt(r[:, oc, :W], pY[:, :W], modT[:, G1 + oc, cm:cm + 1], X[:, oc, :W], ALU.mult, ALU.add)
        emit_ln(P, ps, r, x1, W, lns[:, 0, :], lns[:, 1, :], onesD, m, st1, st2, epsT)
        for kc in range(8):
            P.ts(h[:, kc, :W], x1[:, kc, :W], modT[:, SC2 + kc, cm:cm + 1], ALU.mult,
                 modT[:, SH2 + kc, cm:cm + 1], ALU.add)
        for fc in range(22):
            w1 = wf1s[cnt["f1"] % 8]
            w2 = wf1s[(cnt["f1"] + 1) % 8]
            cnt["f1"] += 2
            P.dma(w1[:], E["wf1"](fc))
            P.dma(w2[:], E["wf1"](22 + fc))
            pG = ps.get()
            for kc in range(8):
                P.mm(pG[:, :W], w1[:, kc * 128:(kc + 1) * 128], h[:, kc, :W], start=kc == 0, stop=kc == 7)
            pU = ps.get()
            for kc in range(8):
                P.mm(pU[:, :W], w2[:, kc * 128:(kc + 1) * 128], h[:, kc, :W], start=kc == 0, stop=kc == 7)
            gt = gate[cnt["gt"] % 2]
            cnt["gt"] += 1
            P.act(gt[:, :W], pG[:, :W], AF.Silu)
            P.tt(a[:, fc, :W], gt[:, :W], pU[:, :W], ALU.mult)
        for kc in range(8):
            P.ts(x1[:, kc, :W], x1[:, kc, :W], ALPHA, ALU.mult)
        for oc in range(8):
            wt = wf2s[cnt["f2"] % 3]
            cnt["f2"] += 1
            P.dma(wt[:], E["wf2"](oc))
            pF = ps.get()
            for fc in range(22):
                P.mm(pF[:, :W], wt[:, fc * 128:(fc + 1) * 128], a[:, fc, :W], start=fc == 0, stop=fc == 21)
            P.stt(r[:, oc, :W], pF[:, :W], modT[:, G2 + oc, cm:cm + 1], x1[:, oc, :W], ALU.mult, ALU.add)
        emit_ln(P, ps, r, m, W, lns[:, 2, :], lns[:, 3, :], onesD, x1, st1, st2, epsT)
        E["xtile_store"](m, t0, W)
        if last and not cm:
            P.dma(xv(E["xout"], t0 - 128, W), m[:, :, :W], eng="pool", owner=m)
    P.barrier()
    P.release(mk_all)


RMS_EPS = 1e-6
MLA_SCALE = 96 ** -0.5
DIFF_SCALE = 64 ** -0.5
HG_SCALE = 128 ** -0.5
CTX = 256


def emit_M(P, ps, S, E, modT):
    nc = P.nc
    dbg = False
    Tt = CTX + S
    NT = Tt // 128
    mk_all = P.mark()
    cvec, wmod, bmod = E["cvec"], E["wmod"], E["bmod"]
    wA, wq, wkv, ng, wD, wH = E["wA"], E["wq"], E["wkv"], E["ng"], E["wD"], E["wH"]
    lbfm, lbrep, dlrep, lcon, hgn, sub = E["lbfm"], E["lbrep"], E["dlrep"], E["lcon"], E["hgn"], E["sub"]
    cmT, smT, cdT, sdT, tri, negm = E["cmT"], E["smT"], E["cdT"], E["sdT"], E["tri"], E["negm"]
    hT, cqnT, ckvnT, kpeT, qTs, qdT, kdT = E["hT"], E["cqnT"], E["ckvnT"], E["kpeT"], E["qTs"], E["qdT"], E["kdT"]
    hqT, hkT, hogT, hg_, hkt, hv = E["hqT"], E["hkT"], E["hogT"], E["hg_"], E["hkt"], E["hv"]
    phases = "ABCD"
    bank = ps.b
    emit_mod(P, ps, cvec, wmod, bmod, list(range(12)), modT=modT)
    P.ts(modT[:, 8:16, :], modT[:, 8:16, :], 1.0, ALU.add)
    P.ts(modT[:, 32:40, :], modT[:, 32:40, :], 1.0, ALU.add)
    SH1, SC1 = 0, 8

    ones32 = P.sbuf("ones32", [128, 128], F32)
    P.memset(ones32[:], 1.0)
    onesb = P.sbuf("onesb", [128, 128], BF16)
    P.memset(onesb[:], 1.0)
    o256 = P.sbuf("o256", [128, 128], F32)
    P.memset(o256[:], 1.0 / 256)
    o128 = P.sbuf("o128", [128, 128], F32)
    P.memset(o128[:], 1.0 / 128)
    epsR = P.sbuf("epsR", [128, 1], F32)
    P.memset(epsR[:], RMS_EPS)
    ngs = P.sbuf("ngs", [128, 3], F32)
    P.dma(ngs[:], ng[:])
    lc = P.sbuf("lc", [128, 8], F32)
    P.dma(lc[:], lcon[:])
    hgns = P.sbuf("hgns", [128, 1], F32)
    P.dma(hgns[:], hgn[:])
    subs = P.sbuf("subs", [128, 1], F32)
    P.dma(subs[:], sub[:])
    P.ts(subs[:], subs[:], lc[:, 5:6], ALU.mult)
    tris = P.sbuf("tris", [128, 4, 128], F32)
    P.dma(tris[:], tri[:])
    lbf = P.sbuf("lbf", [128, 2, 4, 2], F32)
    P.dma(lbf[:], lbfm[:])
    lbr = P.sbuf("lbr", [128, 2, 4, 256], F32)
    P.dma(lbr[:], lbrep[:])
    lb_fm = P.sbuf("lb_fm", [128, 2, 2], F32)
    oml_fm = P.sbuf("oml_fm", [128, 2, 2], F32)
    lb_row = P.sbuf("lb_row", [128, 2, 256], F32)
    oml_row = P.sbuf("oml_row", [128, 2, 256], F32)
    nlam = P.sbuf("nlam", [128, 1], F32)
    mk0 = P.mark()
    for (src, dst, dst1, n) in ((lbf, lb_fm, oml_fm, 2), (lbr, lb_row, oml_row, 256)):
        P.act(src[:], src[:], AF.Exp)
        tot = P.sbuf("lb_tot", [128, 2, n], F32)
        part = P.sbuf("lb_part", [128, 2, n], F32)
        tmp = P.sbuf("lb_tmp", [128, 2, n], F32)
        P.tt(tot[:], src[:, :, 0, :], src[:, :, 1, :], ALU.add)
        P.tt(tot[:], tot[:], src[:, :, 2, :], ALU.add)
        P.tt(tot[:], tot[:], src[:, :, 3, :], ALU.add)
        P.ts(part[:], src[:, :, 0, :], lc[:, 0:1], ALU.mult)
        for j in range(1, 4):
            P.ts(tmp[:], src[:, :, j, :], lc[:, j:j + 1], ALU.mult)
            P.tt(part[:], part[:], tmp[:], ALU.add)
        P.recip(tot[:], tot[:])
        P.tt(dst[:], part[:], tot[:], ALU.mult)
        P.ts(dst1[:], dst[:], -1.0, ALU.mult, 1.0, ALU.add)
    dls = P.sbuf("dls", [128, 4, 64], F32)
    P.dma(dls[:], dlrep[:])
    pr = P.sbuf("dl_pr", [128, 2, 64], F32)
    P.tt(pr[:, 0, :], dls[:, 0, :], dls[:, 1, :], ALU.mult)
    P.tt(pr[:, 1, :], dls[:, 2, :], dls[:, 3, :], ALU.mult)
    sm2 = P.sbuf("dl_sm", [128, 2], F32)
    w_ = 64
    while w_ > 1:
        h_ = w_ // 2
        P.tt(pr[:, :, 0:h_], pr[:, :, 0:h_], pr[:, :, h_:w_], ALU.add)
        w_ = h_
    P.copy(sm2[:], pr[:, :, 0])
    P.act(sm2[:], sm2[:], AF.Exp)
    P.tt(nlam[:], sm2[:, 1:2], sm2[:, 0:1], ALU.subtract)
    P.tt(nlam[:], nlam[:], lc[:, 4:5], ALU.subtract)
    P.barrier()
    P.release(mk0)

    tiles = [(0, 256, 1)] + [(CTX + 512 * i, 512, 0) for i in range(S // 512)]

    def fm(dr, t0, W, nch):
        if nch == 1:
            return dr[:, t0:t0 + W]
        return dr[:, t0:t0 + W].m(lambda ap: ap.rearrange("(c p) t -> p c t", p=128))

    def proj_fm(pt, prow, w_of_kc, hsrc, W, KC=8):
        for kc in range(KC):
            P.mm(pt[0:prow, :W], w_of_kc(kc), hsrc(kc), start=kc == 0, stop=kc == KC - 1)

    mk = P.mark()
    xt = P.sbuf("xt", [128, 8, 512], F32, nbufs=2)
    hb = P.sbuf("hb", [128, 8, 512], BF16, nbufs=2)
    for ti, (t0, W, cm) in enumerate(tiles):
        X = xt[ti % 2]
        H = hb[ti % 2]
        E["xload"](X, ti, t0, W)
        for kc in range(8):
            P.ts(H[:, kc, :W], X[:, kc, :W], modT[:, SC1 + kc, cm:cm + 1], ALU.mult,
                 modT[:, SH1 + kc, cm:cm + 1], ALU.add)
        P.dma(fm(hT, t0, W, 8), H[:, :, :W], eng="pool")
    P.barrier()
    P.release(mk)

    def rope(o1, o2, a, b, cs, sn, W, n, t1, t2):
        P.tt(t1[0:n, :W], a, cs, ALU.mult)
        P.tt(t2[0:n, :W], b, sn, ALU.mult)
        P.tt(o1, t1[0:n, :W], t2[0:n, :W], ALU.subtract)
        P.tt(t1[0:n, :W], a, sn, ALU.mult)
        P.tt(t2[0:n, :W], b, cs, ALU.mult)
        P.tt(o2, t1[0:n, :W], t2[0:n, :W], ALU.add)

    if "A" in phases:
        mk = P.mark()
        wAs = P.sbuf("wAs", [128, 8, 416], BF16)
        P.dma(wAs[:], wA[:])
        hb = P.sbuf("hbA", [128, 8, 512], BF16, nbufs=2)
        cq = P.sbuf("cq", [128, 3, 512], F32)
        sq = P.sbuf("sqA", [128, 3, 512], F32)
        rs = P.sbuf("rsA", [128, 2, 512], F32)
        cqn = P.sbuf("cqn", [128, 3, 512], BF16, nbufs=2)
        cs = P.sbuf("csA", [16, 2, 512], F32, nbufs=2)
        t1 = P.sbuf("t1A", [16, 512], F32)
        t2 = P.sbuf("t2A", [16, 512], F32)
        kp = P.sbuf("kpA", [16, 2, 512], BF16, nbufs=2)
        for ti, (t0, W, cm) in enumerate(tiles):
            H = hb[ti % 2]
            CN = cqn[ti % 2]
            CS = cs[ti % 2]
            KP = kp[ti % 2]
            P.dma(H[:, :, :W], fm(hT, t0, W, 8))
            P.dma(CS[:, 0, :W], cmT[:, t0:t0 + W])
            P.dma(CS[:, 1, :W], smT[:, t0:t0 + W])
            for oc in range(3):
                pt = ps.get()
                proj_fm(pt, 128, lambda kc: wAs[:, kc, oc * 128:(oc + 1) * 128], lambda kc: H[:, kc, :W], W)
                P.copy(cq[:, oc, :W], pt[:, :W], eng="act")
                P.act(sq[:, oc, :W], pt[:, :W], AF.Square)
            p1 = ps.get()
            P.mm(p1[:, :W], o256[:], sq[:, 0, :W], start=True, stop=False)
            P.mm(p1[:, :W], o256[:], sq[:, 1, :W], start=False, stop=True)
            p2 = ps.get()
            P.mm(p2[:, :W], o128[:], sq[:, 2, :W], start=True, stop=True)
            for j, pp in enumerate((p1, p2)):
                P.act(rs[:, j, :W], pp[:, :W], AF.Sqrt, bias=epsR[:, 0:1], scale=1.0)
                P.recip(rs[:, j, :W], rs[:, j, :W])
            for oc in range(3):
                P.stt(CN[:, oc, :W], cq[:, oc, :W], ngs[:, oc:oc + 1], rs[:, 0 if oc < 2 else 1, :W], ALU.mult, ALU.mult)
            P.dma(fm(cqnT, t0, W, 2), CN[:, 0:2, :W], eng="pool")
            P.dma(ckvnT[:, t0:t0 + W], CN[:, 2, :W], eng="pool")
            pa = ps.get()
            proj_fm(pa, 16, lambda kc: wAs[:, kc, 384:400], lambda kc: H[:, kc, :W], W)
            pb = ps.get()
            proj_fm(pb, 16, lambda kc: wAs[:, kc, 400:416], lambda kc: H[:, kc, :W], W)
            rope(KP[:, 0, :W], KP[:, 1, :W], pa[0:16, :W], pb[0:16, :W], CS[:, 0, :W], CS[:, 1, :W], W, 16, t1, t2)
            P.dma(kpeT[0:16, t0:t0 + W], KP[:, 0, :W], eng="pool")
            P.dma(kpeT[16:32, t0:t0 + W], KP[:, 1, :W], eng="pool")
        P.barrier()
        P.release(mk)

    if "B" in phases:
        mk = P.mark()
        wqs = P.sbuf("wqs", [128, 2, 4, 96], BF16)
        P.dma(wqs[:], wq[:])
        wkvs = P.sbuf("wkvs", [128, 4, 128], BF16)
        P.dma(wkvs[:], wkv[:])
        KT = P.sbuf("KT", [96, Tt], BF16)
        VA = P.sbuf("VA", [128, NT, 65], BF16)
        cqb = P.sbuf("cqb", [128, 3, 512], BF16, nbufs=2)
        cs = P.sbuf("csB", [16, 2, 512], F32, nbufs=2)
        t1 = P.sbuf("t1B", [16, 512], F32)
        t2 = P.sbuf("t2B", [16, 512], F32)
        qn = P.sbuf("qnB", [64, 512], BF16, nbufs=2)
        qp = P.sbuf("qpB", [16, 2, 512], BF16, nbufs=2)
        qt = P.sbuf("qtB", [96, 512], BF16, nbufs=2)
        pT = P.sbuf("pTB", [128, 512], BF16, nbufs=3)
        rsum = P.sbuf("rsumB", [128, 512], F32)
        rc = P.sbuf("rcB", [64, 512], F32)
        yo = P.sbuf("yoB", [64, 512], BF16, nbufs=2)
        psS = [bank[0], bank[1], bank[2]]
        psO = [bank[3], bank[4]]
        psX = [bank[5], bank[6], bank[7]]
        cS = cO = cX = cP = 0
        for hh in range(4):
            P.memset(VA[:, :, 64:65], 1.0)
            for ti, (t0, W, cm) in enumerate(tiles):
                C = cqb[ti % 2]
                CS = cs[ti % 2]
                P.dma(C[:, 0:2, :W], fm(cqnT, t0, W, 2))
                P.dma(C[:, 2, :W], ckvnT[:, t0:t0 + W])
                P.dma(CS[:, 0, :W], cmT[:, t0:t0 + W])
                P.dma(CS[:, 1, :W], smT[:, t0:t0 + W])
                pt = psX[cX % 3]; cX += 1
                proj_fm(pt, 64, lambda kc: wqs[:, kc, hh, 0:64], lambda kc: C[:, kc, :W], W, KC=2)
                QN = qn[ti % 2]
                P.copy(QN[:, :W], pt[0:64, :W], eng="act")
                P.dma(qTs[0:64, t0:t0 + W], QN[:, :W], eng="pool")
                pa = psX[cX % 3]; cX += 1
                proj_fm(pa, 16, lambda kc: wqs[:, kc, hh, 64:80], lambda kc: C[:, kc, :W], W, KC=2)
                pb = psX[cX % 3]; cX += 1
                proj_fm(pb, 16, lambda kc: wqs[:, kc, hh, 80:96], lambda kc: C[:, kc, :W], W, KC=2)
                QP = qp[ti % 2]
                rope(QP[:, 0, :W], QP[:, 1, :W], pa[0:16, :W], pb[0:16, :W], CS[:, 0, :W], CS[:, 1, :W], W, 16, t1, t2)
                P.dma(qTs[64:80, t0:t0 + W], QP[:, 0, :W], eng="pool")
                P.dma(qTs[80:96, t0:t0 + W], QP[:, 1, :W], eng="pool")
                pk = psX[cX % 3]; cX += 1
                P.mm(pk[0:64, :W], wkvs[:, hh, 0:64], C[:, 2, :W])
                P.copy(KT[0:64, t0:t0 + W], pk[0:64, :W], eng="act")
                pv = psX[cX % 3]; cX += 1
                for j in range(W // 128):
                    P.mm(pv[:, j * 64:(j + 1) * 64], C[:, 2, j * 128:(j + 1) * 128], wkvs[:, hh, 64:128])
                kt0 = t0 // 128
                P.copy(VA[:, kt0:kt0 + W // 128, 0:64],
                       pv[:, 0:(W // 128) * 64].m(lambda ap: ap.rearrange("p (j d) -> p j d", d=64)))
            P.dma(KT[64:96, :], kpeT[:, :])
            P.barrier()
            for ti, (t0, W, cm) in enumerate(tiles):
                Q = qt[ti % 2]
                P.dma(Q[:, :W], qTs[:, t0:t0 + W])
                nk = 2 if cm else NT
                po = psO[cO % 2]; cO += 1
                LA = 2
                pend = []
                for kt in range(nk + LA):
                    if kt < nk:
                        pS = psS[cS % 3]; cS += 1
                        P.mm(pS[:, :W], KT[:, kt * 128:(kt + 1) * 128], Q[:, :W])
                        pt_ = pT[cP % 3]; cP += 1
                        P.act(pt_[:, :W], pS[:, :W], AF.Exp, scale=MLA_SCALE)
                        pend.append(pt_)
                    if kt >= LA:
                        k2 = kt - LA
                        P.mm(po[0:65, :W], VA[:, k2, :], pend[k2][:, :W], start=k2 == 0, stop=k2 == nk - 1)
                P.copy(rsum[64:65, :W], po[64:65, :W])
                pbx = psX[cX % 3]; cX += 1
                P.mm(pbx[0:64, :W], ones32[64:65, 0:64], rsum[64:65, :W])
                P.recip(rc[:, :W], pbx[0:64, :W])
                Y = yo[ti % 2]
                P.tt(Y[:, :W], po[0:64, :W], rc[:, :W], ALU.mult)
                E["ywrite"](hh * 64, 64, t0, W, Y[:, :W])
            P.barrier()
        P.release(mk)

    if "C" in phases:
        mk = P.mark()
        wDs = P.sbuf("wDs", [128, 8, 3, 128], BF16)
        K12 = P.sbuf("K12", [128, Tt], BF16)
        VD = P.sbuf("VD", [128, NT, 128], BF16)
        hb = P.sbuf("hbC", [128, 8, 512], BF16, nbufs=2)
        cs = P.sbuf("csC", [32, 2, 512], F32, nbufs=2)
        t1 = P.sbuf("t1C", [32, 512], F32)
        t2 = P.sbuf("t2C", [32, 512], F32)
        rp = P.sbuf("rpC", [32, 8, 512], BF16, nbufs=2)
        qt = P.sbuf("qtC", [128, 512], BF16, nbufs=2)
        pT = P.sbuf("pTC", [128, 512], BF16, nbufs=4)
        o1 = P.sbuf("o1C", [128, 512], F32)
        o2 = P.sbuf("o2C", [128, 512], F32)
        rcp = P.sbuf("rcpC", [128, 512], F32)
        sqd = P.sbuf("sqdC", [128, 512], F32)
        yo = P.sbuf("yoC", [128, 512], BF16, nbufs=2)
        psS = [bank[0], bank[1], bank[2], bank[3]]
        psO = [bank[4], bank[5]]
        psM = [bank[6], bank[7]]
        cS = cP = 0
        for hh in range(2):
            P.dma(wDs[:], wD[:, :, hh])
            for ti, (t0, W, cm) in enumerate(tiles):
                H = hb[ti % 2]
                CS = cs[ti % 2]
                R = rp[ti % 2]
                P.dma(H[:, :, :W], fm(hT, t0, W, 8))
                P.dma(CS[:, 0, :W], cdT[:, t0:t0 + W])
                P.dma(CS[:, 1, :W], sdT[:, t0:t0 + W])
                for w in range(2):
                    for j in range(2):
                        pa = psS[cS % 4]; cS += 1
                        proj_fm(pa, 32, lambda kc: wDs[:, kc, w, j * 64:j * 64 + 32], lambda kc: H[:, kc, :W], W)
                        pb = psS[cS % 4]; cS += 1
                        proj_fm(pb, 32, lambda kc: wDs[:, kc, w, j * 64 + 32:j * 64 + 64], lambda kc: H[:, kc, :W], W)
                        i0 = (w * 2 + j) * 2
                        rope(R[:, i0, :W], R[:, i0 + 1, :W], pa[0:32, :W], pb[0:32, :W], CS[:, 0, :W], CS[:, 1, :W], W, 32, t1, t2)
                        dst = qdT if w == 0 else kdT
                        P.dma(dst[j, 0:32, t0:t0 + W], R[:, i0, :W], eng="pool")
                        P.dma(dst[j, 32:64, t0:t0 + W], R[:, i0 + 1, :W], eng="pool")
                pv = psS[cS % 4]; cS += 1
                for j in range(W // 128):
                    for kc in range(8):
                        P.mm(pv[:, j * 128:(j + 1) * 128], H[:, kc, j * 128:(j + 1) * 128], wDs[:, kc, 2, :], start=kc == 0, stop=kc == 7)
                kt0 = t0 // 128
                P.copy(VD[:, kt0:kt0 + W // 128, :], pv[:, 0:W].m(lambda ap: ap.rearrange("p (j d) -> p j d", d=128)))
            P.barrier()
            P.dma(K12[:, :], kdT[:].m(lambda ap: ap.rearrange("j p t -> (j p) t")))
            for ti, (t0, W, cm) in enumerate(tiles):
                Q = qt[ti % 2]
                P.dma(Q[:, :W], qdT[:, :, t0:t0 + W].m(lambda ap: ap.rearrange("j p t -> (j p) t")))
                nk = 2 if cm else NT
                LA = 1
                pend = []
                for kt in range(nk + LA):
                    if kt < nk:
                        cur = []
                        for j in range(2):
                            pS = psS[cS % 4]; cS += 1
                            P.mm(pS[:, :W], K12[64 * j:64 * (j + 1), kt * 128:(kt + 1) * 128], Q[64 * j:64 * (j + 1), :W])
                            cur.append(pS)
                        pts = []
                        for j in range(2):
                            pt_ = pT[cP % 4]; cP += 1
                            P.act(pt_[:, :W], cur[j][:, :W], AF.Exp, scale=DIFF_SCALE)
                            pts.append(pt_)
                        pend.append(pts)
                    if kt >= LA:
                        k2 = kt - LA
                        for j in range(2):
                            P.mm(psO[j][:, :W], VD[:, k2, :], pend[k2][j][:, :W], start=k2 == 0, stop=k2 == nk - 1)
                            P.mm(psM[j][:, :W], onesb[:], pend[k2][j][:, :W], start=k2 == 0, stop=k2 == nk - 1)
                for j in range(2):
                    P.recip(rcp[:, :W], psM[j][:, :W])
                    P.tt((o1 if j == 0 else o2)[:, :W], psO[j][:, :W], rcp[:, :W], ALU.mult)
                P.stt(o1[:, :W], o2[:, :W], nlam[:, 0:1], o1[:, :W], ALU.mult, ALU.add)
                P.act(sqd[:, :W], o1[:, :W], AF.Square)
                px = psS[cS % 4]; cS += 1
                P.mm(px[:, :W], o128[:], sqd[:, :W])
                P.act(rcp[:, :W], px[:, :W], AF.Sqrt, bias=epsR[:, 0:1], scale=1.0)
                P.recip(rcp[:, :W], rcp[:, :W])
                if dbg and hh == 1 and ti == 1:
                    dvd = _pd("dbg_vd", [128, NT, 128], BF16, kind="ExternalOutput")
                    P.dma(dvd[:], VD[:], eng="pool")
                    do1 = _pd("dbg_o1", [128, 512], F32, kind="ExternalOutput")
                    P.dma(do1[:], o1[:], eng="pool")
                    do2 = _pd("dbg_o2", [128, 512], F32, kind="ExternalOutput")
                    P.dma(do2[:], o2[:], eng="pool")
                    dnl = _pd("dbg_nl", [128, 1], F32, kind="ExternalOutput")
                    P.dma(dnl[:], nlam[:], eng="pool")
                Y = yo[ti % 2]
                P.stt(Y[:, :W], o1[:, :W], subs[:, 0:1], rcp[:, :W], ALU.mult, ALU.mult)
                E["ywrite"](512 + hh * 128, 128, t0, W, Y[:, :W])
            P.barrier()
        P.release(mk)

    if "D" in phases:
        mk = P.mark()
        wHs = P.sbuf("wHs", [128, 8, 5, 128], BF16)
        hb = P.sbuf("hbD", [128, 8, 512], BF16, nbufs=2)
        sg = P.sbuf("sgD", [128, 512], F32, nbufs=2)
        ofm = P.sbuf("ofmD", [128, 4, 512], BF16, nbufs=2)
        ztm = P.sbuf("ztmD", [128, 4, 3, 128], F32)
        gtm = P.sbuf("gtmD", [128, 4, 2, 128], F32, nbufs=2)
        ktm = P.sbuf("ktmD", [128, 4, 3, 128], BF16, nbufs=2)
        Oacc = P.sbuf("Oacc", [128, Tt], F32)
        qb = P.sbuf("qbD", [128, 512], BF16, nbufs=2)
        kb_ = P.sbuf("kbD", [128, 512], BF16, nbufs=2)
        gb = P.sbuf("gbD", [128, 4, 128], F32, nbufs=2)
        kt_ = P.sbuf("ktD", [128, 4, 128], BF16, nbufs=2)
        vb = P.sbuf("vbD", [128, 4, 128], BF16, nbufs=2)
        Eh = P.sbuf("EhD", [128, 512], F32)
        ER = P.sbuf("ERD", [128, 4, 128], F32)
        Qh = P.sbuf("QhD", [128, 512], BF16)
        Kh = P.sbuf("KhD", [128, 4, 128], BF16)
        Qt = P.sbuf("QtD", [128, 512], BF16)
        Dq = P.sbuf("DqD", [128, 512], F32)
        Dk = P.sbuf("DkD", [128, 8, 4, 64], F32)
        Kall = P.sbuf("KallD", [128, 8, 4, 128], BF16)
        P.memset(Kall[:], 0.0)
        refs = P.sbuf("refsD", [128, 8, 4], F32)
        negs = P.sbuf("negsD", [128, 2, 4, 64], F32)
        P.dma(negs[:], negm[:])
        PTt = P.sbuf("PTD", [128, 4, 128], BF16)
        Sf = P.sbuf("SfD", [128, 128], F32)
        Sb = P.sbuf("SbD", [128, 128], BF16)
        sqh = P.sbuf("sqhD", [128, 512], F32)
        rch = P.sbuf("rchD", [128, 512], F32)
        ogb = P.sbuf("ogbD", [128, 512], BF16, nbufs=2)
        yo = P.sbuf("yoD", [128, 512], BF16, nbufs=2)
        pX = [bank[0], bank[1], bank[2], bank[3]]
        cX = 0
        for hh in range(2):
            P.dma(wHs[:], wH[:, :, hh])
            for ti, (t0, W, cm) in enumerate(tiles):
                H = hb[ti % 2]
                OF = ofm[ti % 2]
                GT = gtm[ti % 2]
                KTm = ktm[ti % 2]
                P.dma(H[:, :, :W], fm(hT, t0, W, 8))
                pt = pX[cX % 4]; cX += 1
                proj_fm(pt, 128, lambda kc: wHs[:, kc, 0, :], lambda kc: H[:, kc, :W], W)
                S_ = sg[0]
                P.act(S_[:, :W], pt[:, :W], AF.Silu)
                P.ts(OF[:, 0, :W], S_[:, :W], HG_SCALE, ALU.mult)
                for d in range(2):
                    pt = pX[cX % 4]; cX += 1
                    proj_fm(pt, 128, lambda kc: wHs[:, kc, 1 + d, :], lambda kc: H[:, kc, :W], W)
                    S_ = sg[1]
                    P.act(S_[:, :W], pt[:, :W], AF.Sigmoid, scale=-1.0)
                    P.ts(OF[:, 1 + d, :W], S_[:, :W], oml_fm[:, d, hh:hh + 1], ALU.mult)
                pt = pX[cX % 4]; cX += 1
                proj_fm(pt, 128, lambda kc: wHs[:, kc, 4, :], lambda kc: H[:, kc, :W], W)
                P.act(OF[:, 3, :W], pt[:, :W], AF.Silu)
                P.dma(hqT[:, t0:t0 + W], OF[:, 0, :W], eng="pool")
                P.dma(hkT[0, :, t0:t0 + W], OF[:, 1, :W], eng="pool")
                P.dma(hkT[1, :, t0:t0 + W], OF[:, 2, :W], eng="pool")
                P.dma(hogT[:, t0:t0 + W], OF[:, 3, :W], eng="pool")
                nj = W // 128
                for j in range(nj):
                    pt = pX[cX % 4]; cX += 1
                    for c3, wi in enumerate((1, 2, 3)):
                        for kc in range(8):
                            P.mm(pt[:, c3 * 128:(c3 + 1) * 128], H[:, kc, j * 128:(j + 1) * 128], wHs[:, kc, wi, :],
                                 start=kc == 0, stop=kc == 7)
                    P.act(ztm[:, j, 0:2, :], pt[:, 0:256].m(lambda ap: ap.rearrange("p (a d) -> p a d", d=128)), AF.Sigmoid)
                    P.copy(KTm[:, j, 2, :], pt[:, 256:384], eng="act")
                for d in range(2):
                    lbv = lb_row[:, d, hh * 128:(hh + 1) * 128]
                    omv = oml_row[:, d, hh * 128:(hh + 1) * 128]
                    for j in range(nj):
                        P.tt(ztm[:, j, d, :], ztm[:, j, d, :], omv, ALU.mult)
                        P.tt(ztm[:, j, d, :], ztm[:, j, d, :], lbv, ALU.add)
                    P.act(GT[:, :nj, d, :], ztm[:, :nj, d, :], AF.Ln)
                    P.ts(KTm[:, :nj, d, :], ztm[:, :nj, d, :], -1.0, ALU.mult, 1.0, ALU.add)
                for d in range(2):
                    P.dma(hg_[d, t0:t0 + W, :].m(lambda ap: ap.rearrange("(j p) k -> p j k", p=128)), GT[:, :nj, d, :], eng="pool")
                    P.dma(hkt[d, t0:t0 + W, :].m(lambda ap: ap.rearrange("(j p) k -> p j k", p=128)), KTm[:, :nj, d, :], eng="pool")
                P.dma(hv[t0:t0 + W, :].m(lambda ap: ap.rearrange("(j p) k -> p j k", p=128)), KTm[:, :nj, 2, :], eng="pool")
            P.barrier()
            for d in range(2):
                TI = tris[:, d, :]
                TS = tris[:, 2 + d, :]
                order = list(range(len(tiles))) if d == 0 else [0] + list(range(len(tiles) - 1, 0, -1))
                P.memset(Sf[:], 0.0)
                P.memset(Sb[:], 0.0)
                for bi, ti in enumerate(order):
                    t0, W, cm = tiles[ti]
                    nj = W // 128
                    QB, KB, GB, KTB, VB = qb[bi % 2], kb_[bi % 2], gb[bi % 2], kt_[bi % 2], vb[bi % 2]
                    P.dma(QB[:, :W], hqT[:, t0:t0 + W])
                    P.dma(KB[:, :W], hkT[d, :, t0:t0 + W])
                    P.dma(GB[:, :nj, :], hg_[d, t0:t0 + W, :].m(lambda ap: ap.rearrange("(j p) k -> p j k", p=128)))
                    P.dma(KTB[:, :nj, :], hkt[d, t0:t0 + W, :].m(lambda ap: ap.rearrange("(j p) k -> p j k", p=128)))
                    P.dma(VB[:, :nj, :], hv[t0:t0 + W, :].m(lambda ap: ap.rearrange("(j p) k -> p j k", p=128)))
                    pA = bank[4]
                    pR = bank[5]
                    for j in range(nj):
                        P.mm(pA[:, j * 128:(j + 1) * 128], GB[:, j, :], TI)
                        P.mm(pR[:, j * 128:(j + 1) * 128], TS, GB[:, j, :])
                    P.act(Eh[:, :W], pA[:, :W], AF.Exp)
                    P.act(ER[:, :nj, :], pR[:, :W].m(lambda ap: ap.rearrange("p (j k) -> p j k", k=128)), AF.Exp)
                    P.tt(Qh[:, :W], QB[:, :W], Eh[:, :W], ALU.mult)
                    P.tt(Kh[:, :nj, :], KTB[:, :nj, :], ER[:, :nj, :], ALU.mult)
                    ncb = W // 64
                    nsb = W // 16
                    pA3 = pA[:, :W].m(lambda ap: ap.rearrange("p (c s) -> p c s", s=64))
                    P.memset(refs[:], 0.0)
                    if d == 0:
                        P.copy(refs[:, :ncb, 1:4], pA3[:, :, 15:48:16])
                    else:
                        P.copy(refs[:, :ncb, 0:3], pA3[:, :, 16:64:16])
                    rflat = refs[:, :ncb, :].m(lambda ap: ap.rearrange("p c i -> p (c i)"))
                    P.tt(Dq[:, :W].m(lambda ap: ap.rearrange("p (b s) -> p b s", s=16)),
                         pA[:, :W].m(lambda ap: ap.rearrange("p (b s) -> p b s", s=16)),
                         rflat.m(lambda ap: ap.unsqueeze(2).broadcast_to([128, nsb, 16])), ALU.subtract)
                    P.act(Dq[:, :W], Dq[:, :W], AF.Exp)
                    P.tt(Qt[:, :W], QB[:, :W], Dq[:, :W], ALU.mult)
                    P.tt(Dk[:, :ncb, :, :],
                         refs[:, :ncb, :].m(lambda ap: ap.unsqueeze(3).broadcast_to([128, ncb, 4, 64])),
                         pA3.m(lambda ap: ap.unsqueeze(2).broadcast_to([128, ncb, 4, 64])), ALU.subtract)
                    P.tt(Dk[:, :ncb, :, :], Dk[:, :ncb, :, :],
                         negs[:, d, :, :].m(lambda ap: ap.unsqueeze(1).broadcast_to([128, ncb, 4, 64])), ALU.add)
                    P.act(Dk[:, :ncb, :, :], Dk[:, :ncb, :, :], AF.Exp)
                    for hf in range(2):
                        P.tt(Kall[:, :ncb, :, :].m(lambda ap: ap.rearrange("p (j h) i s -> p j h i s", h=2)[:, :, hf, :, hf * 64:(hf + 1) * 64]),
                             Dk[:, :ncb, :, :].m(lambda ap: ap.rearrange("p (j h) i s -> p j h i s", h=2)[:, :, hf, :, :]),
                             KB[:, :W].m(lambda ap: ap.rearrange("p (j h s) -> p j h s", h=2, s=64)[:, :, hf, :].unsqueeze(2).broadcast_to([128, nj, 4, 64])),
                             ALU.mult)
                    pS = bank[6]
                    for j in range(nj):
                        for Ig in range(8):
                            col = j * 128 + Ig * 16
                            P.mm(pS[:, col:col + 16], Kall[:, 2 * j + Ig // 4, Ig % 4, :], Qt[:, col:col + 16])
                        P.tt(PTt[:, j, :], pS[:, j * 128:(j + 1) * 128], TI, ALU.mult)
                    pO = bank[7]
                    chunks = list(range(W // 64))
                    if d == 1:
                        chunks = chunks[::-1]
                    for c in chunks:
                        c0 = c * 64
                        j, hf = c // 2, c % 2
                        pb0 = 64 * hf
                        P.mm(pO[:, c0:c0 + 64], VB[pb0:pb0 + 64, j, :], PTt[pb0:pb0 + 64, j, pb0:pb0 + 64], start=True, stop=False)
                        P.mm(pO[:, c0:c0 + 64], Sb[:], Qh[:, c0:c0 + 64], start=False, stop=True)
                        pU = pX[cX % 4]; cX += 1
                        P.mm(pU[:, 0:128], Kh[pb0:pb0 + 64, j, :], VB[pb0:pb0 + 64, j, :])
                        ecol = c0 + 63 if d == 0 else c0
                        P.stt(Sf[:], Sf[:], Eh[:, ecol:ecol + 1], pU[:, 0:128], ALU.mult, ALU.add)
                        P.copy(Sb[:], Sf[:], eng="act")
                    if d == 0:
                        P.copy(Oacc[:, t0:t0 + W], pO[:, :W])
                    else:
                        P.tt(Oacc[:, t0:t0 + W], Oacc[:, t0:t0 + W], pO[:, :W], ALU.add)
            for ti, (t0, W, cm) in enumerate(tiles):
                OG = ogb[ti % 2]
                P.dma(OG[:, :W], hogT[:, t0:t0 + W])
                P.act(sqh[:, :W], Oacc[:, t0:t0 + W], AF.Square)
                px = pX[cX % 4]; cX += 1
                P.mm(px[:, :W], o128[:], sqh[:, :W])
                P.act(rch[:, :W], px[:, :W], AF.Sqrt, bias=epsR[:, 0:1], scale=1.0)
                P.recip(rch[:, :W], rch[:, :W])
                P.stt(rch[:, :W], Oacc[:, t0:t0 + W], hgns[:, 0:1], rch[:, :W], ALU.mult, ALU.mult)
                Y = yo[ti % 2]
                P.tt(Y[:, :W], rch[:, :W], OG[:, :W], ALU.mult)
                E["ywrite"](256 + hh * 128, 128, t0, W, Y[:, :W])
            P.barrier()
        P.release(mk)
    P.barrier()
    P.release(mk_all)


PAIRS = [[0, 1], [2, 3], [4, 5], [6, 7]]
W_SIZES = [("wA", 8 * 416), ("wq", 2 * 4 * 96), ("wkv", 4 * 128), ("wD", 8 * 2 * 3 * 128), ("wH", 8 * 2 * 5 * 128),
           ("wg", 3 * 8 * 1024), ("wb", 3 * 8 * 512), ("wo", 8 * 1024), ("wf1", 44 * 1024), ("wf2", 8 * 2816)]
W_LAYER = sum(n for _, n in W_SIZES)
NW = 4 * W_LAYER


def w_off(l, key):
    o = l * W_LAYER
    for k, n in W_SIZES:
        if k == key:
            return o, n
        o += n
    raise KeyError(key)


def build_fused(S):
    nc = bass.Bass("TRN2", target_bir_lowering=False)
    P = Prog(nc)
    Tt = CTX + S
    Sh = S // 2
    TT = 128 + Sh
    inp = lambda n, sh, dt=F32: P.dram(n, sh, dt, kind="ExternalInput")
    x_own = inp("x_own", [1024, TT])
    wall = inp("wall", [128, NW])
    cvec = inp("cvec", [128, 8, 2])
    wmod = inp("wmod", [4, 48, 128, 1024])
    bmod = inp("bmod", [4, 128, 48])
    ng = inp("ng", [4, 128, 3])
    lbfm = inp("lbfm", [128, 2, 4, 2])
    lbrep = inp("lbrep", [128, 2, 4, 256])
    dlrep = inp("dlrep", [4, 128, 4, 64])
    lcon = inp("lcon", [4, 128, 8])
    hgn = inp("hgn", [4, 128, 1])
    sub = inp("sub", [4, 128, 1])
    cmT = inp("cmT", [16, Tt])
    smT = inp("smT", [16, Tt])
    cdT = inp("cdT", [32, Tt])
    sdT = inp("sdT", [32, Tt])
    tri = inp("tri", [128, 4, 128])
    negm = inp("negm", [128, 2, 4, 64])
    bg = inp("bg", [4, 128, 3, 8])
    lnp = inp("lnp", [4, 128, 4, 8])
    sel = inp("sel", [128, 2])
    xout = P.dram("xout", [1024, Sh], F32, kind="ExternalOutput")

    w16 = P.dram("w16", [128, NW], BF16)
    XW = 4 * 512
    xcb = [0] + [min(TT, 128 + XW * (j + 1)) for j in range((Sh + XW - 1) // XW)]
    nxc = len(xcb) - 1
    xo_c = [[P.dram(f"xo_{c}_{j}", [128, xcb[j + 1] - xcb[j]], F32) for j in range(nxc)] for c in range(8)]
    xg_c = [[P.dram(f"xg_{c}_{j}", [256, xcb[j + 1] - xcb[j]], F32) for j in range(nxc)] for c in range(8)]
    ycb = [0, CTX + Sh, Tt]
    yp_c = [[P.dram(f"yp_{c}_{j}", [128, ycb[j + 1] - ycb[j]], BF16) for j in range(2)] for c in range(6)]
    ya_c = [[P.dram(f"ya_{c}_{j}", [256, ycb[j + 1] - ycb[j]], BF16) for j in range(2)] for c in range(6)]

    def xchunk(t0):
        for j in range(nxc):
            if xcb[j] <= t0 < xcb[j + 1]:
                return j, t0 - xcb[j]
        raise ValueError(t0)

    def xtile_load(X, t0, W):
        j, o = xchunk(t0)
        for c in range(8):
            P.dma(X[:, c, :W], xo_c[c][j][:, o:o + W])

    def xtile_store(M_, t0, W):
        j, o = xchunk(t0)
        for c in range(8):
            P.dma(xo_c[c][j][:, o:o + W], M_[:, c, :W], eng="pool", owner=M_)

    def ywrite(row0, nrows, t0, W, src):
        c, po = row0 // 128, row0 % 128
        j = 0 if t0 < ycb[1] else 1
        o = t0 - ycb[j]
        P.dma(yp_c[c][j][po:po + nrows, o:o + W], src, eng="pool")

    def ycand_load(dst, hsel, t0, W, cm):
        gc = hsel * 128 if cm else CTX + hsel * Sh + (t0 - 128)
        j = 0 if gc < ycb[1] else 1
        o = gc - ycb[j]
        for i in range(3):
            for rr in range(2):
                for cc in range(2):
                    P.dma(dst[:, 4 * i + 2 * rr + cc, :W], ya_c[2 * i + cc][j][rr * 128:(rr + 1) * 128, o:o + W])

    def exchange_x():
        for c in range(8):
            for j in range(nxc):
                P.allgather(xg_c[c][j], xo_c[c][j], PAIRS)
        P.barrier()

    def exchange_y():
        for c in range(6):
            for j in range(2):
                P.allgather(ya_c[c][j], yp_c[c][j], PAIRS)
        P.barrier()

    scr = {
        "hT": P.dram("s_hT", [1024, Tt], BF16), "cqnT": P.dram("s_cqn", [256, Tt], BF16),
        "ckvnT": P.dram("s_ckvn", [128, Tt], BF16), "kpeT": P.dram("s_kpe", [32, Tt], BF16),
        "qTs": P.dram("s_q", [96, Tt], BF16), "qdT": P.dram("s_qd", [2, 64, Tt], BF16),
        "kdT": P.dram("s_kd", [2, 64, Tt], BF16), "hqT": P.dram("s_hq", [128, Tt], BF16),
        "hkT": P.dram("s_hk", [2, 128, Tt], BF16), "hogT": P.dram("s_hog", [128, Tt], BF16),
        "hg_": P.dram("s_hg", [2, Tt, 128], F32), "hkt": P.dram("s_hkt", [2, Tt, 128], BF16),
        "hv": P.dram("s_hv", [Tt, 128], BF16),
    }
    ps = PS(nc)
    modT = P.sbuf("modT", [128, 48, 2], F32)
    sels = P.sbuf("sels", [128, 2], F32)
    P.dma(sels[:], sel[:])

    mk = P.mark()
    CH = 2048
    ca = P.sbuf("ca", [128, CH], F32, nbufs=3)
    cb = P.sbuf("cb", [128, CH], BF16, nbufs=3)
    for i in range(NW // CH):
        A = ca[i % 3]
        B = cb[i % 3]
        P.dma(A[:], wall[:, i * CH:(i + 1) * CH])
        P.copy(B[:], A[:], eng="dve" if i % 2 == 0 else "act")
        P.dma(w16[:, i * CH:(i + 1) * CH], B[:], eng="pool")
    xb = P.sbuf("xb", [128, 8, 512], F32, nbufs=2)
    for i, (t0, W) in enumerate([(0, 128)] + [(128 + 512 * q, 512) for q in range(Sh // 512)]):
        X = xb[i % 2]
        P.dma(X[:, :, :W], x_own[:, t0:t0 + W].m(lambda ap: ap.rearrange("(c p) t -> p c t", p=128)))
        xtile_store(X, t0, W)
    P.barrier()
    P.release(mk)
    exchange_x()

    def xload(X, ti, t0, W):
        if ti == 0:
            for r in range(2):
                for c in range(8):
                    P.dma(X[:, c, r * 128:(r + 1) * 128], xg_c[c][0][r * 128:(r + 1) * 128, 0:128])
        else:
            tok = t0 - CTX
            r = tok // Sh
            j, o = xchunk(128 + tok - r * Sh)
            for c in range(8):
                P.dma(X[:, c, :W], xg_c[c][j][r * 128:(r + 1) * 128, o:o + W])

    def wv(l, key, pat=None, **kw):
        o, n = w_off(l, key)
        v = w16[:, o:o + n]
        if pat is not None:
            v = v.m(lambda ap: ap.rearrange(pat, **kw))
        return v

    for l in range(4):
        wg_v = wv(l, "wg", "p (i o k) -> p i o k", i=3, o=8)
        wb_v = wv(l, "wb", "p (i o k) -> p i o k", i=3, o=8)
        wo_v = wv(l, "wo", "p (o k) -> p o k", o=8)
        wf1_v = wv(l, "wf1", "p (o k) -> p o k", o=44)
        wf2_v = wv(l, "wf2", "p (o k) -> p o k", o=8)
        E = dict(scr)
        E.update({
            "cvec": cvec, "wmod": wmod[l], "bmod": bmod[l], "ng": ng[l],
            "wA": wv(l, "wA", "p (k n) -> p k n", k=8), "wq": wv(l, "wq", "p (k h n) -> p k h n", k=2, h=4),
            "wkv": wv(l, "wkv", "p (h n) -> p h n", h=4),
            "wD": wv(l, "wD", "p (k h w n) -> p k h w n", k=8, h=2, w=3),
            "wH": wv(l, "wH", "p (k h w n) -> p k h w n", k=8, h=2, w=5),
            "lbfm": lbfm, "lbrep": lbrep, "dlrep": dlrep[l], "lcon": lcon[l], "hgn": hgn[l], "sub": sub[l],
            "cmT": cmT, "smT": smT, "cdT": cdT, "sdT": sdT, "tri": tri, "negm": negm,
            "xload": xload, "ywrite": ywrite, "xtile_load": xtile_load, "xtile_store": xtile_store,
            "ycand_load": ycand_load, "bg": bg[l], "lnp": lnp[l], "xout": xout,
            "wg": (lambda i, oc, v=wg_v: v[:, i, oc, :]), "wb": (lambda i, oc, v=wb_v: v[:, i, oc, :]),
            "wo": (lambda oc, v=wo_v: v[:, oc, :]), "wf1": (lambda fc, v=wf1_v: v[:, fc, :]),
            "wf2": (lambda oc, v=wf2_v: v[:, oc, :]),
        })
        emit_M(P, ps, S, E, modT)
        exchange_y()
        emit_T(P, ps, Sh, E, modT, sels, last=(l == 3))
        if l < 3:
            exchange_x()
    P.emit()
    return nc, P

bf16 = ml_dtypes.bfloat16
def lay_oc(W, mo=128):
    K, N = W.shape
    KC, NC = K // 128, N // mo
    return np.ascontiguousarray(W.reshape(KC, 128, NC, mo).transpose(2, 1, 0, 3).reshape(NC, 128, KC * mo))
def lay_k(W):
    K, N = W.shape
    return np.ascontiguousarray(W.reshape(K // 128, 128, N).transpose(1, 0, 2))
def vec_chunks(v):
    n = v.shape[-1] // 128
    return np.ascontiguousarray(v.reshape(n, 128).T)

def rope_tables(S, rot_dim, ctx=256):
    rows = S // 64
    row, col = np.meshgrid(np.arange(rows, dtype=np.float32), np.arange(64, dtype=np.float32), indexing='ij')
    n_freq = rot_dim // 4
    inv_freq = (np.float32(10000.0) ** (-np.arange(n_freq, dtype=np.float32) / np.float32(n_freq))).astype(np.float32)
    ang = np.concatenate([row.reshape(-1, 1) * inv_freq, col.reshape(-1, 1) * inv_freq], axis=-1).astype(np.float32)
    cos = np.concatenate([np.ones((ctx, rot_dim // 2), np.float32), np.cos(ang)], 0)
    sin = np.concatenate([np.zeros((ctx, rot_dim // 2), np.float32), np.sin(ang)], 0)
    return np.ascontiguousarray(cos.T), np.ascontiguousarray(sin.T)

def tri_consts():
    t = np.zeros((128, 4, 128), np.float32)
    for blk in range(2):
        o = blk * 64
        for s in range(64):
            for u in range(64):
                t[o + s, 0, o + u] = 1.0 if s <= u else 0.0
                t[o + s, 1, o + u] = 1.0 if s >= u else 0.0
                t[o + s, 2, o + u] = 1.0 if s > u else 0.0
                t[o + s, 3, o + u] = 1.0 if s < u else 0.0
    return t

def neg_consts():
    n = np.zeros((128, 2, 4, 64), np.float32)
    for I in range(4):
        for s in range(64):
            n[:, 0, I, s] = 0.0 if s < (I + 1) * 16 else -1e4
            n[:, 1, I, s] = 0.0 if s >= I * 16 else -1e4
    return n

def m_inputs(inp, l, b, g, xfull, S, W16=None, xT=None):
    import math
    if W16 is None:
        Wk = lay_k(inp["w_in"][l])
        heads2 = [2 * g, 2 * g + 1]
        wD = np.stack([np.stack([Wk[:, :, base + h * 128: base + (h + 1) * 128] for base in (2976, 3488, 4000)], 2) for h in heads2], 2)
        wH = np.stack([np.stack([Wk[:, :, base + h * 128: base + (h + 1) * 128] for base in (416, 928, 1440, 1952, 2464)], 2) for h in heads2], 2)
        wq = lay_k(inp["w_uq"][l]).reshape(128, 2, 8, 96)[:, :, 4 * g:4 * g + 4]
        wkv = lay_k(inp["w_ukv"][l]).reshape(128, 8, 128)[:, 4 * g:4 * g + 4]
        cast = lambda a: np.ascontiguousarray(a).astype(bf16)
        wd = {"wA": cast(Wk[:, :, 0:416]), "wq": cast(wq), "wkv": cast(wkv), "wD": cast(wD), "wH": cast(wH)}
    else:
        wd = {"wA": W16[("wA", l)], "wq": W16[("wq", l, g)], "wkv": W16[("wkv", l, g)], "wD": W16[("wD", l, g)], "wH": W16[("wH", l, g)]}
    if xT is None:
        xT = np.ascontiguousarray(xfull.T)
    heads2 = [2 * g, 2 * g + 1]
    lg = inp["hg_lb_logits"]
    lbfm = np.stack([lg[:, :, h * 128:(h + 1) * 128] for h in heads2], -1)
    lbfm = np.ascontiguousarray(lbfm.transpose(2, 0, 1, 3))
    lbrep = np.ascontiguousarray(np.broadcast_to(lg[None, :, :, 2 * g * 128: 2 * g * 128 + 256], (128, 2, 4, 256)))
    lam_init = 0.8 - 0.6 * math.exp(-0.3 * l)
    lcon = np.zeros((128, 8), np.float32)
    for j in range(4):
        lcon[:, j] = 1.0 if 1 <= j <= l else 0.0
    lcon[:, 4] = lam_init
    lcon[:, 5] = 1.0 - lam_init
    cm, sm = rope_tables(S, 32)
    cd, sd = rope_tables(S, 64)
    d = {
        "xT": xT,
        "cvec": np.ascontiguousarray(np.stack([vec_chunks(inp["c"][b]), vec_chunks(inp["c_ctx"])], -1)),
        "wmod": lay_oc(inp["w_mod"][l]), "bmod": vec_chunks(inp["b_mod"][l]),
        "ng": np.ascontiguousarray(np.concatenate([vec_chunks(inp["mla_q_norm"][l]), inp["mla_kv_norm"][l][:, None]], 1)),
        "lbfm": lbfm, "lbrep": lbrep,
        "dlrep": np.ascontiguousarray(np.broadcast_to(inp["diff_lambda"][l][None], (128, 4, 64))),
        "lcon": lcon, "hgn": np.ascontiguousarray(inp["hg_norm"][l][:, None]), "sub": np.ascontiguousarray(inp["diff_subln"][l][:, None]),
        "cmT": cm, "smT": sm, "cdT": cd, "sdT": sd, "tri": tri_consts(), "negm": neg_consts(),
    }
    d.update(wd)
    return d

_PROGS = {}


def _pack_weights(inp, g):
    out = np.zeros((128, NW), np.float32)
    for l in range(4):
        Wk = lay_k(inp["w_in"][l])
        heads2 = [2 * g, 2 * g + 1]
        wD = np.stack([np.stack([Wk[:, :, base + h * 128: base + (h + 1) * 128] for base in (2976, 3488, 4000)], 2) for h in heads2], 2)
        wH = np.stack([np.stack([Wk[:, :, base + h * 128: base + (h + 1) * 128] for base in (416, 928, 1440, 1952, 2464)], 2) for h in heads2], 2)
        parts = {
            "wA": Wk[:, :, 0:416],
            "wq": lay_k(inp["w_uq"][l]).reshape(128, 2, 8, 96)[:, :, 4 * g:4 * g + 4],
            "wkv": lay_k(inp["w_ukv"][l]).reshape(128, 8, 128)[:, 4 * g:4 * g + 4],
            "wD": wD, "wH": wH,
            "wg": np.stack([lay_oc(inp["w_gate"][l, i]) for i in range(3)]).transpose(2, 0, 1, 3),
            "wb": np.stack([lay_oc(inp["w_branch"][l, i]) for i in range(3)]).transpose(2, 0, 1, 3),
            "wo": lay_oc(inp["w_o"][l]).transpose(1, 0, 2),
            "wf1": lay_oc(inp["w_ff1"][l]).transpose(1, 0, 2),
            "wf2": lay_oc(inp["w_ff2"][l]).transpose(1, 0, 2),
        }
        for k, n in W_SIZES:
            o, _ = w_off(l, k)
            out[:, o:o + n] = parts[k].reshape(128, n)
    return out


def kernel(x, c, ctx, c_ctx, w_mod, b_mod, w_in, mla_q_norm, mla_kv_norm, w_uq, w_ukv, hg_lb_logits, hg_norm,
           diff_lambda, diff_subln, w_branch, w_gate, b_gate, w_o, ln1_g, ln1_b, w_ff1, w_ff2, ln2_g, ln2_b):
    import math
    inp = dict(x=x, c=c, ctx=ctx, c_ctx=c_ctx, w_mod=w_mod, b_mod=b_mod, w_in=w_in, mla_q_norm=mla_q_norm,
               mla_kv_norm=mla_kv_norm, w_uq=w_uq, w_ukv=w_ukv, hg_lb_logits=hg_lb_logits, hg_norm=hg_norm,
               diff_lambda=diff_lambda, diff_subln=diff_subln, w_branch=w_branch, w_gate=w_gate, b_gate=b_gate,
               w_o=w_o, ln1_g=ln1_g, ln1_b=ln1_b, w_ff1=w_ff1, w_ff2=w_ff2, ln2_g=ln2_g, ln2_b=ln2_b)
    inp = {k: np.asarray(v, dtype=np.float32) for k, v in inp.items()}
    B, S = inp["x"].shape[0], inp["x"].shape[1]
    Sh = S // 2
    if S not in _PROGS:
        _PROGS[S] = build_fused(S)[0]
    nc = _PROGS[S]
    wpk = [_pack_weights(inp, g) for g in range(2)]
    wmod = np.stack([lay_oc(inp["w_mod"][l]) for l in range(4)])
    bmod = np.stack([vec_chunks(inp["b_mod"][l]) for l in range(4)])
    ng = np.stack([np.concatenate([vec_chunks(inp["mla_q_norm"][l]), inp["mla_kv_norm"][l][:, None]], 1) for l in range(4)])
    dlrep = np.stack([np.broadcast_to(inp["diff_lambda"][l][None], (128, 4, 64)) for l in range(4)])
    lcon = np.zeros((4, 128, 8), np.float32)
    for l in range(4):
        lam_init = 0.8 - 0.6 * math.exp(-0.3 * l)
        for j in range(4):
            lcon[l, :, j] = 1.0 if 1 <= j <= l else 0.0
        lcon[l, :, 4] = lam_init
        lcon[l, :, 5] = 1.0 - lam_init
    hgn = np.stack([inp["hg_norm"][l][:, None] for l in range(4)])
    sub = np.stack([inp["diff_subln"][l][:, None] for l in range(4)])
    bg = np.stack([np.stack([vec_chunks(inp["b_gate"][l, i]) for i in range(3)], 1) for l in range(4)])
    lnp = np.stack([np.stack([vec_chunks(inp[k][l]) for k in ("ln1_g", "ln1_b", "ln2_g", "ln2_b")], 1) for l in range(4)])
    cm, sm = rope_tables(S, 32)
    cd, sd = rope_tables(S, 64)
    tri, negm = tri_consts(), neg_consts()
    lg = inp["hg_lb_logits"]
    ins = []
    for core in range(8):
        b, g = core // 2, core % 2
        heads2 = [2 * g, 2 * g + 1]
        lbfm = np.ascontiguousarray(np.stack([lg[:, :, h * 128:(h + 1) * 128] for h in heads2], -1).transpose(2, 0, 1, 3))
        lbrep = np.ascontiguousarray(np.broadcast_to(lg[None, :, :, 2 * g * 128: 2 * g * 128 + 256], (128, 2, 4, 256)))
        x_own = np.concatenate([inp["ctx"][b, g * 128:(g + 1) * 128], inp["x"][b, g * Sh:(g + 1) * Sh]], 0).T
        selv = np.zeros((128, 2), np.float32)
        selv[:, g] = 1.0
        ins.append({
            "x_own": np.ascontiguousarray(x_own), "wall": wpk[g],
            "cvec": np.ascontiguousarray(np.stack([vec_chunks(inp["c"][b]), vec_chunks(inp["c_ctx"])], -1)),
            "wmod": wmod, "bmod": np.ascontiguousarray(bmod), "ng": np.ascontiguousarray(ng),
            "lbfm": lbfm, "lbrep": lbrep, "dlrep": np.ascontiguousarray(dlrep), "lcon": lcon,
            "hgn": np.ascontiguousarray(hgn), "sub": np.ascontiguousarray(sub),
            "cmT": cm, "smT": sm, "cdT": cd, "sdT": sd, "tri": tri, "negm": negm,
            "bg": np.ascontiguousarray(bg), "lnp": np.ascontiguousarray(lnp), "sel": selv,
        })
    res = run_bass_kernel_spmd(nc, ins, core_ids=list(range(8)))
    out = np.empty((B, S, 1024), np.float32)
    for core in range(8):
        b, g = core // 2, core % 2
        out[b, g * Sh:(g + 1) * Sh] = np.asarray(res.results[core]["xout"]).T
    return out
```
